# Optimizing a Trainium2 kernel written in Bass

```python
import math
import jax, jax.numpy as jnp
from jax import lax
import numpy as np

D_MODEL = 1024
BATCH = 8
SEQ = 4096
DEPTH = 4

GRID_W = 64
CTX_LEN = 256
N_GROUPS = 4
GROUP_W = D_MODEL // N_GROUPS
HEAD_DIM = 64
N_HEADS = GROUP_W // HEAD_DIM
NA_WIN_ROWS = 8
NA_WIN_COLS = 16
SGU_CHUNK = 128
DIFF_QK_DIM = HEAD_DIM // 2
ROPE_BASE = 10000.0
D_FF = 4 * D_MODEL
Q_BLOCK = 128
EPS = 1e-6
COL_FOURIER = 0
COL_NA = GROUP_W
COL_SGU = 4 * GROUP_W
COL_DIFF = 6 * GROUP_W
D_IN = 9 * GROUP_W

kernel_name = "hymba_style_fourier_na_sgu_diffattn_dit"


def rms_norm(x, g):
    x32 = x.astype(jnp.float32)
    y = x32 * lax.rsqrt(jnp.mean(x32 * x32, axis=-1, keepdims=True) + EPS)
    return (y * g.astype(jnp.float32)).astype(x.dtype)


def layer_norm(x, g, b):
    x32 = x.astype(jnp.float32)
    mu = jnp.mean(x32, axis=-1, keepdims=True)
    var = jnp.mean(jnp.square(x32 - mu), axis=-1, keepdims=True)
    y = (x32 - mu) * lax.rsqrt(var + EPS)
    return (y * g.astype(jnp.float32) + b.astype(jnp.float32)).astype(x.dtype)


def ada_params(cond, w, b):
    m = jax.nn.silu(cond) @ w + b
    return jnp.split(m, 6, axis=-1)


def modulate(h, shift, scale):
    return h * (1 + scale) + shift


def heads(t):
    return t.reshape(t.shape[0], t.shape[1], N_HEADS, HEAD_DIM)


def fourier_mix(a):
    bsz, n, _ = a.shape
    a4 = a.astype(jnp.float32).reshape(bsz, n, N_HEADS, GROUP_W // N_HEADS)
    f = jnp.fft.fft2(a4, axes=(1, 3), norm="ortho").real
    return f.reshape(bsz, n, GROUP_W).astype(a.dtype)


def dense_attention(q, k, v):
    s = jnp.einsum('bqhd,bkhd->bhqk', q, k).astype(jnp.float32) * (q.shape[-1] ** -0.5)
    p = jax.nn.softmax(s, axis=-1).astype(v.dtype)
    return jnp.einsum('bhqk,bkhd->bqhd', p, v)


def neighbourhood_attention(q, k, v, k_ctx, v_ctx, rpb, rows):
    bsz, n, h, dh = q.shape
    wr = min(NA_WIN_ROWS, rows)
    qg = q.reshape(bsz, rows, GRID_W, h, dh)
    kg = k.reshape(bsz, rows, GRID_W, h, dh)
    vg = v.reshape(bsz, rows, GRID_W, h, dh)
    r = jnp.arange(rows)
    row_start = jnp.clip(r - wr // 2, 0, rows - wr)
    key_rows = row_start[:, None] + jnp.arange(wr)[None, :]
    k_band = kg[:, key_rows]
    v_band = vg[:, key_rows]
    cidx = jnp.arange(GRID_W)
    col_start = jnp.clip(cidx - NA_WIN_COLS // 2, 0, GRID_W - NA_WIN_COLS)
    in_win = (cidx[None, :] >= col_start[:, None]) & (cidx[None, :] < col_start[:, None] + NA_WIN_COLS)
    dr = key_rows - r[:, None] + NA_WIN_ROWS - 1
    dc = jnp.clip(cidx[None, :] - cidx[:, None], 1 - NA_WIN_COLS, NA_WIN_COLS - 1) + NA_WIN_COLS - 1
    bias = rpb[:, dr[:, None, :, None], dc[None, :, None, :]]
    scale = dh ** -0.5
    s_loc = jnp.einsum('brqhd,brikhd->bhrqik', qg, k_band).astype(jnp.float32) * scale + bias.astype(jnp.float32)
    s_loc = jnp.where(in_win[:, None, :], s_loc, -jnp.inf)
    s_loc = s_loc.reshape(bsz, h, rows, GRID_W, wr * GRID_W)
    s_ctx = jnp.einsum('brqhd,blhd->bhrql', qg, k_ctx).astype(jnp.float32) * scale
    p = jax.nn.softmax(jnp.concatenate([s_loc, s_ctx], axis=-1), axis=-1).astype(v.dtype)
    p_loc = p[..., :wr * GRID_W].reshape(bsz, h, rows, GRID_W, wr, GRID_W)
    p_ctx = p[..., wr * GRID_W:]
    o = jnp.einsum('bhrqik,brikhd->brqhd', p_loc, v_band) + jnp.einsum('bhrql,blhd->brqhd', p_ctx, v_ctx)
    return o.reshape(bsz, n, h * dh)


def spatial_gating(uv, ln_g, ln_b, w_s, b_s):
    bsz, n, _ = uv.shape
    u, v = jnp.split(jax.nn.gelu(uv), 2, axis=-1)
    v = layer_norm(v, ln_g, ln_b)
    vc = v.reshape(bsz, n // SGU_CHUNK, SGU_CHUNK, N_HEADS, GROUP_W // N_HEADS)
    s = jnp.einsum('gpq,bnqgc->bnpgc', w_s, vc) + b_s.T[None, None, :, :, None]
    return u * s.reshape(bsz, n, GROUP_W)


def axial_rope(n):
    t = jnp.arange(n)
    rows = (t // GRID_W).astype(jnp.float32)
    cols = (t % GRID_W).astype(jnp.float32)
    n_freq = DIFF_QK_DIM // 4
    inv = ROPE_BASE ** (-jnp.arange(n_freq, dtype=jnp.float32) / n_freq)
    ang = jnp.concatenate([rows[:, None] * inv, cols[:, None] * inv], axis=-1)
    return jnp.cos(ang), jnp.sin(ang)


def apply_rope(x, cos, sin):
    nf = DIFF_QK_DIM // 4
    xs = x.reshape(x.shape[:-1] + (2, 2, nf))
    x1, x2 = xs[..., 0, :], xs[..., 1, :]
    c = cos.reshape(cos.shape[0], 1, 2, nf).astype(x.dtype)
    s = sin.reshape(sin.shape[0], 1, 2, nf).astype(x.dtype)
    out = jnp.stack([x1 * c - x2 * s, x1 * s + x2 * c], axis=-2)
    return out.reshape(x.shape)


def diff_attend(q1, q2, k1, k2, v, lam):
    scale = DIFF_QK_DIM ** -0.5
    s1 = jnp.einsum('bqhd,bkhd->bhqk', q1, k1).astype(jnp.float32) * scale
    s2 = jnp.einsum('bqhd,bkhd->bhqk', q2, k2).astype(jnp.float32) * scale
    p = jax.nn.softmax(s1, axis=-1) - lam * jax.nn.softmax(s2, axis=-1)
    return jnp.einsum('bhqk,bkhd->bqhd', p.astype(v.dtype), v)


def diff_latent(q, k, v, k_ctx, v_ctx, lam, cos, sin):
    bsz, n, h, _ = q.shape
    q1 = apply_rope(q[..., :DIFF_QK_DIM], cos, sin)
    q2 = apply_rope(q[..., DIFF_QK_DIM:], cos, sin)
    k1 = jnp.concatenate([apply_rope(k[..., :DIFF_QK_DIM], cos, sin), k_ctx[..., :DIFF_QK_DIM]], axis=1)
    k2 = jnp.concatenate([apply_rope(k[..., DIFF_QK_DIM:], cos, sin), k_ctx[..., DIFF_QK_DIM:]], axis=1)
    v_all = jnp.concatenate([v, v_ctx], axis=1)
    nb = n // Q_BLOCK

    def to_blocks(t):
        return t.reshape(bsz, nb, Q_BLOCK, h, DIFF_QK_DIM).swapaxes(0, 1)

    o = lax.map(lambda qb: diff_attend(qb[0], qb[1], k1, k2, v_all, lam), (to_blocks(q1), to_blocks(q2)))
    return o.swapaxes(0, 1).reshape(bsz, n, h, HEAD_DIM)


def diff_post(o, g, lam_init):
    y = rms_norm(o, g) * (1 - lam_init)
    return y.reshape(o.shape[0], o.shape[1], GROUP_W)


def sq_relu_mlp(h, w1, w2):
    return jnp.square(jax.nn.relu(h @ w1)) @ w2


def setup_inputs(seed: int = 0) -> dict:
    key = jax.random.key(seed)
    ks = jax.random.split(key, 24)
    f32 = jnp.float32

    def nrm(k, shape, s):
        return jax.random.normal(k, shape, f32) * s

    return {
        "x": nrm(ks[0], (BATCH, SEQ, D_MODEL), 1.0),
        "c": nrm(ks[1], (BATCH, D_MODEL), 1.0),
        "ctx": nrm(ks[2], (BATCH, CTX_LEN, D_MODEL), 1.0),
        "c_ctx": nrm(ks[3], (D_MODEL,), 1.0),
        "ada_w": nrm(ks[4], (DEPTH, D_MODEL, 6 * D_MODEL), 0.5 * D_MODEL ** -0.5),
        "ada_b": nrm(ks[5], (DEPTH, 6 * D_MODEL), 0.01),
        "norm1_g": 1.0 + nrm(ks[6], (DEPTH, D_MODEL), 0.02),
        "norm2_g": 1.0 + nrm(ks[7], (DEPTH, D_MODEL), 0.02),
        "w_in": nrm(ks[8], (DEPTH, D_MODEL, D_IN), D_MODEL ** -0.5),
        "w_out": nrm(ks[9], (DEPTH, D_MODEL, D_MODEL), D_MODEL ** -0.5),
        "na_rpb": nrm(ks[10], (DEPTH, N_HEADS, 2 * NA_WIN_ROWS - 1, 2 * NA_WIN_COLS - 1), 0.1),
        "sgu_ln_g": 1.0 + nrm(ks[11], (DEPTH, GROUP_W), 0.02),
        "sgu_ln_b": nrm(ks[12], (DEPTH, GROUP_W), 0.02),
        "sgu_w": nrm(ks[13], (DEPTH, N_HEADS, SGU_CHUNK, SGU_CHUNK), SGU_CHUNK ** -0.5),
        "sgu_b": 1.0 + nrm(ks[14], (DEPTH, N_HEADS, SGU_CHUNK), 0.02),
        "diff_lq1": nrm(ks[15], (DEPTH, DIFF_QK_DIM), 0.1),
        "diff_lk1": nrm(ks[16], (DEPTH, DIFF_QK_DIM), 0.1),
        "diff_lq2": nrm(ks[17], (DEPTH, DIFF_QK_DIM), 0.1),
        "diff_lk2": nrm(ks[18], (DEPTH, DIFF_QK_DIM), 0.1),
        "diff_subln_g": 1.0 + nrm(ks[19], (DEPTH, HEAD_DIM), 0.02),
        "w_ff1": nrm(ks[20], (DEPTH, D_MODEL, D_FF), D_MODEL ** -0.5),
        "w_ff2": nrm(ks[21], (DEPTH, D_FF, D_MODEL), D_FF ** -0.5),
        "final_g": 1.0 + nrm(ks[22], (D_MODEL,), 0.02),
    }


def reference(x, c, ctx, c_ctx, ada_w, ada_b, norm1_g, norm2_g, w_in, w_out, na_rpb, sgu_ln_g, sgu_ln_b,
              sgu_w, sgu_b, diff_lq1, diff_lk1, diff_lq2, diff_lk2, diff_subln_g, w_ff1, w_ff2, final_g):
    bsz, n, _ = x.shape
    rows = n // GRID_W
    cos, sin = axial_rope(n)
    cx = ctx
    gw = GROUP_W
    for l in range(DEPTH):
        last = l == DEPTH - 1
        sh1, sc1, g1, sh2, sc2, g2 = ada_params(c[:, None, :], ada_w[l], ada_b[l])
        csh1, csc1, cg1, csh2, csc2, cg2 = ada_params(c_ctx[None, None, :], ada_w[l], ada_b[l])
        h = modulate(rms_norm(x, norm1_g[l]), sh1, sc1)
        hc = modulate(rms_norm(cx, norm1_g[l]), csh1, csc1)
        w = w_in[l]
        p = h @ w
        if last:
            na_kv_c = hc @ w[:, COL_NA + gw:COL_NA + 3 * gw]
            df_kv_c = hc @ w[:, COL_DIFF + gw:COL_DIFF + 3 * gw]
        else:
            pc = hc @ w
            na_kv_c = pc[..., COL_NA + gw:COL_NA + 3 * gw]
            df_kv_c = pc[..., COL_DIFF + gw:COL_DIFF + 3 * gw]
        na_kc, na_vc = heads(na_kv_c[..., :gw]), heads(na_kv_c[..., gw:])
        df_kc, df_vc = heads(df_kv_c[..., :gw]), heads(df_kv_c[..., gw:])
        lam_init = 0.8 - 0.6 * math.exp(-0.3 * l)
        lam = (jnp.exp(jnp.sum(diff_lq1[l].astype(jnp.float32) * diff_lk1[l].astype(jnp.float32)))
               - jnp.exp(jnp.sum(diff_lq2[l].astype(jnp.float32) * diff_lk2[l].astype(jnp.float32))) + lam_init)

        y_a = fourier_mix(p[..., COL_FOURIER:COL_FOURIER + gw])
        y_b = neighbourhood_attention(heads(p[..., COL_NA:COL_NA + gw]), heads(p[..., COL_NA + gw:COL_NA + 2 * gw]),
                                      heads(p[..., COL_NA + 2 * gw:COL_NA + 3 * gw]), na_kc, na_vc, na_rpb[l], rows)
        y_c = spatial_gating(p[..., COL_SGU:COL_SGU + 2 * gw], sgu_ln_g[l], sgu_ln_b[l], sgu_w[l], sgu_b[l])
        o_d = diff_latent(heads(p[..., COL_DIFF:COL_DIFF + gw]), heads(p[..., COL_DIFF + gw:COL_DIFF + 2 * gw]),
                          heads(p[..., COL_DIFF + 2 * gw:COL_DIFF + 3 * gw]), df_kc, df_vc, lam, cos, sin)
        y_d = diff_post(o_d, diff_subln_g[l], lam_init)
        y = jnp.concatenate([y_a, y_b, y_c, y_d], axis=-1) @ w_out[l]

        if not last:
            yc_a = fourier_mix(pc[..., COL_FOURIER:COL_FOURIER + gw])
            yc_b = dense_attention(heads(pc[..., COL_NA:COL_NA + gw]), na_kc, na_vc).reshape(bsz, -1, gw)
            yc_c = spatial_gating(pc[..., COL_SGU:COL_SGU + 2 * gw], sgu_ln_g[l], sgu_ln_b[l], sgu_w[l], sgu_b[l])
            qc = heads(pc[..., COL_DIFF:COL_DIFF + gw])
            oc_d = diff_attend(qc[..., :DIFF_QK_DIM], qc[..., DIFF_QK_DIM:], df_kc[..., :DIFF_QK_DIM],
                               df_kc[..., DIFF_QK_DIM:], df_vc, lam)
            yc_d = diff_post(oc_d, diff_subln_g[l], lam_init)
            yc = jnp.concatenate([yc_a, yc_b, yc_c, yc_d], axis=-1) @ w_out[l]
            cx = cx + cg1 * yc
            hc2 = modulate(rms_norm(cx, norm2_g[l]), csh2, csc2)
            cx = cx + cg2 * sq_relu_mlp(hc2, w_ff1[l], w_ff2[l])

        x = x + g1 * y
        h2 = modulate(rms_norm(x, norm2_g[l]), sh2, sc2)
        x = x + g2 * sq_relu_mlp(h2, w_ff1[l], w_ff2[l])
    return rms_norm(x, final_g)
```

```python
import contextlib
import math
import numpy as np
import ml_dtypes
import concourse.bass as bass
import concourse.mybir as mybir
from concourse.bass_utils import run_bass_kernel_spmd

F32 = mybir.dt.float32
BF16 = mybir.dt.bfloat16
AF = mybir.ActivationFunctionType
ALU = mybir.AluOpType
AX = mybir.AxisListType

D = 1024
SEQ = 4096
CTX = 256
NTOK = SEQ + CTX
NTILE = NTOK // 128
L = 4
GW = 256
DFF = 4096
DIN = 2304
EPS = 1e-6
NFM = 14
WIN = NFM * 128 + 1024
NEG = -30000.0

ENGS = ("pe", "act", "dve", "pool", "sp")


class Tk:
    def __init__(self, name):
        self.name = name
        self.w = {}
        self.r = {}
        self.slots = {}
        self.dram = False


class Dk(Tk):
    def __init__(self, name):
        super().__init__(name)
        self.dram = True


class Slot:
    def __init__(self, sem):
        self.sem = sem
        self.cnt = 0


class Rec:
    def __init__(self, nc, es, nslots=44):
        self.nc = nc
        self.sem = {e: es.enter_context(nc.semaphore("sem_" + e)) for e in ENGS}
        self.n = {e: 0 for e in ENGS}
        self.seen = {e: {} for e in ENGS}
        self.prog = {e: [] for e in ENGS}
        self.pools = {q: [Slot(es.enter_context(nc.semaphore("d%s%d" % (q, i)))) for i in range(nslots)]
                      for q in ("sp", "pool")}
        self.base = {"sp": 0, "pool": 0}
        self.nxt = {"sp": 0, "pool": 0}

    def tile(self, name, dma=False):
        return Tk(name)

    def _slot(self, t, q):
        sl = t.slots.get(q)
        if sl is None:
            assert self.nxt[q] < len(self.pools[q]), "out of dma semaphores"
            sl = self.pools[q][self.nxt[q]]
            self.nxt[q] += 1
            t.slots[q] = sl
        return sl

    def freeze_global(self):
        self.base = dict(self.nxt)

    def _deps(self, reads, writes, own=None, own_t=None):
        deps = {}

        def add(d, skip=None):
            for k, (sem, v) in d.items():
                if skip is not None and k == skip:
                    continue
                if deps.get(k, (None, 0))[1] < v:
                    deps[k] = (sem, v)
        for t in reads:
            add(t.w)
        for t in writes:
            if t.dram:
                add(t.r)
            else:
                add(t.w, skip=(own if (own_t is t) else None))
                add(t.r)
        return deps

    def _filter(self, eng, deps):
        seen = self.seen[eng]
        out = []
        for k, (sem, v) in deps.items():
            if eng == "pe" and k == self.sem["pe"].num:
                continue
            if seen.get(k, 0) >= v:
                continue
            seen[k] = v
            out.append((sem, v))
        return out

    def _mark(self, ev, reads, writes):
        k = ev[0].num
        for t in reads:
            t.r[k] = ev
        for t in writes:
            if t.dram:
                t.w[k] = ev
            else:
                t.w = {k: ev}
                t.r = {}

    def op(self, eng, fn, reads=(), writes=(), inc=True):
        waits = self._filter(eng, self._deps(reads, writes))
        ev = (self.sem[eng], self.n[eng] + 1)
        if inc:
            self.n[eng] += 1
        self.prog[eng].append((waits, fn, inc, None))
        self._mark(ev, reads, writes)

    def dma(self, q, out, in_, st, reads=(), writes=()):
        sl = self._slot(st, q)
        waits = self._filter(q, self._deps(reads, writes, own=sl.sem.num, own_t=st))
        sl.cnt += 16
        ev = (sl.sem, sl.cnt)
        self.prog[q].append((waits, (lambda e: e.dma_start(out=out, in_=in_)), False, sl.sem))
        self._mark(ev, reads, writes)

    def end_phase(self, blk):
        deps = {s.sem.num: (s.sem, s.cnt) for q in self.pools for s in self.pools[q] if s.cnt > 0}
        self.prog["sp"].append((self._filter("sp", deps), None, False, None))
        names = dict(pe="tensor", act="scalar", dve="vector", pool="gpsimd", sp="sync")
        for e in ENGS:
            prog = self.prog[e]
            sem_e = self.sem[e]

            def body(eh, prog=prog, sem_e=sem_e):
                for waits, fn, inc, dsem in prog:
                    for sem, v in waits:
                        eh.wait_ge(sem, v)
                    if fn is None:
                        continue
                    ins = fn(eh)
                    if dsem is not None:
                        ins.then_inc(dsem, 16)
                    elif inc:
                        ins.then_inc(sem_e, 1)
            getattr(blk, names[e])(body)
        self.prog = {e: [] for e in ENGS}
        self.nxt = dict(self.base)


def build(nlayers=L, dbg=()):
    nc = bass.Bass("TRN2", target_bir_lowering=False)

    def din(name, shape, dt=F32):
        return nc.dram_tensor(name, list(shape), dt, kind="ExternalInput").ap()

    def dscr(name, shape, dt):
        kind = "ExternalOutput" if name in dbg else "Internal"
        return nc.dram_tensor(name, list(shape), dt, kind=kind).ap()

    xin = din("xin", [NTOK, D])
    cc = din("cc", [128, 8, 2])
    ada_w = din("ada_w", [L, D, 6 * D])
    ada_b = din("ada_b", [L, 6 * D])
    ident2_d = din("ident2", [2, 2])
    n1gc = din("n1gc", [128, L, 8])
    n2gc = din("n2gc", [128, L, 8])
    w_in_r = din("w_in_r", [L, D, WIN])
    w_out_r = din("w_out_r", [L, 10 * 128, D])
    w_ff1 = din("w_ff1", [L, D, DFF])
    w_ff2 = din("w_ff2", [L, DFF, D])
    maskT = din("maskT", [L, 128, 12800])
    sgu_wT = din("sgu_wT", [L, 128, 512])
    sgu_bc = din("sgu_bc", [128, L, 4])
    sgu_lng = din("sgu_lng", [L, GW])
    sgu_lnb = din("sgu_lnb", [L, GW])
    dl = [din("diff_l%d" % i, [L, 32]) for i in range(4)]
    subg_c = din("subg_c", [64, L])
    final_g = din("final_g", [1, D])
    identb_d = din("identb", [128, 128], BF16)
    cs_d = din("cs_tab", [128, 256], BF16)
    dft_d = din("dft", [2, 8, 128, 32 * 512], BF16)
    dft256_d = din("dft256", [128, 2 * 2 * 256], BF16)
    cos_d = din("cos_tab", [128, NTOK])
    sin_d = din("sin_tab", [128, NTOK])
    sel_d = din("sel65", [65, 64])
    ones64_d = din("ones64", [64, 64])
    onesb_d = din("onesb", [128, 128], BF16)
    laminit_d = din("laminit", [64, L])
    omli_d = din("omli", [64, L])
    out_d = nc.dram_tensor("out", [SEQ, D], F32, kind="ExternalOutput").ap()

    xres = dscr("xres", [NTOK, D], F32)
    Bd = dscr("Bd", [NTOK, 512], BF16)
    fmT = dscr("fmT", [8, 128, NTOK], BF16)
    vtok = dscr("vtok", [NTOK, 772], BF16)
    yT = dscr("yT", [10, 128, NTOK], BF16)
    gates = dscr("gates", [L, 2, 2 * D], F32)
    w_in_b = dscr("w_in_b", [L, D, WIN], BF16)
    w_out_b = dscr("w_out_b", [L, 10 * 128, D], BF16)
    w_ff1_b = dscr("w_ff1_b", [L, D, DFF], BF16)
    w_ff2_b = dscr("w_ff2_b", [L, DFF, D], BF16)
    maskT_b = dscr("maskT_b", [L, 128, 12800], BF16)
    sgu_wT_b = dscr("sgu_wT_b", [L, 128, 512], BF16)

    Dx, DB, Dfm, Dv, Dy, Dg, Dw, Dout = (Dk("x"), Dk("B"), Dk("fm"), Dk("v"), Dk("y"), Dk("g"), Dk("w"),
                                          Dk("out"))

    with contextlib.ExitStack() as es:
        R = Rec(nc, es)

        uid = [0]

        def SB(st, name, shape, dt, dma=False):
            uid[0] += 1
            nm = "s%d_%s" % (uid[0], name)
            return st.enter_context(nc.sbuf_tensor(nm, list(shape), dt)), R.tile(nm, dma)

        def PS(st, name, shape, dt=F32):
            uid[0] += 1
            nm = "p%d_%s" % (uid[0], name)
            return st.enter_context(nc.psum_tensor(nm, list(shape), dt)), R.tile(nm)

        def mm(out, lhsT, rhs, start, stop, rd, wr, inc=True, skip=False):
            R.op("pe", lambda e: e.matmul(out, lhsT=lhsT, rhs=rhs, start=start, stop=stop,
                                          skip_group_check=skip), rd, wr, inc)

        def tr(out, in_, ident, rd, wr, inc=True):
            R.op("pe", lambda e: e.transpose(out=out, in_=in_, identity=ident), rd, wr, inc)

        def act(out, in_, func, rd, wr, **kw):
            R.op("act", lambda e: e.activation(out=out, in_=in_, func=func, **kw), rd, wr)

        def tt(eng, out, in0, in1, op, rd, wr):
            R.op(eng, lambda e: e.tensor_tensor(out=out, in0=in0, in1=in1, op=op), rd, wr)

        def ts(eng, out, in0, s1, s2, op0, op1, rd, wr):
            if op1 is None:
                R.op(eng, lambda e: e.tensor_scalar(out=out, in0=in0, scalar1=s1, scalar2=None, op0=op0), rd, wr)
            else:
                R.op(eng, lambda e: e.tensor_scalar(out=out, in0=in0, scalar1=s1, scalar2=s2, op0=op0, op1=op1),
                     rd, wr)

        def stt(out, in0, scalar, in1, op0, op1, rd, wr):
            R.op("dve", lambda e: e.scalar_tensor_tensor(out=out, in0=in0, scalar=scalar, in1=in1, op0=op0,
                                                        op1=op1), rd, wr)

        def recip(out, in_, rd, wr):
            R.op("dve", lambda e: e.reciprocal(out=out, in_=in_), rd, wr)

        def cp(eng, out, in_, rd, wr):
            R.op(eng, lambda e: e.tensor_copy(out=out, in_=in_), rd, wr)

        def memset(eng, ap, val, wr):
            R.op(eng, lambda e: e.memset(ap, val), (), wr)

        def bcast(ap2d, rows):
            n = ap2d.shape[-1]
            return bass.AP(tensor=ap2d.tensor, offset=ap2d.offset, ap=[[0, rows], [1, n]])

        identb, Tid = SB(es, "identb", [128, 128], BF16, True)
        colp, Tcolp = SB(es, "colp", [128, L, 4, 8, 2], F32)
        neglam, Tnl = SB(es, "neglam", [64, L], F32)
        gsub, Tgs = SB(es, "gsub", [64, L], F32, True)
        sgub, Tsgub = SB(es, "sgub", [128, L, 4], F32, True)
        sel65, Tsel = SB(es, "sel65", [65, 64], F32, True)
        ones64, Tones = SB(es, "ones64", [64, 64], F32, True)
        onesb, Tonesb = SB(es, "onesb", [128, 128], BF16, True)
        R.freeze_global()

        def cast_weights(l):
            Twc = R.tile("wcast%d" % l, True)
            for (src, dst, rows) in ((w_in_r, w_in_b, D), (w_out_r, w_out_b, 1280), (w_ff1, w_ff1_b, D),
                                     (w_ff2, w_ff2_b, DFF), (maskT, maskT_b, 128), (sgu_wT, sgu_wT_b, 128)):
                for r0 in range(0, rows, 512):
                    r1_ = min(rows, r0 + 512)
                    R.dma("pool", dst[l, r0:r1_, :], src[l, r0:r1_, :], Twc, writes=[Dw])

        with contextlib.ExitStack() as ps:
            R.dma("sp", identb[:], identb_d[:, :], Tid, writes=[Tid])
            R.dma("sp", sgub[:], sgu_bc[:, :, :], Tsgub, writes=[Tsgub])
            R.dma("sp", sel65[:], sel_d[:, :], Tsel, writes=[Tsel])
            R.dma("sp", ones64[:], ones64_d[:, :], Tones, writes=[Tones])
            R.dma("sp", onesb[:], onesb_d[:, :], Tonesb, writes=[Tonesb])
            cast_weights(0)
            zt, Tzt = SB(ps, "zt", [64, NTOK], BF16, True)
            memset("pool", zt[:], 0.0, [Tzt])
            for h in range(4):
                R.dma("sp", yT[6 + h, 64:128, :], zt[:], Tzt, reads=[Tzt], writes=[Dy])
            cct, Tcc = SB(ps, "cct", [128, 8, 2], F32, True)
            sct, Tsc = SB(ps, "sct", [128, 8, 2], F32)
            g1c, Tg1c = SB(ps, "g1c", [128, L, 8], F32, True)
            g2c, Tg2c = SB(ps, "g2c", [128, L, 8], F32, True)
            ab2, Tab2 = SB(ps, "ab2", [2, 6 * D], F32, True)
            idf2, Tidf2 = SB(ps, "idf2", [2, 2], F32, True)
            rows_sb, Trows = SB(ps, "rows_sb", [2, 6 * D], F32, True)
            aw = [SB(ps, "aw%d" % i, [128, 8, D], F32, True) for i in range(2)]
            colps_f, Tcolps = PS(ps, "colps", [128, 512])
            colps = colps_f[:, 0:64].rearrange("p (w f s) -> p w f s", w=4, f=8)
            rowps = [PS(ps, "rowps%d" % i, [2, 512]) for i in range(4)]
            R.dma("sp", cct[:], cc[:, :, :], Tcc, writes=[Tcc])
            R.dma("sp", g1c[:], n1gc[:, :, :], Tg1c, writes=[Tg1c])
            R.dma("sp", g2c[:], n2gc[:, :, :], Tg2c, writes=[Tg2c])
            R.dma("sp", idf2[:], ident2_d[:, :], Tidf2, writes=[Tidf2])
            act(sct[:], cct[:], AF.Silu, [Tcc], [Tsc])
            slab_w = {0: 0, 1: 1, 3: 2, 4: 3}
            ai = 0
            for l in range(nlayers):
                R.dma("sp", ab2[:], bass.AP(tensor=ada_b.tensor, offset=l * 6 * D, ap=[[0, 2], [1, 6 * D]]), Tab2,
                      writes=[Tab2])
                awv = ada_w[l].rearrange("(k p) n -> p k n", p=128)
                for sl in range(6):
                    awt, Taw = aw[ai % 2]
                    ai += 1
                    R.dma("sp", awt[:], awv[:, :, sl * D:(sl + 1) * D], Taw, writes=[Taw])
                    for j in range(2):
                        rp, Trp = rowps[(sl % 2) * 2 + j]
                        for k in range(8):
                            mm(rp[:, :], sct[:, k, :], awt[:, k, j * 512:(j + 1) * 512], k == 0, k == 7, [Taw, Tsc],
                               [Trp], inc=(k == 7))
                        c0 = sl * D + j * 512
                        tt("dve", rows_sb[:, c0:c0 + 512], rp[:, :], ab2[:, c0:c0 + 512], ALU.add, [Trp, Tab2], [Trows])
                    if sl in slab_w:
                        w = slab_w[sl]
                        for fc in range(8):
                            tr(colps[:, w, fc, :], rows_sb[:, sl * D + fc * 128: sl * D + (fc + 1) * 128], idf2[:],
                               [Trows, Tidf2], [Tcolps], inc=(fc == 7))
                R.dma("sp", gates[l, :, 0:D], rows_sb[:, 2 * D:3 * D], Trows, reads=[Trows], writes=[Dg])
                R.dma("sp", gates[l, :, D:2 * D], rows_sb[:, 5 * D:6 * D], Trows, reads=[Trows], writes=[Dg])
                cp("dve", colp[:, l, :, :, :], colps[:, :, :, :], [Tcolps], [Tcolp])
                for s_ in range(2):
                    for (w, gt, Tg) in ((1, g1c, Tg1c), (3, g2c, Tg2c)):
                        stt(colp[:, l, w, :, s_], colp[:, l, w, :, s_], 1.0, gt[:, l, :], ALU.add, ALU.mult,
                            [Tcolp, Tg], [Tcolp])
            lq = [SB(ps, "lq%d" % i, [64, L, 32], F32, True) for i in range(4)]
            li, Tli = SB(ps, "li", [64, L], F32, True)
            om, Tom = SB(ps, "om", [64, L], F32, True)
            sgc, Tsgc = SB(ps, "sgc", [64, L], F32, True)
            pr, Tpr = SB(ps, "pr", [64, 2, L, 32], F32)
            sm, Tsm = SB(ps, "sm", [64, 2, L], F32)
            for i in range(4):
                src = bass.AP(tensor=dl[i].tensor, offset=0, ap=[[0, 64], [1, L * 32]])
                R.dma("sp", lq[i][0][:].rearrange("p l d -> p (l d)"), src, lq[i][1], writes=[lq[i][1]])
            R.dma("sp", li[:], laminit_d[:, :], Tli, writes=[Tli])
            R.dma("sp", om[:], omli_d[:, :], Tom, writes=[Tom])
            R.dma("sp", sgc[:], subg_c[:, :], Tsgc, writes=[Tsgc])
            for m in range(2):
                tt("dve", pr[:, m, :, :], lq[2 * m][0][:], lq[2 * m + 1][0][:], ALU.mult,
                   [lq[2 * m][1], lq[2 * m + 1][1]], [Tpr])
            R.op("dve", lambda e: e.tensor_reduce(out=sm[:], in_=pr[:], axis=AX.X, op=ALU.add), [Tpr], [Tsm])
            act(sm[:], sm[:], AF.Exp, [Tsm], [Tsm])
            tt("dve", neglam[:], sm[:, 1, :], sm[:, 0, :], ALU.subtract, [Tsm], [Tnl])
            tt("dve", neglam[:], neglam[:], li[:], ALU.subtract, [Tnl, Tli], [Tnl])
            tt("dve", gsub[:], sgc[:], om[:], ALU.mult, [Tsgc, Tom], [Tgs])
            with nc.Block() as blk:
                R.end_phase(blk)

        blocks = [(i * 512, 512, 0) for i in range(8)] + [(SEQ, 256, 1)]

        def norm_A1(st_tiles, xt, Txt, nt, nhalf=None):
            junk, Tjunk, ss, Tss, rstd, Trstd, xn, Txn = st_tiles
            for j in range(nt):
                act(junk[:], xt[:, j, :], AF.Square, [Txt], [Tjunk, Tss], accum_out=ss[:, j:j + 1])
            if nhalf is None:
                act(rstd[:, 0:nt], ss[:, 0:nt], AF.Sqrt, [Tss], [Trstd], scale=1.0 / D, bias=EPS)
                recip(rstd[:, 0:nt], rstd[:, 0:nt], [Trstd], [Trstd])
            else:
                ts("dve", rstd[:, 0:nt], ss[:, 0:nt], 1.0 / D, EPS, ALU.mult, ALU.add, [Tss], [Trstd])
                tt("pool", rstd[:, 0:nt], rstd[:, 0:nt], nhalf[0][:, 0:nt], ALU.pow, [Trstd, nhalf[1]], [Trstd])
            for j in range(nt):
                if j % 2 == 0:
                    act(xn[:, j, :], xt[:, j, :], AF.Copy, [Txt, Trstd], [Txn], scale=rstd[:, j:j + 1])
                else:
                    ts("dve", xn[:, j, :], xt[:, j, :], rstd[:, j:j + 1], None, ALU.mult, None, [Txt, Trstd], [Txn])

        def norm_A2(st_tiles, nt, l, wsh, wsc, s, tps, hT, ThT):
            junk, Tjunk, ss, Tss, rstd, Trstd, xn, Txn = st_tiles
            for k in range(8):
                tp, Ttp = tps[k % 2]
                for j in range(nt):
                    tr(tp[:, j * 128:(j + 1) * 128], xn[:, j, k * 128:(k + 1) * 128], identb[:], [Txn, Tid], [Ttp],
                       inc=(j == nt - 1))
                if k % 2 == 0:
                    act(hT[:, k, 0:nt * 128], tp[:, 0:nt * 128], AF.Identity, [Ttp, Tcolp], [ThT],
                        scale=colp[:, l, wsc, k, s:s + 1], bias=colp[:, l, wsh, k, s:s + 1])
                else:
                    ts("dve", hT[:, k, 0:nt * 128], tp[:, 0:nt * 128], colp[:, l, wsc, k, s:s + 1],
                       colp[:, l, wsh, k, s:s + 1], ALU.mult, ALU.add, [Ttp, Tcolp], [ThT])

        def norm_to_hT(st_tiles, xt, Txt, nt, l, wsh, wsc, s, tps, hT, ThT, tag):
            norm_A1(st_tiles, xt, Txt, nt)
            norm_A2(st_tiles, nt, l, wsh, wsc, s, tps, hT, ThT)

        for l in range(nlayers):
            last = (l == L - 1)
            xsrc = xin if l == 0 else xres
            Dxs = Dk("xin") if l == 0 else Dx

            with contextlib.ExitStack() as ps:
                wi, Twi = SB(ps, "wi", [128, 8, WIN], BF16, True)
                cst, Tcs = SB(ps, "cst", [128, 256], BF16, True)
                wsT, TwsT = SB(ps, "wsT", [128, 512], BF16, True)
                lng, Tlng = SB(ps, "lng", [128, GW], F32, True)
                lnb, Tlnb = SB(ps, "lnb", [128, GW], F32, True)
                xts = [SB(ps, "xt%d" % i, [128, 4, D], F32, True) for i in range(3)]
                junk, Tjunk = SB(ps, "junk", [128, D], BF16)
                nst = []
                for i in range(2):
                    ss_, Tss_ = SB(ps, "ss%d" % i, [128, 4], F32)
                    rstd_, Trstd_ = SB(ps, "rstd%d" % i, [128, 4], F32)
                    xn_, Txn_ = SB(ps, "xn%d" % i, [128, 4, D], BF16)
                    nst.append((junk, Tjunk, ss_, Tss_, rstd_, Trstd_, xn_, Txn_))
                hTs = [SB(ps, "hT%d" % i, [128, 8, 512], BF16) for i in range(2)]
                coss = [SB(ps, "cost%d" % i, [128, 512], F32, True) for i in range(3)]
                sins = [SB(ps, "sint%d" % i, [128, 512], F32, True) for i in range(3)]
                aT, TaT = SB(ps, "aT", [128, 2, 512], BF16)
                Bsb, TBsb = SB(ps, "Bsb", [128, 4, 512], BF16, True)
                fmo = [SB(ps, "fmo%d" % i, [128, 512], BF16, True) for i in range(3)]
                r1, Tr1 = SB(ps, "r1", [128, 512], F32)
                r2, Tr2 = SB(ps, "r2", [128, 512], F32)
                vt, Tvt = SB(ps, "vt", [128, 4, 772], BF16, True)
                bst, Tbst = SB(ps, "bst", [128, 6], F32)
                mv, Tmv = SB(ps, "mv", [128, 2], F32)
                vn, Tvn = SB(ps, "vn", [128, GW], F32)
                tps = [PS(ps, "tp%d" % i, [128, 1024], BF16) for i in range(2)]
                fps = [PS(ps, "fps%d" % i, [128, 512]) for i in range(2)]
                tms = [PS(ps, "tms%d" % i, [128, 512]) for i in range(3)]
                sgp, Tsgp = PS(ps, "sgp", [128, 1024], BF16)

                R.dma("sp", wi[:], w_in_b[l].rearrange("(k p) n -> p k n", p=128), Twi, reads=[Dw], writes=[Twi])
                R.dma("sp", cst[:], cs_d[:, :], Tcs, writes=[Tcs])
                R.dma("sp", wsT[:], sgu_wT_b[l], TwsT, reads=[Dw], writes=[TwsT])
                R.dma("sp", lng[:], bcast(sgu_lng[l:l + 1, :], 128), Tlng, writes=[Tlng])
                R.dma("sp", lnb[:], bcast(sgu_lnb[l:l + 1, :], 128), Tlnb, writes=[Tlnb])
                memset("pool", vt[:], 1.0, [Tvt])
                nhf, Tnhf = SB(ps, "nhf", [128, 4], F32)
                memset("pool", nhf[:], -0.5, [Tnhf])
                fmi = 0
                nblk = len(blocks)

                def p1_LD(bi):
                    t0, ntok, s = blocks[bi]
                    xt, Txt = xts[bi % 3]
                    R.dma("sp", xt[:, 0:ntok // 128, :], xsrc[t0:t0 + ntok, :].rearrange("(j p) d -> p j d", p=128),
                          Txt, reads=[Dxs], writes=[Txt])
                    R.dma("sp", coss[bi % 3][0][:, 0:ntok], cos_d[:, t0:t0 + ntok], coss[bi % 3][1],
                          writes=[coss[bi % 3][1]])
                    R.dma("sp", sins[bi % 3][0][:, 0:ntok], sin_d[:, t0:t0 + ntok], sins[bi % 3][1],
                          writes=[sins[bi % 3][1]])

                def p1_A1(bi):
                    t0, ntok, s = blocks[bi]
                    norm_A1(nst[bi % 2], xts[bi % 3][0], xts[bi % 3][1], ntok // 128, nhalf=(nhf, Tnhf))

                def p1_A2(bi):
                    t0, ntok, s = blocks[bi]
                    norm_A2(nst[bi % 2], ntok // 128, l, 0, 1, s, tps, hTs[bi % 2][0], hTs[bi % 2][1])

                gels = [SB(ps, "gel%d" % i, [128, 512], F32) for i in range(3)]
                vlns = [SB(ps, "vln%d" % i, [128, GW], BF16) for i in range(3)]
                ycs = [SB(ps, "yc%d" % i, [128, GW], BF16) for i in range(2)]
                ycTs = [SB(ps, "ycT%d" % i, [128, 2, 512], BF16, True) for i in range(2)]
                gtile = [0]
                sgu_ent = {}

                def p1_sgu_step(kind, ent):
                    gi, bi_, j, t0_, ntok_ = ent
                    gel, Tgel = gels[gi % 3]
                    vln, Tvln = vlns[gi % 3]
                    yc, Tyc = ycs[gi % 2]
                    ycT, TycT = ycTs[bi_ % 2]
                    if kind == "S1":
                        tm, Ttm = tms[2]
                        for g in range(4):
                            mm(tm[:, g * 64:(g + 1) * 64], wsT[:, g * 128:(g + 1) * 128], vln[:, g * 64:(g + 1) * 64],
                               True, True, [TwsT, Tvln], [Ttm], inc=(g == 3))
                        for g in range(4):
                            stt(yc[:, g * 64:(g + 1) * 64], tm[:, g * 64:(g + 1) * 64], sgub[:, l, g:g + 1],
                                gel[:, g * 64:(g + 1) * 64], ALU.add, ALU.mult, [Ttm, Tsgub, Tgel], [Tyc])
                    else:
                        for c_ in range(2):
                            tr(sgp[:, c_ * 128:(c_ + 1) * 128], yc[:, c_ * 128:(c_ + 1) * 128], identb[:], [Tyc, Tid],
                               [Tsgp], inc=(c_ == 1))
                        act(ycT[:, :, j * 128:(j + 1) * 128], sgp[:, 0:256].rearrange("p (c t) -> p c t", c=2), AF.Copy,
                            [Tsgp], [TycT])
                        if (j + 1) * 128 == ntok_:
                            R.dma("pool", yT[4:6, :, t0_:t0_ + ntok_].rearrange("c p t -> p c t"), ycT[:, :, 0:ntok_],
                                  TycT, reads=[TycT], writes=[Dy])

                p1_LD(0)
                p1_LD(1)
                p1_A1(0)
                p1_A2(0)
                for bi, (t0, ntok, s) in enumerate(blocks):
                    nt = ntok // 128
                    hT, ThT = hTs[bi % 2]
                    cost, Tcos = coss[bi % 3]
                    sint, Tsin = sins[bi % 3]
                    if bi + 2 < nblk:
                        p1_LD(bi + 2)
                    if bi + 1 < nblk:
                        p1_A1(bi + 1)
                    order = [0, 1, 2, 3, 4, 5, 8, 6, 9, 7, 12, 10, 13, 11]
                    for ci, c in enumerate(order):
                        fp, Tfp = fps[ci % 2]
                        for k in range(8):
                            mm(fp[:, 0:ntok], wi[:, k, c * 128:(c + 1) * 128], hT[:, k, 0:ntok], k == 0, k == 7,
                               [Twi, ThT], [Tfp], inc=(k == 7))
                        if c < 2:
                            act(aT[:, c, 0:ntok], fp[:, 0:ntok], AF.Copy, [Tfp], [TaT])
                        elif c < 6:
                            fo, Tfo = fmo[fmi % 3]
                            fmi += 1
                            act(fo[:, 0:ntok], fp[:, 0:ntok], AF.Copy, [Tfp], [Tfo], scale=(0.125 if c < 4 else 1.0))
                            R.dma("pool", fmT[c - 2, :, t0:t0 + ntok], fo[:, 0:ntok], Tfo, reads=[Tfo], writes=[Dfm])
                        elif c in (8, 9, 12, 13):
                            tt("dve", r1[:, 0:ntok], fp[:, 0:ntok], sint[:, 0:ntok], ALU.mult, [Tfp, Tsin], [Tr1])
                        else:
                            tt("dve", r2[:, 0:ntok], fp[:, 0:ntok], cost[:, 0:ntok], ALU.mult, [Tfp, Tcos], [Tr2])
                            fo, Tfo = fmo[fmi % 3]
                            fmi += 1
                            tt("pool", fo[:, 0:ntok], r1[:, 0:ntok], r2[:, 0:ntok], ALU.add, [Tr1, Tr2], [Tfo])
                            dst = {6: 4, 7: 5, 10: 6, 11: 7}[c]
                            R.dma("pool", fmT[dst, :, t0:t0 + ntok], fo[:, 0:ntok], Tfo, reads=[Tfo], writes=[Dfm])
                    for j in range(nt):
                        gi = gtile[0]
                        gtile[0] += 1
                        gel, Tgel = gels[gi % 3]
                        vln, Tvln = vlns[gi % 3]
                        tm, Ttm = tms[0]
                        for c_ in range(2):
                            mm(tm[:, c_ * 256:(c_ + 1) * 256], aT[:, c_, j * 128:(j + 1) * 128], cst[:, :], True, True,
                               [TaT, Tcs], [Ttm])
                        cp("dve", Bsb[:, j, :], tm[:, :], [Ttm], [TBsb])
                        tm, Ttm = tms[1]
                        for k in range(8):
                            mm(tm[:, :], hT[:, k, j * 128:(j + 1) * 128], wi[:, k, NFM * 128:NFM * 128 + 512], k == 0,
                               k == 7, [ThT, Twi], [Ttm], inc=(k == 7))
                        act(vt[:, j, 0:260].rearrange("p (h d) -> p h d", d=65)[:, :, 0:64],
                            tm[:, 0:256].rearrange("p (h d) -> p h d", d=64), AF.Copy, [Ttm], [Tvt])
                        cp("dve", vt[:, j, 260:772].rearrange("p (h d) -> p h d", d=128)[:, :, 0:64],
                           tm[:, 256:512].rearrange("p (h d) -> p h d", d=64), [Ttm], [Tvt])
                        tm, Ttm = tms[0]
                        for k in range(8):
                            mm(tm[:, :], hT[:, k, j * 128:(j + 1) * 128], wi[:, k, NFM * 128 + 512:NFM * 128 + 1024],
                               k == 0, k == 7, [ThT, Twi], [Ttm], inc=(k == 7))
                        act(gel[:], tm[:, :], AF.Gelu_apprx_tanh, [Ttm], [Tgel])
                        R.op("dve", lambda e, gel=gel: e.bn_stats(out=bst[:], in_=gel[:, 256:512]), [Tgel], [Tbst])
                        R.op("dve", lambda e: e.bn_aggr(out=mv[:], in_=bst[:]), [Tbst], [Tmv])
                        ts("dve", vn[:], gel[:, 256:512], mv[:, 0:1], None, ALU.subtract, None, [Tgel, Tmv], [Tvn])
                        ts("dve", mv[:, 1:2], mv[:, 1:2], EPS, None, ALU.add, None, [Tmv], [Tmv])
                        tt("pool", mv[:, 1:2], mv[:, 1:2], nhf[:, 0:1], ALU.pow, [Tmv, Tnhf], [Tmv])
                        stt(vn[:], vn[:], mv[:, 1:2], lng[:], ALU.mult, ALU.mult, [Tvn, Tmv, Tlng], [Tvn])
                        tt("pool", vln[:], vn[:], lnb[:], ALU.add, [Tvn, Tlnb], [Tvln])
                        sgu_ent[gi] = (gi, bi, j, t0, ntok)
                        if gi - 2 >= 0:
                            p1_sgu_step("S1", sgu_ent[gi - 2])
                        if gi - 3 >= 0:
                            p1_sgu_step("S2", sgu_ent[gi - 3])
                    if bi + 1 < nblk:
                        p1_A2(bi + 1)
                    R.dma("pool", Bd[t0:t0 + ntok, :].rearrange("(j p) c -> p j c", p=128), Bsb[:, 0:nt, :], TBsb,
                          reads=[TBsb], writes=[DB])
                    R.dma("pool", vtok[t0:t0 + ntok, :].rearrange("(j p) c -> p j c", p=128),
                          vt[:, 0:nt, :], Tvt, reads=[Tvt], writes=[Dv])
                gl_ = gtile[0] - 1
                p1_sgu_step("S1", sgu_ent[gl_ - 1])
                p1_sgu_step("S2", sgu_ent[gl_ - 2])
                p1_sgu_step("S1", sgu_ent[gl_])
                p1_sgu_step("S2", sgu_ent[gl_ - 1])
                p1_sgu_step("S2", sgu_ent[gl_])
                with nc.Block() as blk:
                    R.end_phase(blk)

            with contextlib.ExitStack() as ps:
                Ball, TBall = SB(ps, "Ball", [128, NTILE, 512], BF16, True)
                d256, Td256 = SB(ps, "d256", [128, 2, 2, 256], BF16, True)
                dts = [[SB(ps, "dft%d_%d" % (kd, i), [128, 8, 512], BF16, True) for i in range(3)] for kd in range(2)]
                yaT = [SB(ps, "yaT%d" % i, [128, 2, 512], BF16, True) for i in range(2)]
                yps = [PS(ps, "yps%d" % i, [128, 512]) for i in range(4)]
                R.dma("sp", Ball[:], Bd.rearrange("(j p) c -> p j c", p=128), TBall, reads=[DB], writes=[TBall])
                R.dma("sp", d256[:].rearrange("p a b c -> p (a b c)"), dft256_d[:, :], Td256, writes=[Td256])
                li_ = 0
                for nb in range(8):
                    ya, Tya = yaT[nb % 2]
                    for qd in range(4):
                        bufs = []
                        for kd in range(2):
                            dt_, Tdt = dts[kd][li_ % 3]
                            R.dma("sp", dt_[:].rearrange("p a b -> p (a b)"),
                                  dft_d[kd, nb, :, qd * 8 * 512:(qd + 1) * 8 * 512], Tdt, writes=[Tdt])
                            bufs.append((dt_, Tdt))
                        li_ += 1
                        for c in range(2):
                            yp, Typ = yps[(nb % 2) * 2 + c]
                            for n8 in range(8):
                                nti = qd * 8 + n8
                                for kd in range(2):
                                    dt_, Tdt = bufs[kd]
                                    lastmm = (qd == 3 and n8 == 7 and kd == 1)
                                    mm(yp[:, :], Ball[:, nti, c * 256 + kd * 128:c * 256 + (kd + 1) * 128],
                                       dt_[:, n8, :], (qd == 0 and n8 == 0 and kd == 0), lastmm, [TBall, Tdt], [Typ],
                                       inc=(lastmm or (n8 == 7 and kd == 1)))
                    for c in range(2):
                        yp, Typ = yps[(nb % 2) * 2 + c]
                        if c == 0:
                            act(ya[:, c, :], yp[:, :], AF.Copy, [Typ], [Tya])
                        else:
                            cp("dve", ya[:, c, :], yp[:, :], [Typ], [Tya])
                    R.dma("pool", yT[0:2, :, nb * 512:(nb + 1) * 512].rearrange("c p t -> p c t"), ya[:], Tya,
                          reads=[Tya], writes=[Dy])
                if not last:
                    ya, Tya = yaT[0]
                    for c in range(2):
                        yp, Typ = yps[c]
                        for n2 in range(2):
                            for kd in range(2):
                                mm(yp[:, 0:256], Ball[:, 32 + n2, c * 256 + kd * 128:c * 256 + (kd + 1) * 128],
                                   d256[:, n2, kd, :], (n2 == 0 and kd == 0), (n2 == 1 and kd == 1), [TBall, Td256],
                                   [Typ], inc=(n2 == 1 and kd == 1))
                        cp("dve", ya[:, c, 0:256], yp[:, 0:256], [Typ], [Tya])
                    R.dma("pool", yT[0:2, :, SEQ:NTOK].rearrange("c p t -> p c t"), ya[:, :, 0:256], Tya, reads=[Tya],
                          writes=[Dy])
                with nc.Block() as blk:
                    R.end_phase(blk)

            with contextlib.ExitStack() as ps:
                qT, TqT = SB(ps, "qT", [128, 2, NTOK], BF16, True)
                kTs = [SB(ps, "kTp%d" % h, [128, NTOK], BF16, True) for h in range(4)]
                vna, Tvna = SB(ps, "vna", [128, NTILE, 260], BF16, True)
                msk, Tmsk = SB(ps, "msk", [128, 12800], BF16, True)
                Es = [SB(ps, "E%d" % i, [128, 7, 128], BF16) for i in range(3)]
                rc, Trc = SB(ps, "rc", [128, 4], F32)
                ybs = [SB(ps, "yb%d" % i, [128, GW], BF16) for i in range(2)]
                ybTs = [SB(ps, "ybT%d" % i, [128, 2, 512], BF16, True) for i in range(2)]
                sABs = [PS(ps, "sAB%d" % i, [128, 1024]) for i in range(2)]
                ops_ = [PS(ps, "ops%d" % i, [128, 4, 65]) for i in range(2)]
                trps = [PS(ps, "trp%d" % i, [128, 1024], BF16) for i in range(2)]
                R.dma("sp", qT[:], fmT[0:2].rearrange("c p t -> p c t"), TqT, reads=[Dfm], writes=[TqT])
                for h in range(4):
                    r0 = (h % 2) * 64
                    memset("dve" if h % 2 == 0 else "pool", kTs[h][0][:], 0.0, [kTs[h][1]])
                    R.dma("sp", kTs[h][0][r0:r0 + 64, :], fmT[2 + h // 2, r0:r0 + 64, :], kTs[h][1], reads=[Dfm],
                          writes=[kTs[h][1]])
                R.dma("sp", vna[:], vtok[:, 0:260].rearrange("(j p) c -> p j c", p=128), Tvna, reads=[Dv],
                      writes=[Tvna])
                R.dma("sp", msk[:], maskT_b[l], Tmsk, reads=[Dw], writes=[Tmsk])
                ntq = 32 if last else 34
                items = []
                for t in range(ntq):
                    if t < 32:
                        kt0 = min(max(t - 2, 0), 27)
                        kts = [kt0 + i for i in range(5)] + [32, 33]
                        pat = {0: 0, 1: 1, 30: 3, 31: 4}.get(t, 2)
                    else:
                        kts, pat = [32, 33], None
                    for h in range(4):
                        items.append((t, h, kts, pat))

                def na_S(i):
                    t, h, kts, pat = items[i]
                    cq, b0 = h // 2, (h % 2) * 64
                    sAB, TsAB = sABs[i % 2]
                    for idx, kt in enumerate(kts):
                        dsl = sAB[:, idx * 128:(idx + 1) * 128]
                        masked = (pat is not None and idx < 5)
                        lastm = (idx == len(kts) - 1)
                        mm(dsl, kTs[h][0][:, kt * 128:(kt + 1) * 128], qT[:, cq, t * 128:(t + 1) * 128],
                           True, not masked, [kTs[h][1], TqT], [TsAB], inc=(lastm and not masked))
                        if masked:
                            m0 = ((pat * 4 + h) * 5 + idx) * 128
                            mm(dsl, identb[:], msk[:, m0:m0 + 128], False, True, [Tid, Tmsk], [TsAB], inc=lastm)

                def na_EX(i):
                    t, h, kts, pat = items[i]
                    nk = len(kts)
                    sAB, TsAB = sABs[i % 2]
                    E, TE = Es[i % 3]
                    act(E[:, 0:nk, :], sAB[:, 0:nk * 128].rearrange("p (a q) -> p a q", q=128), AF.Exp, [TsAB], [TE])

                def na_PV(i):
                    t, h, kts, pat = items[i]
                    nk = len(kts)
                    E, TE = Es[i % 3]
                    op_, Top = ops_[t % 2]
                    for idx, kt in enumerate(kts):
                        mm(op_[:, h, :], E[:, idx, :], vna[:, kt, h * 65:(h + 1) * 65], idx == 0, idx == nk - 1,
                           [TE, Tvna], [Top], inc=(idx == nk - 1))

                def na_tail1(t):
                    op_, Top = ops_[t % 2]
                    yb, Tyb = ybs[t % 2]
                    recip(rc[:], op_[:, :, 64], [Top], [Trc])
                    for h in range(4):
                        ts("dve", yb[:, h * 64:(h + 1) * 64], op_[:, h, 0:64], rc[:, h:h + 1], None, ALU.mult, None,
                           [Top, Trc], [Tyb])

                def na_tail2(t):
                    yb, Tyb = ybs[t % 2]
                    trp, Ttrp = trps[t % 2]
                    for c in range(2):
                        tr(trp[:, c * 128:(c + 1) * 128], yb[:, c * 128:(c + 1) * 128], identb[:], [Tyb, Tid], [Ttrp],
                           inc=(c == 1))

                def na_tail3(t):
                    trp, Ttrp = trps[t % 2]
                    ybT, TybT = ybTs[(t // 4) % 2]
                    j4 = t % 4
                    act(ybT[:, :, j4 * 128:(j4 + 1) * 128], trp[:, 0:256].rearrange("p (c t) -> p c t", c=2), AF.Copy,
                        [Ttrp], [TybT])
                    if j4 == 3 or t == ntq - 1:
                        tb = (t // 4) * 512
                        n_ = (j4 + 1) * 128
                        R.dma("pool", yT[2:4, :, tb:tb + n_].rearrange("c p t -> p c t"), ybT[:, :, 0:n_], TybT,
                              reads=[TybT], writes=[Dy])

                pend = []
                n_it = len(items)
                na_S(0)
                na_S(1)
                for i in range(n_it):
                    na_EX(i)
                    if i + 2 < n_it:
                        na_S(i + 2)
                    na_PV(i)
                    t, h = items[i][0], items[i][1]
                    if h == 3:
                        na_tail1(t)
                        pend.append((i + 1, na_tail2, t))
                        pend.append((i + 2, na_tail3, t))
                    keep = []
                    for (due, fn, arg) in pend:
                        if due <= i:
                            fn(arg)
                        else:
                            keep.append((due, fn, arg))
                    pend = keep
                for (due, fn, arg) in pend:
                    fn(arg)
                with nc.Block() as blk:
                    R.end_phase(blk)

            with contextlib.ExitStack() as ps:
                Qh, TQh = SB(ps, "Qc", [128, 2, NTOK], BF16, True)
                Kps = [SB(ps, "Kp%d" % v, [128, NTOK], BF16, True) for v in range(8)]
                vdf, Tvdf = SB(ps, "vdf", [128, NTILE, 512], BF16, True)
                Eb = [SB(ps, "Eb%d" % i, [128, 2, 512], BF16) for i in range(3)]
                rr = [SB(ps, "rr%d" % i, [64, 512], F32) for i in range(2)]
                dd, Tdd = SB(ps, "dd", [64, 512], F32)
                d2, Td2 = SB(ps, "d2", [64, 512], F32)
                dsq, Tdsq = SB(ps, "dsq", [64, 512], F32)
                rs_, Trs = SB(ps, "rs_", [64, 512], F32)
                ydT, TydT = SB(ps, "ydT", [64, 512], BF16, True)
                sps = [PS(ps, "sps%d" % i, [128, 2, 512]) for i in range(2)]
                ops2 = [PS(ps, "ops2_%d" % i, [128, 512]) for i in range(4)]

                def acc_of(bn_, m_):
                    return ops2[2 * (bn_ % 2) + m_]
                dhi, Tdhi = SB(ps, "dhi", [128, 512], BF16)
                dlo, Tdlo = SB(ps, "dlo", [128, 512], BF16)
                memset("pool", dhi[:], 0.0, [Tdhi])
                memset("pool", dlo[:], 0.0, [Tdlo])
                R.dma("sp", Qh[:], fmT[4:6].rearrange("c p t -> p c t"), TQh, reads=[Dfm], writes=[TQh])
                for v in range(8):
                    ch, r0 = v // 4, (v % 4) * 32
                    memset("dve" if v % 2 == 0 else "pool", Kps[v][0][:], 0.0, [Kps[v][1]])
                    R.dma("sp", Kps[v][0][r0:r0 + 32, :], fmT[6 + ch, r0:r0 + 32, :], Kps[v][1], reads=[Dfm],
                          writes=[Kps[v][1]])
                R.dma("sp", vdf[:], vtok[:, 260:772].rearrange("(j p) c -> p j c", p=128), Tvdf, reads=[Dv],
                      writes=[Tvdf])
                if l + 1 < nlayers:
                    cast_weights(l + 1)
                qblocks = blocks[:8] if last else blocks
                sc_ = 32.0 ** -0.5
                steps = []
                bnum = 0
                for h in range(4):
                    for (q0, nq, s) in qblocks:
                        kts = list(range(NTILE)) if s == 0 else [32, 33]
                        for ki, kt in enumerate(kts):
                            steps.append((h, q0, nq, ki, kt, len(kts), bnum))
                        bnum += 1

                def df_S(i):
                    h, q0, nq, ki, kt, nk, bn = steps[i]
                    sp_, Tsp = sps[i % 2]
                    for m in range(2):
                        mm(sp_[:, m, 0:nq], Kps[h * 2 + m][0][:, kt * 128:(kt + 1) * 128], Qh[:, h // 2, q0:q0 + nq], True,
                           True, [Kps[h * 2 + m][1], TQh], [Tsp], inc=(m == 1))

                def df_EX(i):
                    h, q0, nq, ki, kt, nk, bn = steps[i]
                    sp_, Tsp = sps[i % 2]
                    E, TE = Eb[i % 3]
                    act(E[:, :, 0:nq], sp_[:, :, 0:nq], AF.Exp, [Tsp], [TE], scale=sc_)

                def df_PV(i):
                    h, q0, nq, ki, kt, nk, bn = steps[i]
                    E, TE = Eb[i % 3]
                    for m in range(2):
                        o2, To2 = acc_of(bn, m)
                        mm(o2[:, 0:nq], vdf[:, kt, h * 128:(h + 1) * 128], E[:, m, 0:nq], ki == 0, ki == nk - 1,
                           [Tvdf, TE], [To2], inc=True)

                def df_post1(arg):
                    h, q0, nq, bn = arg
                    a0, Ta0 = acc_of(bn, 0)
                    a1, Ta1 = acc_of(bn, 1)
                    recip(rr[0][0][:, 0:nq], a0[64:128, 0:nq], [Ta0], [rr[0][1]])
                    tt("dve", dd[:, 0:nq], a0[0:64, 0:nq], rr[0][0][:, 0:nq], ALU.mult, [Ta0, rr[0][1]], [Tdd])
                    recip(rr[1][0][:, 0:nq], a1[64:128, 0:nq], [Ta1], [rr[1][1]])
                    tt("dve", d2[:, 0:nq], a1[0:64, 0:nq], rr[1][0][:, 0:nq], ALU.mult, [Ta1, rr[1][1]], [Td2])
                    stt(dd[:, 0:nq], d2[:, 0:nq], neglam[:, l:l + 1], dd[:, 0:nq], ALU.mult, ALU.add, [Td2, Tnl, Tdd],
                        [Tdd])
                    tt("pool", dsq[:, 0:nq], dd[:, 0:nq], dd[:, 0:nq], ALU.mult, [Tdd], [Tdsq])
                    cp("dve", dhi[0:64, 0:nq], dsq[:, 0:nq], [Tdsq], [Tdhi])
                    tt("pool", dlo[0:64, 0:nq], dsq[:, 0:nq], dhi[0:64, 0:nq], ALU.subtract, [Tdsq, Tdhi], [Tdlo])

                def df_post2(arg):
                    h, q0, nq, bn = arg
                    bp, Tbp = acc_of(bn, 0)
                    mm(bp[:, 0:nq], onesb[:, :], dhi[:, 0:nq], True, False, [Tonesb, Tdhi], [Tbp], inc=False)
                    mm(bp[:, 0:nq], onesb[:, :], dlo[:, 0:nq], False, True, [Tonesb, Tdlo], [Tbp])

                def df_post3(arg):
                    h, q0, nq, bn = arg
                    bp, Tbp = acc_of(bn, 0)
                    act(rs_[:, 0:nq], bp[0:64, 0:nq], AF.Ln, [Tbp], [Trs], scale=1.0 / 64, bias=EPS)
                    act(rs_[:, 0:nq], rs_[:, 0:nq], AF.Exp, [Trs], [Trs], scale=-0.5)
                    stt(ydT[:, 0:nq], dd[:, 0:nq], gsub[:, l:l + 1], rs_[:, 0:nq], ALU.mult, ALU.mult,
                        [Tdd, Tgs, Trs], [TydT])
                    R.dma("pool", yT[6 + h, 0:64, q0:q0 + nq], ydT[:, 0:nq], TydT, reads=[TydT], writes=[Dy])

                pend = []
                n_it = len(steps)
                df_S(0)
                df_S(1)
                for i in range(n_it):
                    df_EX(i)
                    if i + 2 < n_it:
                        df_S(i + 2)
                    df_PV(i)
                    h, q0, nq, ki, kt, nk, bn = steps[i]
                    if ki == nk - 1:
                        for (due, fn, arg) in pend:
                            fn(arg)
                        pend = []
                        df_post1((h, q0, nq, bn))
                        pend.append((i + 12, df_post2, (h, q0, nq, bn)))
                        pend.append((i + 16, df_post3, (h, q0, nq, bn)))
                    keep = []
                    for (due, fn, arg) in pend:
                        if due <= i:
                            fn(arg)
                        else:
                            keep.append((due, fn, arg))
                    pend = keep
                for (due, fn, arg) in pend:
                    fn(arg)
                with nc.Block() as blk:
                    R.end_phase(blk)

            with contextlib.ExitStack() as ps:
                wo, Two = SB(ps, "wo", [128, 10, D], BF16, True)
                gb = [[SB(ps, "gb%d_%d" % (s, g), [128, D], F32, True) for g in range(2)] for s in range(2)]
                fgb, Tfgb = SB(ps, "fgb", [128, D], F32, True)
                xts = [SB(ps, "x3_%d" % i, [128, 4, D], F32, True) for i in range(2)]
                yTb, TyTb = SB(ps, "yTb", [128, 10, 512], BF16, True)
                junk, Tjunk = SB(ps, "junk3", [128, D], BF16)
                ss, Tss = SB(ps, "ss3", [128, 4], F32)
                rstd, Trstd = SB(ps, "rstd3", [128, 4], F32)
                xn, Txn = SB(ps, "xn3", [128, 4, D], BF16)
                hT, ThT = SB(ps, "h2T", [128, 8, 512], BF16)
                aT, TaT = SB(ps, "aT3", [128, 32, 512], BF16)
                tmpv = [SB(ps, "tmp%d" % i, [128, 512], F32) for i in range(2)]
                sqv = [SB(ps, "sq%d" % i, [128, 512], F32) for i in range(2)]
                w1b = [SB(ps, "w1b%d" % i, [128, 8, 512], BF16, True) for i in range(2)]
                w2b = [SB(ps, "w2b%d" % i, [128, 4, 512], BF16, True) for i in range(3)]
                acc = [PS(ps, "acc%d" % i, [128, 512]) for i in range(4)]
                f1p = [PS(ps, "f1p%d" % i, [128, 512]) for i in range(2)]
                tps = [PS(ps, "tp3_%d" % i, [128, 1024], BF16) for i in range(2)]
                R.dma("sp", wo[:], w_out_b[l].rearrange("(c p) n -> p c n", p=128), Two, reads=[Dw], writes=[Two])
                for s in range(2):
                    for g in range(2):
                        R.dma("sp", gb[s][g][0][:], bcast(gates[l, s:s + 1, g * D:(g + 1) * D], 128), gb[s][g][1],
                              reads=[Dg], writes=[gb[s][g][1]])
                if last:
                    R.dma("sp", fgb[:], bcast(final_g[0:1, :], 128), Tfgb, writes=[Tfgb])
                w1v = w_ff1_b[l].rearrange("(k p) f -> p k f", p=128)
                w2v = w_ff2_b[l].rearrange("(c p) d -> p c d", p=128)
                i1 = i2 = 0
                tcount = 0
                p3blocks = blocks[:8] if last else blocks

                def p3_load(bi_, q):
                    t0_, ntok_, s_ = p3blocks[bi_]
                    xt_, Txt_ = xts[bi_ % 2]
                    R.dma(q, xt_[:, 0:ntok_ // 128, :], xsrc[t0_:t0_ + ntok_, :].rearrange("(j p) d -> p j d", p=128),
                          Txt_, reads=[Dxs], writes=[Txt_])
                    R.dma(q, yTb[:, :, 0:ntok_], yT[:, :, t0_:t0_ + ntok_].rearrange("c p t -> p c t"), TyTb,
                          reads=[Dy], writes=[TyTb])

                p3_load(0, "sp")
                for bi, (t0, ntok, s) in enumerate(p3blocks):
                    nt = ntok // 128
                    xt, Txt = xts[bi % 2]
                    g1b, Tg1b = gb[s][0]
                    g2b, Tg2b = gb[s][1]
                    for j in range(nt):
                        for n in range(2):
                            ap_, Tap = acc[(j * 2 + n) % 4]
                            for c in range(10):
                                kc = 128 if c < 6 else 64
                                mm(ap_[:, :], yTb[0:kc, c, j * 128:(j + 1) * 128], wo[0:kc, c, n * 512:(n + 1) * 512],
                                   c == 0, c == 9, [TyTb, Two], [Tap], inc=(c == 9))
                            tv, Ttv = tmpv[tcount % 2]
                            tcount += 1
                            tt("dve", tv[:], ap_[:, :], g1b[:, n * 512:(n + 1) * 512], ALU.mult, [Tap, Tg1b], [Ttv])
                            tt("pool", xt[:, j, n * 512:(n + 1) * 512], xt[:, j, n * 512:(n + 1) * 512], tv[:], ALU.add,
                               [Txt, Ttv], [Txt])
                    if bi + 1 < len(p3blocks):
                        p3_load(bi + 1, "pool")
                    norm_to_hT((junk, Tjunk, ss, Tss, rstd, Trstd, xn, Txn), xt, Txt, nt, l, 2, 3, s, tps, hT, ThT, "p3")
                    for g8 in range(8):
                        w1t, Tw1 = w1b[i1 % 2]
                        i1 += 1
                        R.dma("sp", w1t[:], w1v[:, :, g8 * 512:(g8 + 1) * 512], Tw1, reads=[Dw], writes=[Tw1])
                        for c4 in range(4):
                            c = g8 * 4 + c4
                            fp, Tfp = f1p[c % 2]
                            for k in range(8):
                                mm(fp[:, 0:ntok], w1t[:, k, c4 * 128:(c4 + 1) * 128], hT[:, k, 0:ntok], k == 0, k == 7,
                                   [Tw1, ThT], [Tfp], inc=(k == 7))
                            sq, Tsq = sqv[c % 2]
                            act(sq[:, 0:ntok], fp[:, 0:ntok], AF.Square, [Tfp], [Tsq])
                            stt(aT[:, c, 0:ntok], fp[:, 0:ntok], 0.0, sq[:, 0:ntok], ALU.is_gt, ALU.mult, [Tfp, Tsq], [TaT])
                    for n in range(2):
                        for g8 in range(8):
                            w2t, Tw2 = w2b[i2 % 3]
                            i2 += 1
                            R.dma("sp", w2t[:], w2v[:, g8 * 4:(g8 + 1) * 4, n * 512:(n + 1) * 512], Tw2, reads=[Dw],
                                  writes=[Tw2])
                            for j in range(nt):
                                ap_, Tap = acc[j]
                                for c4 in range(4):
                                    c = g8 * 4 + c4
                                    mm(ap_[:, :], aT[:, c, j * 128:(j + 1) * 128], w2t[:, c4, :], (g8 == 0 and c4 == 0),
                                       (g8 == 7 and c4 == 3), [TaT, Tw2], [Tap], inc=(c4 == 3))
                        for j in range(nt):
                            ap_, Tap = acc[j]
                            tv, Ttv = tmpv[tcount % 2]
                            tcount += 1
                            tt("dve", tv[:], ap_[:, :], g2b[:, n * 512:(n + 1) * 512], ALU.mult, [Tap, Tg2b], [Ttv])
                            tt("pool", xt[:, j, n * 512:(n + 1) * 512], xt[:, j, n * 512:(n + 1) * 512], tv[:], ALU.add,
                               [Txt, Ttv], [Txt])
                    if not last:
                        R.dma("pool", xres[t0:t0 + ntok, :].rearrange("(j p) d -> p j d", p=128), xt[:, 0:nt, :], Txt,
                              reads=[Txt], writes=[Dx])
                    else:
                        for j in range(nt):
                            act(junk[:], xt[:, j, :], AF.Square, [Txt], [Tjunk, Tss], accum_out=ss[:, j:j + 1])
                        act(rstd[:, 0:nt], ss[:, 0:nt], AF.Sqrt, [Tss], [Trstd], scale=1.0 / D, bias=EPS)
                        recip(rstd[:, 0:nt], rstd[:, 0:nt], [Trstd], [Trstd])
                        for j in range(nt):
                            stt(xt[:, j, :], xt[:, j, :], rstd[:, j:j + 1], fgb[:], ALU.mult, ALU.mult,
                                [Txt, Trstd, Tfgb], [Txt])
                        R.dma("pool", out_d[t0:t0 + ntok, :].rearrange("(j p) d -> p j d", p=128), xt[:, 0:nt, :], Txt,
                              reads=[Txt], writes=[Dout])
                with nc.Block() as blk:
                    R.end_phase(blk)
    return nc


def _consts():
    bf = ml_dtypes.bfloat16
    c = {}
    c["identb"] = np.eye(128, dtype=np.float32).astype(bf)
    k = np.arange(64)
    th = 2 * np.pi * np.outer(k, k) / 64.0
    cc, sc = np.cos(th) / 8.0, np.sin(th) / 8.0
    z = np.zeros((64, 64))
    cs = np.concatenate([np.block([[cc, z], [z, cc]]), np.block([[sc, z], [z, sc]])], axis=1)
    c["cs_tab"] = cs.astype(np.float32).astype(bf)
    n = np.arange(SEQ)
    dft = np.empty((2, 8, 128, 32, 512), dtype=bf)
    for nb in range(8):
        npr = nb * 512 + np.arange(512)
        m = (np.outer(n, npr) % SEQ).astype(np.float64)
        ang = 2 * np.pi * m / SEQ
        cm = (np.cos(ang) / 64.0).astype(np.float32).reshape(32, 128, 512).transpose(1, 0, 2)
        sm = (-np.sin(ang) / 64.0).astype(np.float32).reshape(32, 128, 512).transpose(1, 0, 2)
        dft[0, nb] = cm.astype(bf)
        dft[1, nb] = sm.astype(bf)
    c["dft"] = dft.reshape(2, 8, 128, 32 * 512)
    n2 = np.arange(CTX)
    ang = 2 * np.pi * (np.outer(n2, n2) % CTX) / CTX
    d256 = np.stack([np.cos(ang) / 16.0, -np.sin(ang) / 16.0], axis=0)
    d256 = d256.reshape(2, 2, 128, 256).transpose(2, 1, 0, 3)
    c["dft256"] = np.ascontiguousarray(d256).astype(np.float32).astype(bf).reshape(128, 1024)
    t = np.arange(SEQ)
    rows, cols = (t // 64).astype(np.float32), (t % 64).astype(np.float32)
    inv = (10000.0 ** (-np.arange(8, dtype=np.float32) / 8)).astype(np.float32)
    cos_t = np.ones((128, NTOK), dtype=np.float32)
    sin_t = np.zeros((128, NTOK), dtype=np.float32)
    for p in range(128):
        i = p % 32
        a, f, hh = i // 16, i % 8, (i // 8) % 2
        ang = ((rows if a == 0 else cols) * inv[f]).astype(np.float32)
        cos_t[p, :SEQ] = np.cos(ang)
        sin_t[p, :SEQ] = np.sin(ang) * (-1.0 if hh == 0 else 1.0)
    c["cos_tab"], c["sin_t"] = cos_t, sin_t
    sel = np.zeros((65, 64), dtype=np.float32)
    sel[64, :] = 1.0
    c["sel65"] = sel
    c["ones64"] = np.ones((64, 64), dtype=np.float32)
    ob = np.zeros((128, 128), dtype=np.float32)
    ob[0:64, :] = 1.0
    c["onesb"] = ob.astype(bf)
    li = np.array([0.8 - 0.6 * math.exp(-0.3 * l) for l in range(L)], dtype=np.float32)
    c["laminit"] = np.tile(li[None, :], (64, 1)).astype(np.float32)
    c["omli"] = np.tile((1.0 - li)[None, :], (64, 1)).astype(np.float32)
    c["ident2"] = np.eye(2, dtype=np.float32)
    return c


def _mask_index():
    treps = [0, 1, 2, 30, 31]
    p = np.arange(128)[:, None]
    q = np.arange(128)[None, :]
    idx = np.zeros((128, 5, 5, 128), dtype=np.int64)
    val = np.zeros((128, 5, 5, 128), dtype=bool)
    for pi, t in enumerate(treps):
        kt0 = min(max(t - 2, 0), 27)
        for j in range(5):
            kr = 2 * (kt0 + j) + p // 64
            kc = p % 64
            r = 2 * t + q // 64
            c = q % 64
            rs = np.clip(r - 4, 0, 56)
            cs = np.clip(c - 8, 0, 48)
            ok = (kr >= rs) & (kr < rs + 8) & (kc >= cs) & (kc < cs + 16)
            dr = kr - r + 7
            dc = np.clip(kc - c, -15, 15) + 15
            idx[:, pi, j, :] = np.where(ok, dr * 31 + dc, 0)
            val[:, pi, j, :] = ok
    return idx, val


_CACHE = {}


def _prep_shared(inp):
    f = np.float32
    sh = {}
    sh["ada_w"] = np.ascontiguousarray(inp["ada_w"], dtype=f)
    sh["ada_b"] = np.ascontiguousarray(inp["ada_b"], dtype=f)
    sh["n1gc"] = np.ascontiguousarray(np.asarray(inp["norm1_g"], dtype=f).reshape(L, 8, 128).transpose(2, 0, 1))
    sh["n2gc"] = np.ascontiguousarray(np.asarray(inp["norm2_g"], dtype=f).reshape(L, 8, 128).transpose(2, 0, 1))
    w_in = np.asarray(inp["w_in"], dtype=f)
    cd = 6 * GW
    j = np.arange(256)
    cols = np.concatenate([np.arange(0, 256), np.arange(256, 512), np.arange(512, 768),
                           cd + j, cd + (j ^ 8), cd + 256 + j, cd + 256 + (j ^ 8),
                           np.arange(768, 1024), np.arange(cd + 512, cd + 768), np.arange(1024, 1536)])
    assert cols.shape[0] == WIN
    sh["w_in_r"] = np.ascontiguousarray(w_in[:, :, cols])
    w_out = np.asarray(inp["w_out"], dtype=f)
    wo = np.zeros((L, 10, 128, D), dtype=f)
    wo[:, 0:6] = w_out[:, 0:768].reshape(L, 6, 128, D)
    wo[:, 6:10, 0:64] = w_out[:, 768:1024].reshape(L, 4, 64, D)
    sh["w_out_r"] = wo.reshape(L, 1280, D)
    sh["w_ff1"] = np.ascontiguousarray(inp["w_ff1"], dtype=f)
    sh["w_ff2"] = np.ascontiguousarray(inp["w_ff2"], dtype=f)
    idx, val = _mask_index()
    rpb = np.asarray(inp["na_rpb"], dtype=f).reshape(L, 4, 15 * 31)
    mk = np.empty((L, 128, 5, 4, 5, 128), dtype=f)
    for l in range(L):
        for h in range(4):
            mk[l, :, :, h] = np.where(val, rpb[l, h][idx], f(NEG))
    sh["maskT"] = mk.reshape(L, 128, 12800)
    sw = np.asarray(inp["sgu_w"], dtype=f)
    sh["sgu_wT"] = np.ascontiguousarray(sw.transpose(0, 3, 1, 2)).reshape(L, 128, 512)
    sh["sgu_bc"] = np.ascontiguousarray(np.asarray(inp["sgu_b"], dtype=f).transpose(2, 0, 1))
    sh["sgu_lng"] = np.ascontiguousarray(inp["sgu_ln_g"], dtype=f)
    sh["sgu_lnb"] = np.ascontiguousarray(inp["sgu_ln_b"], dtype=f)
    for i, nm in enumerate(("diff_lq1", "diff_lk1", "diff_lq2", "diff_lk2")):
        sh["diff_l%d" % i] = np.ascontiguousarray(inp[nm], dtype=f)
    sh["subg_c"] = np.ascontiguousarray(np.asarray(inp["diff_subln_g"], dtype=f).T)
    sh["final_g"] = np.asarray(inp["final_g"], dtype=f).reshape(1, D)
    return sh


def make_in_maps(inp):
    if "consts" not in _CACHE:
        c = _consts()
        c["sin_tab"] = c.pop("sin_t")
        _CACHE["consts"] = c
    sh = _prep_shared(inp)
    sh.update(_CACHE["consts"])
    x = np.asarray(inp["x"], dtype=np.float32)
    ctx = np.asarray(inp["ctx"], dtype=np.float32)
    c = np.asarray(inp["c"], dtype=np.float32)
    c_ctx = np.asarray(inp["c_ctx"], dtype=np.float32)
    maps = []
    for b in range(8):
        m = dict(sh)
        m["xin"] = np.concatenate([x[b], ctx[b]], axis=0)
        cc = np.stack([c[b], c_ctx], axis=-1).reshape(8, 128, 2).transpose(1, 0, 2)
        m["cc"] = np.ascontiguousarray(cc)
        maps.append(m)
    return maps


def kernel(**inputs):
    if "nc" not in _CACHE:
        _CACHE["nc"] = build()
    nc = _CACHE["nc"]
    maps = make_in_maps(inputs)
    res = run_bass_kernel_spmd(nc, maps, core_ids=list(range(8)))
    return np.stack([np.asarray(r["out"], dtype=np.float32) for r in res.results], axis=0)
```

```python
import contextlib
import math
import numpy as np
import ml_dtypes
import concourse.bass as bass
import concourse.mybir as mybir
from concourse.bass_utils import run_bass_kernel_spmd

F32 = mybir.dt.float32
BF16 = mybir.dt.bfloat16
AF = mybir.ActivationFunctionType
ALU = mybir.AluOpType
AX = mybir.AxisListType

D = 1024
SEQ = 4096
CTX = 256
NTOK = SEQ + CTX
NTILE = NTOK // 128
L = 4
GW = 256
DFF = 4096
DIN = 2304
EPS = 1e-6
NFM = 14
WIN = NFM * 128 + 1024
NEG = -30000.0

ENGS = ("pe", "act", "dve", "pool", "sp")


class Tk:
    def __init__(self, name):
        self.name = name
        self.w = {}
        self.r = {}
        self.slots = {}
        self.dram = False


class Dk(Tk):
    def __init__(self, name):
        super().__init__(name)
        self.dram = True


class Slot:
    def __init__(self, sem):
        self.sem = sem
        self.cnt = 0


class Rec:
    def __init__(self, nc, es, nslots=44):
        self.nc = nc
        self.sem = {e: es.enter_context(nc.semaphore("sem_" + e)) for e in ENGS}
        self.n = {e: 0 for e in ENGS}
        self.seen = {e: {} for e in ENGS}
        self.prog = {e: [] for e in ENGS}
        self.pools = {q: [Slot(es.enter_context(nc.semaphore("d%s%d" % (q, i)))) for i in range(nslots)]
                      for q in ("sp", "pool")}
        self.base = {"sp": 0, "pool": 0}
        self.nxt = {"sp": 0, "pool": 0}

    def tile(self, name, dma=False):
        return Tk(name)

    def _slot(self, t, q):
        sl = t.slots.get(q)
        if sl is None:
            assert self.nxt[q] < len(self.pools[q]), "out of dma semaphores"
            sl = self.pools[q][self.nxt[q]]
            self.nxt[q] += 1
            t.slots[q] = sl
        return sl

    def freeze_global(self):
        self.base = dict(self.nxt)

    def _deps(self, reads, writes, own=None, own_t=None):
        deps = {}

        def add(d, skip=None):
            for k, (sem, v) in d.items():
                if skip is not None and k == skip:
                    continue
                if deps.get(k, (None, 0))[1] < v:
                    deps[k] = (sem, v)
        for t in reads:
            add(t.w)
        for t in writes:
            if t.dram:
                add(t.r)
            else:
                add(t.w, skip=(own if (own_t is t) else None))
                add(t.r)
        return deps

    def _filter(self, eng, deps):
        seen = self.seen[eng]
        out = []
        for k, (sem, v) in deps.items():
            if eng == "pe" and k == self.sem["pe"].num:
                continue
            if seen.get(k, 0) >= v:
                continue
            seen[k] = v
            out.append((sem, v))
        return out

    def _mark(self, ev, reads, writes):
        k = ev[0].num
        for t in reads:
            t.r[k] = ev
        for t in writes:
            if t.dram:
                t.w[k] = ev
            else:
                t.w = {k: ev}
                t.r = {}

    def op(self, eng, fn, reads=(), writes=(), inc=True):
        waits = self._filter(eng, self._deps(reads, writes))
        ev = (self.sem[eng], self.n[eng] + 1)
        if inc:
            self.n[eng] += 1
        self.prog[eng].append((waits, fn, inc, None))
        self._mark(ev, reads, writes)

    def dma(self, q, out, in_, st, reads=(), writes=()):
        sl = self._slot(st, q)
        waits = self._filter(q, self._deps(reads, writes, own=sl.sem.num, own_t=st))
        sl.cnt += 16
        ev = (sl.sem, sl.cnt)
        self.prog[q].append((waits, (lambda e: e.dma_start(out=out, in_=in_)), False, sl.sem))
        self._mark(ev, reads, writes)

    def end_phase(self, blk):
        deps = {s.sem.num: (s.sem, s.cnt) for q in self.pools for s in self.pools[q] if s.cnt > 0}
        self.prog["sp"].append((self._filter("sp", deps), None, False, None))
        names = dict(pe="tensor", act="scalar", dve="vector", pool="gpsimd", sp="sync")
        for e in ENGS:
            prog = self.prog[e]
            sem_e = self.sem[e]

            def body(eh, prog=prog, sem_e=sem_e):
                for waits, fn, inc, dsem in prog:
                    for sem, v in waits:
                        eh.wait_ge(sem, v)
                    if fn is None:
                        continue
                    ins = fn(eh)
                    if dsem is not None:
                        ins.then_inc(dsem, 16)
                    elif inc:
                        ins.then_inc(sem_e, 1)
            getattr(blk, names[e])(body)
        self.prog = {e: [] for e in ENGS}
        self.nxt = dict(self.base)


def build(nlayers=L, dbg=()):
    nc = bass.Bass("TRN2", target_bir_lowering=False)

    def din(name, shape, dt=F32):
        return nc.dram_tensor(name, list(shape), dt, kind="ExternalInput").ap()

    def dscr(name, shape, dt):
        kind = "ExternalOutput" if name in dbg else "Internal"
        return nc.dram_tensor(name, list(shape), dt, kind=kind).ap()

    xin = din("xin", [NTOK, D])
    cc = din("cc", [128, 8, 2])
    ada_w = din("ada_w", [L, D, 6 * D])
    ada_b = din("ada_b", [L, 6 * D])
    ident2_d = din("ident2", [2, 2])
    n1gc = din("n1gc", [128, L, 8])
    n2gc = din("n2gc", [128, L, 8])
    w_in_r = din("w_in_r", [L, D, WIN])
    w_out_r = din("w_out_r", [L, 10 * 128, D])
    w_ff1 = din("w_ff1", [L, D, DFF])
    w_ff2 = din("w_ff2", [L, DFF, D])
    maskT = din("maskT", [L, 128, 12800])
    sgu_wT = din("sgu_wT", [L, 128, 512])
    sgu_bc = din("sgu_bc", [128, L, 4])
    sgu_lng = din("sgu_lng", [L, GW])
    sgu_lnb = din("sgu_lnb", [L, GW])
    dl = [din("diff_l%d" % i, [L, 32]) for i in range(4)]
    subg_c = din("subg_c", [64, L])
    final_g = din("final_g", [1, D])
    identb_d = din("identb", [128, 128], BF16)
    cs_d = din("cs_tab", [128, 256], BF16)
    dft_d = din("dft", [2, 8, 128, 32 * 512], BF16)
    dft256_d = din("dft256", [128, 2 * 2 * 256], BF16)
    cos_d = din("cos_tab", [128, NTOK])
    sin_d = din("sin_tab", [128, NTOK])
    sel_d = din("sel65", [65, 64])
    ones64_d = din("ones64", [64, 64])
    onesb_d = din("onesb", [128, 128], BF16)
    laminit_d = din("laminit", [64, L])
    omli_d = din("omli", [64, L])
    out_d = nc.dram_tensor("out", [SEQ, D], F32, kind="ExternalOutput").ap()

    xres = dscr("xres", [NTOK, D], F32)
    Bd = dscr("Bd", [NTOK, 512], BF16)
    fmT = dscr("fmT", [8, 128, NTOK], BF16)
    vtok = dscr("vtok", [NTOK, 772], BF16)
    yT = dscr("yT", [10, 128, NTOK], BF16)
    gates = dscr("gates", [L, 2, 2 * D], F32)
    w_in_b = dscr("w_in_b", [L, D, WIN], BF16)
    w_out_b = dscr("w_out_b", [L, 10 * 128, D], BF16)
    w_ff1_b = dscr("w_ff1_b", [L, D, DFF], BF16)
    w_ff2_b = dscr("w_ff2_b", [L, DFF, D], BF16)
    maskT_b = dscr("maskT_b", [L, 128, 12800], BF16)
    sgu_wT_b = dscr("sgu_wT_b", [L, 128, 512], BF16)

    Dx, DB, Dfm, Dv, Dy, Dg, Dw, Dout = (Dk("x"), Dk("B"), Dk("fm"), Dk("v"), Dk("y"), Dk("g"), Dk("w"),
                                          Dk("out"))

    with contextlib.ExitStack() as es:
        R = Rec(nc, es)

        uid = [0]

        def SB(st, name, shape, dt, dma=False):
            uid[0] += 1
            nm = "s%d_%s" % (uid[0], name)
            return st.enter_context(nc.sbuf_tensor(nm, list(shape), dt)), R.tile(nm, dma)

        def PS(st, name, shape, dt=F32):
            uid[0] += 1
            nm = "p%d_%s" % (uid[0], name)
            return st.enter_context(nc.psum_tensor(nm, list(shape), dt)), R.tile(nm)

        def mm(out, lhsT, rhs, start, stop, rd, wr, inc=True, skip=False):
            R.op("pe", lambda e: e.matmul(out, lhsT=lhsT, rhs=rhs, start=start, stop=stop,
                                          skip_group_check=skip), rd, wr, inc)

        def tr(out, in_, ident, rd, wr, inc=True):
            R.op("pe", lambda e: e.transpose(out=out, in_=in_, identity=ident), rd, wr, inc)

        def act(out, in_, func, rd, wr, **kw):
            R.op("act", lambda e: e.activation(out=out, in_=in_, func=func, **kw), rd, wr)

        def tt(eng, out, in0, in1, op, rd, wr):
            R.op(eng, lambda e: e.tensor_tensor(out=out, in0=in0, in1=in1, op=op), rd, wr)

        def ts(eng, out, in0, s1, s2, op0, op1, rd, wr):
            if op1 is None:
                R.op(eng, lambda e: e.tensor_scalar(out=out, in0=in0, scalar1=s1, scalar2=None, op0=op0), rd, wr)
            else:
                R.op(eng, lambda e: e.tensor_scalar(out=out, in0=in0, scalar1=s1, scalar2=s2, op0=op0, op1=op1),
                     rd, wr)

        def stt(out, in0, scalar, in1, op0, op1, rd, wr):
            R.op("dve", lambda e: e.scalar_tensor_tensor(out=out, in0=in0, scalar=scalar, in1=in1, op0=op0,
                                                        op1=op1), rd, wr)

        def recip(out, in_, rd, wr):
            R.op("dve", lambda e: e.reciprocal(out=out, in_=in_), rd, wr)

        def cp(eng, out, in_, rd, wr):
            R.op(eng, lambda e: e.tensor_copy(out=out, in_=in_), rd, wr)

        def memset(eng, ap, val, wr):
            R.op(eng, lambda e: e.memset(ap, val), (), wr)

        def bcast(ap2d, rows):
            n = ap2d.shape[-1]
            return bass.AP(tensor=ap2d.tensor, offset=ap2d.offset, ap=[[0, rows], [1, n]])

        identb, Tid = SB(es, "identb", [128, 128], BF16, True)
        colp, Tcolp = SB(es, "colp", [128, L, 4, 8, 2], F32)
        neglam, Tnl = SB(es, "neglam", [64, L], F32)
        gsub, Tgs = SB(es, "gsub", [64, L], F32, True)
        sgub, Tsgub = SB(es, "sgub", [128, L, 4], F32, True)
        sel65, Tsel = SB(es, "sel65", [65, 64], F32, True)
        ones64, Tones = SB(es, "ones64", [64, 64], F32, True)
        onesb, Tonesb = SB(es, "onesb", [128, 128], BF16, True)
        R.freeze_global()

        def cast_weights(l):
            Twc = R.tile("wcast%d" % l, True)
            for (src, dst, rows) in ((w_in_r, w_in_b, D), (w_out_r, w_out_b, 1280), (w_ff1, w_ff1_b, D),
                                     (w_ff2, w_ff2_b, DFF), (maskT, maskT_b, 128), (sgu_wT, sgu_wT_b, 128)):
                for r0 in range(0, rows, 512):
                    r1_ = min(rows, r0 + 512)
                    R.dma("pool", dst[l, r0:r1_, :], src[l, r0:r1_, :], Twc, writes=[Dw])
                    yield None

        with contextlib.ExitStack() as ps:
            R.dma("sp", identb[:], identb_d[:, :], Tid, writes=[Tid])
            R.dma("sp", sgub[:], sgu_bc[:, :, :], Tsgub, writes=[Tsgub])
            R.dma("sp", sel65[:], sel_d[:, :], Tsel, writes=[Tsel])
            R.dma("sp", ones64[:], ones64_d[:, :], Tones, writes=[Tones])
            R.dma("sp", onesb[:], onesb_d[:, :], Tonesb, writes=[Tonesb])
            for _ in cast_weights(0):
                pass
            zt, Tzt = SB(ps, "zt", [64, NTOK], BF16, True)
            memset("pool", zt[:], 0.0, [Tzt])
            for h in range(4):
                R.dma("sp", yT[6 + h, 64:128, :], zt[:], Tzt, reads=[Tzt], writes=[Dy])
            cct, Tcc = SB(ps, "cct", [128, 8, 2], F32, True)
            sct, Tsc = SB(ps, "sct", [128, 8, 2], F32)
            g1c, Tg1c = SB(ps, "g1c", [128, L, 8], F32, True)
            g2c, Tg2c = SB(ps, "g2c", [128, L, 8], F32, True)
            ab2, Tab2 = SB(ps, "ab2", [2, 6 * D], F32, True)
            idf2, Tidf2 = SB(ps, "idf2", [2, 2], F32, True)
            rows_sb, Trows = SB(ps, "rows_sb", [2, 6 * D], F32, True)
            aw = [SB(ps, "aw%d" % i, [128, 8, D], F32, True) for i in range(2)]
            colps_f, Tcolps = PS(ps, "colps", [128, 512])
            colps = colps_f[:, 0:64].rearrange("p (w f s) -> p w f s", w=4, f=8)
            rowps = [PS(ps, "rowps%d" % i, [2, 512]) for i in range(4)]
            R.dma("sp", cct[:], cc[:, :, :], Tcc, writes=[Tcc])
            R.dma("sp", g1c[:], n1gc[:, :, :], Tg1c, writes=[Tg1c])
            R.dma("sp", g2c[:], n2gc[:, :, :], Tg2c, writes=[Tg2c])
            R.dma("sp", idf2[:], ident2_d[:, :], Tidf2, writes=[Tidf2])
            act(sct[:], cct[:], AF.Silu, [Tcc], [Tsc])
            slab_w = {0: 0, 1: 1, 3: 2, 4: 3}
            ai = 0
            for l in range(nlayers):
                R.dma("sp", ab2[:], bass.AP(tensor=ada_b.tensor, offset=l * 6 * D, ap=[[0, 2], [1, 6 * D]]), Tab2,
                      writes=[Tab2])
                awv = ada_w[l].rearrange("(k p) n -> p k n", p=128)
                for sl in range(6):
                    awt, Taw = aw[ai % 2]
                    ai += 1
                    R.dma("sp", awt[:], awv[:, :, sl * D:(sl + 1) * D], Taw, writes=[Taw])
                    for j in range(2):
                        rp, Trp = rowps[(sl % 2) * 2 + j]
                        for k in range(8):
                            mm(rp[:, :], sct[:, k, :], awt[:, k, j * 512:(j + 1) * 512], k == 0, k == 7, [Taw, Tsc],
                               [Trp], inc=(k == 7))
                        c0 = sl * D + j * 512
                        tt("dve", rows_sb[:, c0:c0 + 512], rp[:, :], ab2[:, c0:c0 + 512], ALU.add, [Trp, Tab2], [Trows])
                    if sl in slab_w:
                        w = slab_w[sl]
                        for fc in range(8):
                            tr(colps[:, w, fc, :], rows_sb[:, sl * D + fc * 128: sl * D + (fc + 1) * 128], idf2[:],
                               [Trows, Tidf2], [Tcolps], inc=(fc == 7))
                R.dma("sp", gates[l, :, 0:D], rows_sb[:, 2 * D:3 * D], Trows, reads=[Trows], writes=[Dg])
                R.dma("sp", gates[l, :, D:2 * D], rows_sb[:, 5 * D:6 * D], Trows, reads=[Trows], writes=[Dg])
                cp("dve", colp[:, l, :, :, :], colps[:, :, :, :], [Tcolps], [Tcolp])
                for s_ in range(2):
                    for (w, gt, Tg) in ((1, g1c, Tg1c), (3, g2c, Tg2c)):
                        stt(colp[:, l, w, :, s_], colp[:, l, w, :, s_], 1.0, gt[:, l, :], ALU.add, ALU.mult,
                            [Tcolp, Tg], [Tcolp])
            lq = [SB(ps, "lq%d" % i, [64, L, 32], F32, True) for i in range(4)]
            li, Tli = SB(ps, "li", [64, L], F32, True)
            om, Tom = SB(ps, "om", [64, L], F32, True)
            sgc, Tsgc = SB(ps, "sgc", [64, L], F32, True)
            pr, Tpr = SB(ps, "pr", [64, 2, L, 32], F32)
            sm, Tsm = SB(ps, "sm", [64, 2, L], F32)
            for i in range(4):
                src = bass.AP(tensor=dl[i].tensor, offset=0, ap=[[0, 64], [1, L * 32]])
                R.dma("sp", lq[i][0][:].rearrange("p l d -> p (l d)"), src, lq[i][1], writes=[lq[i][1]])
            R.dma("sp", li[:], laminit_d[:, :], Tli, writes=[Tli])
            R.dma("sp", om[:], omli_d[:, :], Tom, writes=[Tom])
            R.dma("sp", sgc[:], subg_c[:, :], Tsgc, writes=[Tsgc])
            for m in range(2):
                tt("dve", pr[:, m, :, :], lq[2 * m][0][:], lq[2 * m + 1][0][:], ALU.mult,
                   [lq[2 * m][1], lq[2 * m + 1][1]], [Tpr])
            R.op("dve", lambda e: e.tensor_reduce(out=sm[:], in_=pr[:], axis=AX.X, op=ALU.add), [Tpr], [Tsm])
            act(sm[:], sm[:], AF.Exp, [Tsm], [Tsm])
            tt("dve", neglam[:], sm[:, 1, :], sm[:, 0, :], ALU.subtract, [Tsm], [Tnl])
            tt("dve", neglam[:], neglam[:], li[:], ALU.subtract, [Tnl, Tli], [Tnl])
            tt("dve", gsub[:], sgc[:], om[:], ALU.mult, [Tsgc, Tom], [Tgs])
            with nc.Block() as blk:
                R.end_phase(blk)

        blocks = [(i * 512, 512, 0) for i in range(8)] + [(SEQ, 256, 1)]

        def norm_A1(st_tiles, xt, Txt, nt, nhalf=None):
            junk, Tjunk, ss, Tss, rstd, Trstd, xn, Txn = st_tiles
            for j in range(nt):
                act(junk[:], xt[:, j, :], AF.Square, [Txt], [Tjunk, Tss], accum_out=ss[:, j:j + 1])
            if nhalf is None:
                act(rstd[:, 0:nt], ss[:, 0:nt], AF.Sqrt, [Tss], [Trstd], scale=1.0 / D, bias=EPS)
                recip(rstd[:, 0:nt], rstd[:, 0:nt], [Trstd], [Trstd])
            else:
                ts("dve", rstd[:, 0:nt], ss[:, 0:nt], 1.0 / D, EPS, ALU.mult, ALU.add, [Tss], [Trstd])
                tt("pool", rstd[:, 0:nt], rstd[:, 0:nt], nhalf[0][:, 0:nt], ALU.pow, [Trstd, nhalf[1]], [Trstd])
            for j in range(nt):
                if j % 2 == 0:
                    act(xn[:, j, :], xt[:, j, :], AF.Copy, [Txt, Trstd], [Txn], scale=rstd[:, j:j + 1])
                else:
                    ts("dve", xn[:, j, :], xt[:, j, :], rstd[:, j:j + 1], None, ALU.mult, None, [Txt, Trstd], [Txn])

        def norm_A2(st_tiles, nt, l, wsh, wsc, s, tps, hT, ThT):
            junk, Tjunk, ss, Tss, rstd, Trstd, xn, Txn = st_tiles
            for k in range(8):
                tp, Ttp = tps[k % 2]
                for j in range(nt):
                    tr(tp[:, j * 128:(j + 1) * 128], xn[:, j, k * 128:(k + 1) * 128], identb[:], [Txn, Tid], [Ttp],
                       inc=(j == nt - 1))
                if k % 2 == 0:
                    act(hT[:, k, 0:nt * 128], tp[:, 0:nt * 128], AF.Identity, [Ttp, Tcolp], [ThT],
                        scale=colp[:, l, wsc, k, s:s + 1], bias=colp[:, l, wsh, k, s:s + 1])
                else:
                    ts("dve", hT[:, k, 0:nt * 128], tp[:, 0:nt * 128], colp[:, l, wsc, k, s:s + 1],
                       colp[:, l, wsh, k, s:s + 1], ALU.mult, ALU.add, [Ttp, Tcolp], [ThT])

        def norm_to_hT(st_tiles, xt, Txt, nt, l, wsh, wsc, s, tps, hT, ThT, tag):
            norm_A1(st_tiles, xt, Txt, nt)
            norm_A2(st_tiles, nt, l, wsh, wsc, s, tps, hT, ThT)

        for l in range(nlayers):
            last = (l == L - 1)
            xsrc = xin if l == 0 else xres
            Dxs = Dk("xin") if l == 0 else Dx

            with contextlib.ExitStack() as ps:
                wi, Twi = SB(ps, "wi", [128, 8, WIN], BF16, True)
                cst, Tcs = SB(ps, "cst", [128, 256], BF16, True)
                wsT, TwsT = SB(ps, "wsT", [128, 512], BF16, True)
                lng, Tlng = SB(ps, "lng", [128, GW], F32, True)
                lnb, Tlnb = SB(ps, "lnb", [128, GW], F32, True)
                xts = [SB(ps, "xt%d" % i, [128, 4, D], F32, True) for i in range(3)]
                junk, Tjunk = SB(ps, "junk", [128, D], BF16)
                nst = []
                for i in range(2):
                    ss_, Tss_ = SB(ps, "ss%d" % i, [128, 4], F32)
                    rstd_, Trstd_ = SB(ps, "rstd%d" % i, [128, 4], F32)
                    xn_, Txn_ = SB(ps, "xn%d" % i, [128, 4, D], BF16)
                    nst.append((junk, Tjunk, ss_, Tss_, rstd_, Trstd_, xn_, Txn_))
                hTs = [SB(ps, "hT%d" % i, [128, 8, 512], BF16) for i in range(2)]
                coss = [SB(ps, "cost%d" % i, [128, 512], F32, True) for i in range(3)]
                sins = [SB(ps, "sint%d" % i, [128, 512], F32, True) for i in range(3)]
                aT, TaT = SB(ps, "aT", [128, 2, 512], BF16)
                Bsb, TBsb = SB(ps, "Bsb", [128, 4, 512], BF16, True)
                fmo = [SB(ps, "fmo%d" % i, [128, 512], BF16, True) for i in range(3)]
                r1, Tr1 = SB(ps, "r1", [128, 512], F32)
                r2, Tr2 = SB(ps, "r2", [128, 512], F32)
                vt, Tvt = SB(ps, "vt", [128, 4, 772], BF16, True)
                bst, Tbst = SB(ps, "bst", [128, 6], F32)
                mv, Tmv = SB(ps, "mv", [128, 2], F32)
                vn, Tvn = SB(ps, "vn", [128, GW], F32)
                tps = [PS(ps, "tp%d" % i, [128, 1024], BF16) for i in range(2)]
                fps = [PS(ps, "fps%d" % i, [128, 512]) for i in range(2)]
                tms = [PS(ps, "tms%d" % i, [128, 512]) for i in range(3)]
                sgp, Tsgp = PS(ps, "sgp", [128, 1024], BF16)

                R.dma("sp", wi[:], w_in_b[l].rearrange("(k p) n -> p k n", p=128), Twi, reads=[Dw], writes=[Twi])
                R.dma("sp", cst[:], cs_d[:, :], Tcs, writes=[Tcs])
                R.dma("sp", wsT[:], sgu_wT_b[l], TwsT, reads=[Dw], writes=[TwsT])
                R.dma("sp", lng[:], bcast(sgu_lng[l:l + 1, :], 128), Tlng, writes=[Tlng])
                R.dma("sp", lnb[:], bcast(sgu_lnb[l:l + 1, :], 128), Tlnb, writes=[Tlnb])
                memset("pool", vt[:], 1.0, [Tvt])
                nhf, Tnhf = SB(ps, "nhf", [128, 4], F32)
                memset("pool", nhf[:], -0.5, [Tnhf])
                fmi = 0
                nblk = len(blocks)

                def p1_LD(bi):
                    t0, ntok, s = blocks[bi]
                    xt, Txt = xts[bi % 3]
                    R.dma("sp", xt[:, 0:ntok // 128, :], xsrc[t0:t0 + ntok, :].rearrange("(j p) d -> p j d", p=128),
                          Txt, reads=[Dxs], writes=[Txt])
                    R.dma("sp", coss[bi % 3][0][:, 0:ntok], cos_d[:, t0:t0 + ntok], coss[bi % 3][1],
                          writes=[coss[bi % 3][1]])
                    R.dma("sp", sins[bi % 3][0][:, 0:ntok], sin_d[:, t0:t0 + ntok], sins[bi % 3][1],
                          writes=[sins[bi % 3][1]])

                def p1_A1(bi):
                    t0, ntok, s = blocks[bi]
                    norm_A1(nst[bi % 2], xts[bi % 3][0], xts[bi % 3][1], ntok // 128, nhalf=(nhf, Tnhf))

                def p1_A2(bi):
                    t0, ntok, s = blocks[bi]
                    norm_A2(nst[bi % 2], ntok // 128, l, 0, 1, s, tps, hTs[bi % 2][0], hTs[bi % 2][1])

                gels = [SB(ps, "gel%d" % i, [128, 512], F32) for i in range(3)]
                vlns = [SB(ps, "vln%d" % i, [128, GW], BF16) for i in range(3)]
                ycs = [SB(ps, "yc%d" % i, [128, GW], BF16) for i in range(2)]
                ycTs = [SB(ps, "ycT%d" % i, [128, 2, 512], BF16, True) for i in range(2)]
                gtile = [0]
                sgu_ent = {}

                def p1_sgu_step(kind, ent):
                    gi, bi_, j, t0_, ntok_ = ent
                    gel, Tgel = gels[gi % 3]
                    vln, Tvln = vlns[gi % 3]
                    yc, Tyc = ycs[gi % 2]
                    ycT, TycT = ycTs[bi_ % 2]
                    if kind == "S1":
                        tm, Ttm = tms[2]
                        for g in range(4):
                            mm(tm[:, g * 64:(g + 1) * 64], wsT[:, g * 128:(g + 1) * 128], vln[:, g * 64:(g + 1) * 64],
                               True, True, [TwsT, Tvln], [Ttm], inc=(g == 3))
                        for g in range(4):
                            stt(yc[:, g * 64:(g + 1) * 64], tm[:, g * 64:(g + 1) * 64], sgub[:, l, g:g + 1],
                                gel[:, g * 64:(g + 1) * 64], ALU.add, ALU.mult, [Ttm, Tsgub, Tgel], [Tyc])
                    else:
                        for c_ in range(2):
                            tr(sgp[:, c_ * 128:(c_ + 1) * 128], yc[:, c_ * 128:(c_ + 1) * 128], identb[:], [Tyc, Tid],
                               [Tsgp], inc=(c_ == 1))
                        act(ycT[:, :, j * 128:(j + 1) * 128], sgp[:, 0:256].rearrange("p (c t) -> p c t", c=2), AF.Copy,
                            [Tsgp], [TycT])
                        if (j + 1) * 128 == ntok_:
                            R.dma("pool", yT[4:6, :, t0_:t0_ + ntok_].rearrange("c p t -> p c t"), ycT[:, :, 0:ntok_],
                                  TycT, reads=[TycT], writes=[Dy])

                p1_LD(0)
                p1_LD(1)
                p1_A1(0)
                p1_A2(0)
                for bi, (t0, ntok, s) in enumerate(blocks):
                    nt = ntok // 128
                    hT, ThT = hTs[bi % 2]
                    cost, Tcos = coss[bi % 3]
                    sint, Tsin = sins[bi % 3]
                    if bi + 2 < nblk:
                        p1_LD(bi + 2)
                    if bi + 1 < nblk:
                        p1_A1(bi + 1)
                    order = [0, 1, 2, 3, 4, 5, 8, 6, 9, 7, 12, 10, 13, 11]
                    for ci, c in enumerate(order):
                        fp, Tfp = fps[ci % 2]
                        for k in range(8):
                            mm(fp[:, 0:ntok], wi[:, k, c * 128:(c + 1) * 128], hT[:, k, 0:ntok], k == 0, k == 7,
                               [Twi, ThT], [Tfp], inc=(k == 7))
                        if c < 2:
                            act(aT[:, c, 0:ntok], fp[:, 0:ntok], AF.Copy, [Tfp], [TaT])
                        elif c < 6:
                            fo, Tfo = fmo[fmi % 3]
                            fmi += 1
                            act(fo[:, 0:ntok], fp[:, 0:ntok], AF.Copy, [Tfp], [Tfo], scale=(0.125 if c < 4 else 1.0))
                            R.dma("pool", fmT[c - 2, :, t0:t0 + ntok], fo[:, 0:ntok], Tfo, reads=[Tfo], writes=[Dfm])
                        elif c in (8, 9, 12, 13):
                            tt("dve", r1[:, 0:ntok], fp[:, 0:ntok], sint[:, 0:ntok], ALU.mult, [Tfp, Tsin], [Tr1])
                        else:
                            tt("dve", r2[:, 0:ntok], fp[:, 0:ntok], cost[:, 0:ntok], ALU.mult, [Tfp, Tcos], [Tr2])
                            fo, Tfo = fmo[fmi % 3]
                            fmi += 1
                            tt("pool", fo[:, 0:ntok], r1[:, 0:ntok], r2[:, 0:ntok], ALU.add, [Tr1, Tr2], [Tfo])
                            dst = {6: 4, 7: 5, 10: 6, 11: 7}[c]
                            R.dma("pool", fmT[dst, :, t0:t0 + ntok], fo[:, 0:ntok], Tfo, reads=[Tfo], writes=[Dfm])
                    for j in range(nt):
                        gi = gtile[0]
                        gtile[0] += 1
                        gel, Tgel = gels[gi % 3]
                        vln, Tvln = vlns[gi % 3]
                        tm, Ttm = tms[0]
                        for c_ in range(2):
                            mm(tm[:, c_ * 256:(c_ + 1) * 256], aT[:, c_, j * 128:(j + 1) * 128], cst[:, :], True, True,
                               [TaT, Tcs], [Ttm])
                        cp("dve", Bsb[:, j, :], tm[:, :], [Ttm], [TBsb])
                        tm, Ttm = tms[1]
                        for k in range(8):
                            mm(tm[:, :], hT[:, k, j * 128:(j + 1) * 128], wi[:, k, NFM * 128:NFM * 128 + 512], k == 0,
                               k == 7, [ThT, Twi], [Ttm], inc=(k == 7))
                        act(vt[:, j, 0:260].rearrange("p (h d) -> p h d", d=65)[:, :, 0:64],
                            tm[:, 0:256].rearrange("p (h d) -> p h d", d=64), AF.Copy, [Ttm], [Tvt])
                        cp("dve", vt[:, j, 260:772].rearrange("p (h d) -> p h d", d=128)[:, :, 0:64],
                           tm[:, 256:512].rearrange("p (h d) -> p h d", d=64), [Ttm], [Tvt])
                        tm, Ttm = tms[0]
                        for k in range(8):
                            mm(tm[:, :], hT[:, k, j * 128:(j + 1) * 128], wi[:, k, NFM * 128 + 512:NFM * 128 + 1024],
                               k == 0, k == 7, [ThT, Twi], [Ttm], inc=(k == 7))
                        act(gel[:], tm[:, :], AF.Gelu_apprx_tanh, [Ttm], [Tgel])
                        R.op("dve", lambda e, gel=gel: e.bn_stats(out=bst[:], in_=gel[:, 256:512]), [Tgel], [Tbst])
                        R.op("dve", lambda e: e.bn_aggr(out=mv[:], in_=bst[:]), [Tbst], [Tmv])
                        ts("dve", vn[:], gel[:, 256:512], mv[:, 0:1], None, ALU.subtract, None, [Tgel, Tmv], [Tvn])
                        ts("dve", mv[:, 1:2], mv[:, 1:2], EPS, None, ALU.add, None, [Tmv], [Tmv])
                        tt("pool", mv[:, 1:2], mv[:, 1:2], nhf[:, 0:1], ALU.pow, [Tmv, Tnhf], [Tmv])
                        stt(vn[:], vn[:], mv[:, 1:2], lng[:], ALU.mult, ALU.mult, [Tvn, Tmv, Tlng], [Tvn])
                        tt("pool", vln[:], vn[:], lnb[:], ALU.add, [Tvn, Tlnb], [Tvln])
                        sgu_ent[gi] = (gi, bi, j, t0, ntok)
                        if gi - 2 >= 0:
                            p1_sgu_step("S1", sgu_ent[gi - 2])
                        if gi - 3 >= 0:
                            p1_sgu_step("S2", sgu_ent[gi - 3])
                    if bi + 1 < nblk:
                        p1_A2(bi + 1)
                    R.dma("pool", Bd[t0:t0 + ntok, :].rearrange("(j p) c -> p j c", p=128), Bsb[:, 0:nt, :], TBsb,
                          reads=[TBsb], writes=[DB])
                    R.dma("pool", vtok[t0:t0 + ntok, :].rearrange("(j p) c -> p j c", p=128),
                          vt[:, 0:nt, :], Tvt, reads=[Tvt], writes=[Dv])
                gl_ = gtile[0] - 1
                p1_sgu_step("S1", sgu_ent[gl_ - 1])
                p1_sgu_step("S2", sgu_ent[gl_ - 2])
                p1_sgu_step("S1", sgu_ent[gl_])
                p1_sgu_step("S2", sgu_ent[gl_ - 1])
                p1_sgu_step("S2", sgu_ent[gl_])
                with nc.Block() as blk:
                    R.end_phase(blk)

            with contextlib.ExitStack() as ps:
                Ball, TBall = SB(ps, "Ball", [128, NTILE, 512], BF16, True)
                d256, Td256 = SB(ps, "d256", [128, 2, 2, 256], BF16, True)
                dts = [[SB(ps, "dft%d_%d" % (kd, i), [128, 8, 512], BF16, True) for i in range(3)] for kd in range(2)]
                yaT = [SB(ps, "yaT%d" % i, [128, 2, 512], BF16, True) for i in range(2)]
                yps = [PS(ps, "yps%d" % i, [128, 512]) for i in range(4)]
                R.dma("sp", Ball[:], Bd.rearrange("(j p) c -> p j c", p=128), TBall, reads=[DB], writes=[TBall])
                R.dma("sp", d256[:].rearrange("p a b c -> p (a b c)"), dft256_d[:, :], Td256, writes=[Td256])
                li_ = 0
                for nb in range(8):
                    ya, Tya = yaT[nb % 2]
                    for qd in range(4):
                        bufs = []
                        for kd in range(2):
                            dt_, Tdt = dts[kd][li_ % 3]
                            R.dma("sp", dt_[:].rearrange("p a b -> p (a b)"),
                                  dft_d[kd, nb, :, qd * 8 * 512:(qd + 1) * 8 * 512], Tdt, writes=[Tdt])
                            bufs.append((dt_, Tdt))
                        li_ += 1
                        for c in range(2):
                            yp, Typ = yps[(nb % 2) * 2 + c]
                            for n8 in range(8):
                                nti = qd * 8 + n8
                                for kd in range(2):
                                    dt_, Tdt = bufs[kd]
                                    lastmm = (qd == 3 and n8 == 7 and kd == 1)
                                    mm(yp[:, :], Ball[:, nti, c * 256 + kd * 128:c * 256 + (kd + 1) * 128],
                                       dt_[:, n8, :], (qd == 0 and n8 == 0 and kd == 0), lastmm, [TBall, Tdt], [Typ],
                                       inc=(lastmm or (n8 == 7 and kd == 1)))
                    for c in range(2):
                        yp, Typ = yps[(nb % 2) * 2 + c]
                        if c == 0:
                            act(ya[:, c, :], yp[:, :], AF.Copy, [Typ], [Tya])
                        else:
                            cp("dve", ya[:, c, :], yp[:, :], [Typ], [Tya])
                    R.dma("pool", yT[0:2, :, nb * 512:(nb + 1) * 512].rearrange("c p t -> p c t"), ya[:], Tya,
                          reads=[Tya], writes=[Dy])
                if not last:
                    ya, Tya = yaT[0]
                    for c in range(2):
                        yp, Typ = yps[c]
                        for n2 in range(2):
                            for kd in range(2):
                                mm(yp[:, 0:256], Ball[:, 32 + n2, c * 256 + kd * 128:c * 256 + (kd + 1) * 128],
                                   d256[:, n2, kd, :], (n2 == 0 and kd == 0), (n2 == 1 and kd == 1), [TBall, Td256],
                                   [Typ], inc=(n2 == 1 and kd == 1))
                        cp("dve", ya[:, c, 0:256], yp[:, 0:256], [Typ], [Tya])
                    R.dma("pool", yT[0:2, :, SEQ:NTOK].rearrange("c p t -> p c t"), ya[:, :, 0:256], Tya, reads=[Tya],
                          writes=[Dy])
                with nc.Block() as blk:
                    R.end_phase(blk)

            with contextlib.ExitStack() as ps:
                qT, TqT = SB(ps, "qT", [128, 2, NTOK], BF16, True)
                kTs = [SB(ps, "kTp%d" % h, [128, NTOK], BF16, True) for h in range(4)]
                vna, Tvna = SB(ps, "vna", [128, NTILE, 260], BF16, True)
                msk, Tmsk = SB(ps, "msk", [128, 12800], BF16, True)
                Es = [SB(ps, "E%d" % i, [128, 7, 128], BF16) for i in range(3)]
                rc, Trc = SB(ps, "rc", [128, 4], F32)
                ybs = [SB(ps, "yb%d" % i, [128, GW], BF16) for i in range(2)]
                ybTs = [SB(ps, "ybT%d" % i, [128, 2, 512], BF16, True) for i in range(2)]
                sABs = [PS(ps, "sAB%d" % i, [128, 1024]) for i in range(2)]
                ops_ = [PS(ps, "ops%d" % i, [128, 4, 65]) for i in range(2)]
                trps = [PS(ps, "trp%d" % i, [128, 1024], BF16) for i in range(2)]
                R.dma("sp", qT[:], fmT[0:2].rearrange("c p t -> p c t"), TqT, reads=[Dfm], writes=[TqT])
                for h in range(4):
                    r0 = (h % 2) * 64
                    memset("dve" if h % 2 == 0 else "pool", kTs[h][0][:], 0.0, [kTs[h][1]])
                    R.dma("sp", kTs[h][0][r0:r0 + 64, :], fmT[2 + h // 2, r0:r0 + 64, :], kTs[h][1], reads=[Dfm],
                          writes=[kTs[h][1]])
                R.dma("sp", vna[:], vtok[:, 0:260].rearrange("(j p) c -> p j c", p=128), Tvna, reads=[Dv],
                      writes=[Tvna])
                R.dma("sp", msk[:], maskT_b[l], Tmsk, reads=[Dw], writes=[Tmsk])
                ntq = 32 if last else 34
                items = []
                for t in range(ntq):
                    if t < 32:
                        kt0 = min(max(t - 2, 0), 27)
                        kts = [kt0 + i for i in range(5)] + [32, 33]
                        pat = {0: 0, 1: 1, 30: 3, 31: 4}.get(t, 2)
                    else:
                        kts, pat = [32, 33], None
                    for h in range(4):
                        items.append((t, h, kts, pat))

                def na_S(i):
                    t, h, kts, pat = items[i]
                    cq, b0 = h // 2, (h % 2) * 64
                    sAB, TsAB = sABs[i % 2]
                    for idx, kt in enumerate(kts):
                        dsl = sAB[:, idx * 128:(idx + 1) * 128]
                        masked = (pat is not None and idx < 5)
                        lastm = (idx == len(kts) - 1)
                        mm(dsl, kTs[h][0][:, kt * 128:(kt + 1) * 128], qT[:, cq, t * 128:(t + 1) * 128],
                           True, not masked, [kTs[h][1], TqT], [TsAB], inc=(lastm and not masked))
                        if masked:
                            m0 = ((pat * 4 + h) * 5 + idx) * 128
                            mm(dsl, identb[:], msk[:, m0:m0 + 128], False, True, [Tid, Tmsk], [TsAB], inc=lastm)

                def na_EX(i):
                    t, h, kts, pat = items[i]
                    nk = len(kts)
                    sAB, TsAB = sABs[i % 2]
                    E, TE = Es[i % 3]
                    act(E[:, 0:nk, :], sAB[:, 0:nk * 128].rearrange("p (a q) -> p a q", q=128), AF.Exp, [TsAB], [TE])

                def na_PV(i):
                    t, h, kts, pat = items[i]
                    nk = len(kts)
                    E, TE = Es[i % 3]
                    op_, Top = ops_[t % 2]
                    for idx, kt in enumerate(kts):
                        mm(op_[:, h, :], E[:, idx, :], vna[:, kt, h * 65:(h + 1) * 65], idx == 0, idx == nk - 1,
                           [TE, Tvna], [Top], inc=(idx == nk - 1))

                def na_tail1(t):
                    op_, Top = ops_[t % 2]
                    yb, Tyb = ybs[t % 2]
                    recip(rc[:], op_[:, :, 64], [Top], [Trc])
                    for h in range(4):
                        ts("dve", yb[:, h * 64:(h + 1) * 64], op_[:, h, 0:64], rc[:, h:h + 1], None, ALU.mult, None,
                           [Top, Trc], [Tyb])

                def na_tail2(t):
                    yb, Tyb = ybs[t % 2]
                    trp, Ttrp = trps[t % 2]
                    for c in range(2):
                        tr(trp[:, c * 128:(c + 1) * 128], yb[:, c * 128:(c + 1) * 128], identb[:], [Tyb, Tid], [Ttrp],
                           inc=(c == 1))

                def na_tail3(t):
                    trp, Ttrp = trps[t % 2]
                    ybT, TybT = ybTs[(t // 4) % 2]
                    j4 = t % 4
                    act(ybT[:, :, j4 * 128:(j4 + 1) * 128], trp[:, 0:256].rearrange("p (c t) -> p c t", c=2), AF.Copy,
                        [Ttrp], [TybT])
                    if j4 == 3 or t == ntq - 1:
                        tb = (t // 4) * 512
                        n_ = (j4 + 1) * 128
                        R.dma("pool", yT[2:4, :, tb:tb + n_].rearrange("c p t -> p c t"), ybT[:, :, 0:n_], TybT,
                              reads=[TybT], writes=[Dy])

                pend = []
                n_it = len(items)
                na_S(0)
                na_S(1)
                for i in range(n_it):
                    na_EX(i)
                    if i + 2 < n_it:
                        na_S(i + 2)
                    na_PV(i)
                    t, h = items[i][0], items[i][1]
                    if h == 3:
                        na_tail1(t)
                        pend.append((i + 1, na_tail2, t))
                        pend.append((i + 2, na_tail3, t))
                    keep = []
                    for (due, fn, arg) in pend:
                        if due <= i:
                            fn(arg)
                        else:
                            keep.append((due, fn, arg))
                    pend = keep
                for (due, fn, arg) in pend:
                    fn(arg)
                with nc.Block() as blk:
                    R.end_phase(blk)

            with contextlib.ExitStack() as ps:
                Qh, TQh = SB(ps, "Qc", [128, 2, NTOK], BF16, True)
                Kps = [SB(ps, "Kp%d" % v, [128, NTOK], BF16, True) for v in range(8)]
                vdf, Tvdf = SB(ps, "vdf", [128, NTILE, 512], BF16, True)
                Eb = [SB(ps, "Eb%d" % i, [128, 2, 512], BF16) for i in range(3)]
                rr = [SB(ps, "rr%d" % i, [64, 512], F32) for i in range(2)]
                dd, Tdd = SB(ps, "dd", [64, 512], F32)
                d2, Td2 = SB(ps, "d2", [64, 512], F32)
                dsq, Tdsq = SB(ps, "dsq", [64, 512], F32)
                rs_, Trs = SB(ps, "rs_", [64, 512], F32)
                ydT, TydT = SB(ps, "ydT", [64, 512], BF16, True)
                sps = [PS(ps, "sps%d" % i, [128, 2, 512]) for i in range(2)]
                ops2 = [PS(ps, "ops2_%d" % i, [128, 512]) for i in range(4)]

                def acc_of(bn_, m_):
                    return ops2[2 * (bn_ % 2) + m_]
                dhi, Tdhi = SB(ps, "dhi", [128, 512], BF16)
                dlo, Tdlo = SB(ps, "dlo", [128, 512], BF16)
                memset("pool", dhi[:], 0.0, [Tdhi])
                memset("pool", dlo[:], 0.0, [Tdlo])
                R.dma("sp", Qh[:], fmT[4:6].rearrange("c p t -> p c t"), TQh, reads=[Dfm], writes=[TQh])
                for v in range(8):
                    ch, r0 = v // 4, (v % 4) * 32
                    memset("dve" if v % 2 == 0 else "pool", Kps[v][0][:], 0.0, [Kps[v][1]])
                    R.dma("sp", Kps[v][0][r0:r0 + 32, :], fmT[6 + ch, r0:r0 + 32, :], Kps[v][1], reads=[Dfm],
                          writes=[Kps[v][1]])
                R.dma("sp", vdf[:], vtok[:, 260:772].rearrange("(j p) c -> p j c", p=128), Tvdf, reads=[Dv],
                      writes=[Tvdf])
                cast_gen = cast_weights(l + 1) if l + 1 < nlayers else iter(())
                qblocks = blocks[:8] if last else blocks
                sc_ = 32.0 ** -0.5
                steps = []
                bnum = 0
                for h in range(4):
                    for (q0, nq, s) in qblocks:
                        kts = list(range(NTILE)) if s == 0 else [32, 33]
                        for ki, kt in enumerate(kts):
                            steps.append((h, q0, nq, ki, kt, len(kts), bnum))
                        bnum += 1

                def df_S(i):
                    h, q0, nq, ki, kt, nk, bn = steps[i]
                    sp_, Tsp = sps[i % 2]
                    for m in range(2):
                        mm(sp_[:, m, 0:nq], Kps[h * 2 + m][0][:, kt * 128:(kt + 1) * 128], Qh[:, h // 2, q0:q0 + nq], True,
                           True, [Kps[h * 2 + m][1], TQh], [Tsp], inc=(m == 1))

                def df_EX(i):
                    h, q0, nq, ki, kt, nk, bn = steps[i]
                    sp_, Tsp = sps[i % 2]
                    E, TE = Eb[i % 3]
                    act(E[:, :, 0:nq], sp_[:, :, 0:nq], AF.Exp, [Tsp], [TE], scale=sc_)

                def df_PV(i):
                    h, q0, nq, ki, kt, nk, bn = steps[i]
                    E, TE = Eb[i % 3]
                    for m in range(2):
                        o2, To2 = acc_of(bn, m)
                        mm(o2[:, 0:nq], vdf[:, kt, h * 128:(h + 1) * 128], E[:, m, 0:nq], ki == 0, ki == nk - 1,
                           [Tvdf, TE], [To2], inc=True)

                def df_post1(arg):
                    h, q0, nq, bn = arg
                    a0, Ta0 = acc_of(bn, 0)
                    a1, Ta1 = acc_of(bn, 1)
                    recip(rr[0][0][:, 0:nq], a0[64:128, 0:nq], [Ta0], [rr[0][1]])
                    tt("dve", dd[:, 0:nq], a0[0:64, 0:nq], rr[0][0][:, 0:nq], ALU.mult, [Ta0, rr[0][1]], [Tdd])
                    recip(rr[1][0][:, 0:nq], a1[64:128, 0:nq], [Ta1], [rr[1][1]])
                    tt("dve", d2[:, 0:nq], a1[0:64, 0:nq], rr[1][0][:, 0:nq], ALU.mult, [Ta1, rr[1][1]], [Td2])
                    stt(dd[:, 0:nq], d2[:, 0:nq], neglam[:, l:l + 1], dd[:, 0:nq], ALU.mult, ALU.add, [Td2, Tnl, Tdd],
                        [Tdd])
                    tt("pool", dsq[:, 0:nq], dd[:, 0:nq], dd[:, 0:nq], ALU.mult, [Tdd], [Tdsq])
                    cp("dve", dhi[0:64, 0:nq], dsq[:, 0:nq], [Tdsq], [Tdhi])
                    tt("pool", dlo[0:64, 0:nq], dsq[:, 0:nq], dhi[0:64, 0:nq], ALU.subtract, [Tdsq, Tdhi], [Tdlo])

                def df_post2(arg):
                    h, q0, nq, bn = arg
                    bp, Tbp = acc_of(bn, 0)
                    mm(bp[:, 0:nq], onesb[:, :], dhi[:, 0:nq], True, False, [Tonesb, Tdhi], [Tbp], inc=False)
                    mm(bp[:, 0:nq], onesb[:, :], dlo[:, 0:nq], False, True, [Tonesb, Tdlo], [Tbp])

                def df_post3(arg):
                    h, q0, nq, bn = arg
                    bp, Tbp = acc_of(bn, 0)
                    act(rs_[:, 0:nq], bp[0:64, 0:nq], AF.Ln, [Tbp], [Trs], scale=1.0 / 64, bias=EPS)
                    act(rs_[:, 0:nq], rs_[:, 0:nq], AF.Exp, [Trs], [Trs], scale=-0.5)
                    stt(ydT[:, 0:nq], dd[:, 0:nq], gsub[:, l:l + 1], rs_[:, 0:nq], ALU.mult, ALU.mult,
                        [Tdd, Tgs, Trs], [TydT])
                    R.dma("pool", yT[6 + h, 0:64, q0:q0 + nq], ydT[:, 0:nq], TydT, reads=[TydT], writes=[Dy])

                pend = []
                n_it = len(steps)
                df_S(0)
                df_S(1)
                for i in range(n_it):
                    df_EX(i)
                    if i + 2 < n_it:
                        df_S(i + 2)
                    df_PV(i)
                    if i >= 20 and i % 24 == 0:
                        next(cast_gen, None)
                    h, q0, nq, ki, kt, nk, bn = steps[i]
                    if ki == nk - 1:
                        for (due, fn, arg) in pend:
                            fn(arg)
                        pend = []
                        df_post1((h, q0, nq, bn))
                        pend.append((i + 12, df_post2, (h, q0, nq, bn)))
                        pend.append((i + 16, df_post3, (h, q0, nq, bn)))
                    keep = []
                    for (due, fn, arg) in pend:
                        if due <= i:
                            fn(arg)
                        else:
                            keep.append((due, fn, arg))
                    pend = keep
                for (due, fn, arg) in pend:
                    fn(arg)
                for _ in cast_gen:
                    pass
                with nc.Block() as blk:
                    R.end_phase(blk)

            with contextlib.ExitStack() as ps:
                wo, Two = SB(ps, "wo", [128, 10, D], BF16, True)
                gb = [[SB(ps, "gb%d_%d" % (s, g), [128, D], F32, True) for g in range(2)] for s in range(2)]
                fgb, Tfgb = SB(ps, "fgb", [128, D], F32, True)
                xts = [SB(ps, "x3_%d" % i, [128, 4, D], F32, True) for i in range(2)]
                yTb, TyTb = SB(ps, "yTb", [128, 10, 512], BF16, True)
                junk, Tjunk = SB(ps, "junk3", [128, D], BF16)
                ss, Tss = SB(ps, "ss3", [128, 4], F32)
                rstd, Trstd = SB(ps, "rstd3", [128, 4], F32)
                xn, Txn = SB(ps, "xn3", [128, 4, D], BF16)
                hT, ThT = SB(ps, "h2T", [128, 8, 512], BF16)
                aT, TaT = SB(ps, "aT3", [128, 32, 512], BF16)
                tmpv = [SB(ps, "tmp%d" % i, [128, 512], F32) for i in range(2)]
                sqv = [SB(ps, "sq%d" % i, [128, 512], F32) for i in range(2)]
                w1b = [SB(ps, "w1b%d" % i, [128, 8, 512], BF16, True) for i in range(2)]
                w2b = [SB(ps, "w2b%d" % i, [128, 4, 512], BF16, True) for i in range(3)]
                acc = [PS(ps, "acc%d" % i, [128, 512]) for i in range(4)]
                f1p = [PS(ps, "f1p%d" % i, [128, 512]) for i in range(2)]
                tps = [PS(ps, "tp3_%d" % i, [128, 1024], BF16) for i in range(2)]
                R.dma("sp", wo[:], w_out_b[l].rearrange("(c p) n -> p c n", p=128), Two, reads=[Dw], writes=[Two])
                for s in range(2):
                    for g in range(2):
                        R.dma("sp", gb[s][g][0][:], bcast(gates[l, s:s + 1, g * D:(g + 1) * D], 128), gb[s][g][1],
                              reads=[Dg], writes=[gb[s][g][1]])
                if last:
                    R.dma("sp", fgb[:], bcast(final_g[0:1, :], 128), Tfgb, writes=[Tfgb])
                w1v = w_ff1_b[l].rearrange("(k p) f -> p k f", p=128)
                w2v = w_ff2_b[l].rearrange("(c p) d -> p c d", p=128)
                i1 = i2 = 0
                tcount = 0
                p3blocks = blocks[:8] if last else blocks

                def p3_load(bi_, q):
                    t0_, ntok_, s_ = p3blocks[bi_]
                    xt_, Txt_ = xts[bi_ % 2]
                    R.dma(q, xt_[:, 0:ntok_ // 128, :], xsrc[t0_:t0_ + ntok_, :].rearrange("(j p) d -> p j d", p=128),
                          Txt_, reads=[Dxs], writes=[Txt_])
                    R.dma(q, yTb[:, :, 0:ntok_], yT[:, :, t0_:t0_ + ntok_].rearrange("c p t -> p c t"), TyTb,
                          reads=[Dy], writes=[TyTb])

                p3_load(0, "sp")
                for bi, (t0, ntok, s) in enumerate(p3blocks):
                    nt = ntok // 128
                    xt, Txt = xts[bi % 2]
                    g1b, Tg1b = gb[s][0]
                    g2b, Tg2b = gb[s][1]
                    for j in range(nt):
                        for n in range(2):
                            ap_, Tap = acc[(j * 2 + n) % 4]
                            for c in range(10):
                                kc = 128 if c < 6 else 64
                                mm(ap_[:, :], yTb[0:kc, c, j * 128:(j + 1) * 128], wo[0:kc, c, n * 512:(n + 1) * 512],
                                   c == 0, c == 9, [TyTb, Two], [Tap], inc=(c == 9))
                            tv, Ttv = tmpv[tcount % 2]
                            tcount += 1
                            tt("dve", tv[:], ap_[:, :], g1b[:, n * 512:(n + 1) * 512], ALU.mult, [Tap, Tg1b], [Ttv])
                            tt("pool", xt[:, j, n * 512:(n + 1) * 512], xt[:, j, n * 512:(n + 1) * 512], tv[:], ALU.add,
                               [Txt, Ttv], [Txt])
                    if bi + 1 < len(p3blocks):
                        p3_load(bi + 1, "pool")
                    norm_to_hT((junk, Tjunk, ss, Tss, rstd, Trstd, xn, Txn), xt, Txt, nt, l, 2, 3, s, tps, hT, ThT, "p3")
                    for g8 in range(8):
                        w1t, Tw1 = w1b[i1 % 2]
                        i1 += 1
                        R.dma("sp", w1t[:], w1v[:, :, g8 * 512:(g8 + 1) * 512], Tw1, reads=[Dw], writes=[Tw1])
                        for c4 in range(4):
                            c = g8 * 4 + c4
                            fp, Tfp = f1p[c % 2]
                            for k in range(8):
                                mm(fp[:, 0:ntok], w1t[:, k, c4 * 128:(c4 + 1) * 128], hT[:, k, 0:ntok], k == 0, k == 7,
                                   [Tw1, ThT], [Tfp], inc=(k == 7))
                            sq, Tsq = sqv[c % 2]
                            act(sq[:, 0:ntok], fp[:, 0:ntok], AF.Square, [Tfp], [Tsq])
                            stt(aT[:, c, 0:ntok], fp[:, 0:ntok], 0.0, sq[:, 0:ntok], ALU.is_gt, ALU.mult, [Tfp, Tsq], [TaT])
                    for n in range(2):
                        for g8 in range(8):
                            w2t, Tw2 = w2b[i2 % 3]
                            i2 += 1
                            R.dma("sp", w2t[:], w2v[:, g8 * 4:(g8 + 1) * 4, n * 512:(n + 1) * 512], Tw2, reads=[Dw],
                                  writes=[Tw2])
                            for j in range(nt):
                                ap_, Tap = acc[j]
                                for c4 in range(4):
                                    c = g8 * 4 + c4
                                    mm(ap_[:, :], aT[:, c, j * 128:(j + 1) * 128], w2t[:, c4, :], (g8 == 0 and c4 == 0),
                                       (g8 == 7 and c4 == 3), [TaT, Tw2], [Tap], inc=(c4 == 3))
                        for j in range(nt):
                            ap_, Tap = acc[j]
                            tv, Ttv = tmpv[tcount % 2]
                            tcount += 1
                            tt("dve", tv[:], ap_[:, :], g2b[:, n * 512:(n + 1) * 512], ALU.mult, [Tap, Tg2b], [Ttv])
                            tt("pool", xt[:, j, n * 512:(n + 1) * 512], xt[:, j, n * 512:(n + 1) * 512], tv[:], ALU.add,
                               [Txt, Ttv], [Txt])
                    if not last:
                        R.dma("pool", xres[t0:t0 + ntok, :].rearrange("(j p) d -> p j d", p=128), xt[:, 0:nt, :], Txt,
                              reads=[Txt], writes=[Dx])
                    else:
                        for j in range(nt):
                            act(junk[:], xt[:, j, :], AF.Square, [Txt], [Tjunk, Tss], accum_out=ss[:, j:j + 1])
                        act(rstd[:, 0:nt], ss[:, 0:nt], AF.Sqrt, [Tss], [Trstd], scale=1.0 / D, bias=EPS)
                        recip(rstd[:, 0:nt], rstd[:, 0:nt], [Trstd], [Trstd])
                        for j in range(nt):
                            stt(xt[:, j, :], xt[:, j, :], rstd[:, j:j + 1], fgb[:], ALU.mult, ALU.mult,
                                [Txt, Trstd, Tfgb], [Txt])
                        R.dma("pool", out_d[t0:t0 + ntok, :].rearrange("(j p) d -> p j d", p=128), xt[:, 0:nt, :], Txt,
                              reads=[Txt], writes=[Dout])
                with nc.Block() as blk:
                    R.end_phase(blk)
    return nc


def _consts():
    bf = ml_dtypes.bfloat16
    c = {}
    c["identb"] = np.eye(128, dtype=np.float32).astype(bf)
    k = np.arange(64)
    th = 2 * np.pi * np.outer(k, k) / 64.0
    cc, sc = np.cos(th) / 8.0, np.sin(th) / 8.0
    z = np.zeros((64, 64))
    cs = np.concatenate([np.block([[cc, z], [z, cc]]), np.block([[sc, z], [z, sc]])], axis=1)
    c["cs_tab"] = cs.astype(np.float32).astype(bf)
    n = np.arange(SEQ)
    dft = np.empty((2, 8, 128, 32, 512), dtype=bf)
    for nb in range(8):
        npr = nb * 512 + np.arange(512)
        m = (np.outer(n, npr) % SEQ).astype(np.float64)
        ang = 2 * np.pi * m / SEQ
        cm = (np.cos(ang) / 64.0).astype(np.float32).reshape(32, 128, 512).transpose(1, 0, 2)
        sm = (-np.sin(ang) / 64.0).astype(np.float32).reshape(32, 128, 512).transpose(1, 0, 2)
        dft[0, nb] = cm.astype(bf)
        dft[1, nb] = sm.astype(bf)
    c["dft"] = dft.reshape(2, 8, 128, 32 * 512)
    n2 = np.arange(CTX)
    ang = 2 * np.pi * (np.outer(n2, n2) % CTX) / CTX
    d256 = np.stack([np.cos(ang) / 16.0, -np.sin(ang) / 16.0], axis=0)
    d256 = d256.reshape(2, 2, 128, 256).transpose(2, 1, 0, 3)
    c["dft256"] = np.ascontiguousarray(d256).astype(np.float32).astype(bf).reshape(128, 1024)
    t = np.arange(SEQ)
    rows, cols = (t // 64).astype(np.float32), (t % 64).astype(np.float32)
    inv = (10000.0 ** (-np.arange(8, dtype=np.float32) / 8)).astype(np.float32)
    cos_t = np.ones((128, NTOK), dtype=np.float32)
    sin_t = np.zeros((128, NTOK), dtype=np.float32)
    for p in range(128):
        i = p % 32
        a, f, hh = i // 16, i % 8, (i // 8) % 2
        ang = ((rows if a == 0 else cols) * inv[f]).astype(np.float32)
        cos_t[p, :SEQ] = np.cos(ang)
        sin_t[p, :SEQ] = np.sin(ang) * (-1.0 if hh == 0 else 1.0)
    c["cos_tab"], c["sin_t"] = cos_t, sin_t
    sel = np.zeros((65, 64), dtype=np.float32)
    sel[64, :] = 1.0
    c["sel65"] = sel
    c["ones64"] = np.ones((64, 64), dtype=np.float32)
    ob = np.zeros((128, 128), dtype=np.float32)
    ob[0:64, :] = 1.0
    c["onesb"] = ob.astype(bf)
    li = np.array([0.8 - 0.6 * math.exp(-0.3 * l) for l in range(L)], dtype=np.float32)
    c["laminit"] = np.tile(li[None, :], (64, 1)).astype(np.float32)
    c["omli"] = np.tile((1.0 - li)[None, :], (64, 1)).astype(np.float32)
    c["ident2"] = np.eye(2, dtype=np.float32)
    return c


def _mask_index():
    treps = [0, 1, 2, 30, 31]
    p = np.arange(128)[:, None]
    q = np.arange(128)[None, :]
    idx = np.zeros((128, 5, 5, 128), dtype=np.int64)
    val = np.zeros((128, 5, 5, 128), dtype=bool)
    for pi, t in enumerate(treps):
        kt0 = min(max(t - 2, 0), 27)
        for j in range(5):
            kr = 2 * (kt0 + j) + p // 64
            kc = p % 64
            r = 2 * t + q // 64
            c = q % 64
            rs = np.clip(r - 4, 0, 56)
            cs = np.clip(c - 8, 0, 48)
            ok = (kr >= rs) & (kr < rs + 8) & (kc >= cs) & (kc < cs + 16)
            dr = kr - r + 7
            dc = np.clip(kc - c, -15, 15) + 15
            idx[:, pi, j, :] = np.where(ok, dr * 31 + dc, 0)
            val[:, pi, j, :] = ok
    return idx, val


_CACHE = {}


def _prep_shared(inp):
    f = np.float32
    sh = {}
    sh["ada_w"] = np.ascontiguousarray(inp["ada_w"], dtype=f)
    sh["ada_b"] = np.ascontiguousarray(inp["ada_b"], dtype=f)
    sh["n1gc"] = np.ascontiguousarray(np.asarray(inp["norm1_g"], dtype=f).reshape(L, 8, 128).transpose(2, 0, 1))
    sh["n2gc"] = np.ascontiguousarray(np.asarray(inp["norm2_g"], dtype=f).reshape(L, 8, 128).transpose(2, 0, 1))
    w_in = np.asarray(inp["w_in"], dtype=f)
    cd = 6 * GW
    j = np.arange(256)
    cols = np.concatenate([np.arange(0, 256), np.arange(256, 512), np.arange(512, 768),
                           cd + j, cd + (j ^ 8), cd + 256 + j, cd + 256 + (j ^ 8),
                           np.arange(768, 1024), np.arange(cd + 512, cd + 768), np.arange(1024, 1536)])
    assert cols.shape[0] == WIN
    sh["w_in_r"] = np.ascontiguousarray(w_in[:, :, cols])
    w_out = np.asarray(inp["w_out"], dtype=f)
    wo = np.zeros((L, 10, 128, D), dtype=f)
    wo[:, 0:6] = w_out[:, 0:768].reshape(L, 6, 128, D)
    wo[:, 6:10, 0:64] = w_out[:, 768:1024].reshape(L, 4, 64, D)
    sh["w_out_r"] = wo.reshape(L, 1280, D)
    sh["w_ff1"] = np.ascontiguousarray(inp["w_ff1"], dtype=f)
    sh["w_ff2"] = np.ascontiguousarray(inp["w_ff2"], dtype=f)
    idx, val = _mask_index()
    rpb = np.asarray(inp["na_rpb"], dtype=f).reshape(L, 4, 15 * 31)
    mk = np.empty((L, 128, 5, 4, 5, 128), dtype=f)
    for l in range(L):
        for h in range(4):
            mk[l, :, :, h] = np.where(val, rpb[l, h][idx], f(NEG))
    sh["maskT"] = mk.reshape(L, 128, 12800)
    sw = np.asarray(inp["sgu_w"], dtype=f)
    sh["sgu_wT"] = np.ascontiguousarray(sw.transpose(0, 3, 1, 2)).reshape(L, 128, 512)
    sh["sgu_bc"] = np.ascontiguousarray(np.asarray(inp["sgu_b"], dtype=f).transpose(2, 0, 1))
    sh["sgu_lng"] = np.ascontiguousarray(inp["sgu_ln_g"], dtype=f)
    sh["sgu_lnb"] = np.ascontiguousarray(inp["sgu_ln_b"], dtype=f)
    for i, nm in enumerate(("diff_lq1", "diff_lk1", "diff_lq2", "diff_lk2")):
        sh["diff_l%d" % i] = np.ascontiguousarray(inp[nm], dtype=f)
    sh["subg_c"] = np.ascontiguousarray(np.asarray(inp["diff_subln_g"], dtype=f).T)
    sh["final_g"] = np.asarray(inp["final_g"], dtype=f).reshape(1, D)
    return sh


def make_in_maps(inp):
    if "consts" not in _CACHE:
        c = _consts()
        c["sin_tab"] = c.pop("sin_t")
        _CACHE["consts"] = c
    sh = _prep_shared(inp)
    sh.update(_CACHE["consts"])
    x = np.asarray(inp["x"], dtype=np.float32)
    ctx = np.asarray(inp["ctx"], dtype=np.float32)
    c = np.asarray(inp["c"], dtype=np.float32)
    c_ctx = np.asarray(inp["c_ctx"], dtype=np.float32)
    maps = []
    for b in range(8):
        m = dict(sh)
        m["xin"] = np.concatenate([x[b], ctx[b]], axis=0)
        cc = np.stack([c[b], c_ctx], axis=-1).reshape(8, 128, 2).transpose(1, 0, 2)
        m["cc"] = np.ascontiguousarray(cc)
        maps.append(m)
    return maps


def kernel(**inputs):
    if "nc" not in _CACHE:
        _CACHE["nc"] = build()
    nc = _CACHE["nc"]
    maps = make_in_maps(inputs)
    res = run_bass_kernel_spmd(nc, maps, core_ids=list(range(8)))
    return np.stack([np.asarray(r["out"], dtype=np.float32) for r in res.results], axis=0)
```

```python
import contextlib
import math
import numpy as np
import ml_dtypes
import concourse.bass as bass
import concourse.mybir as mybir
from concourse.bass_utils import run_bass_kernel_spmd

F32 = mybir.dt.float32
BF16 = mybir.dt.bfloat16
AF = mybir.ActivationFunctionType
ALU = mybir.AluOpType
AX = mybir.AxisListType

D = 1024
SEQ = 4096
CTX = 256
NTOK = SEQ + CTX
NTILE = NTOK // 128
L = 4
GW = 256
DFF = 4096
DIN = 2304
EPS = 1e-6
NFM = 14
WIN = NFM * 128 + 1024
NEG = -30000.0

ENGS = ("pe", "act", "dve", "pool", "sp")


class Tk:
    def __init__(self, name):
        self.name = name
        self.w = {}
        self.r = {}
        self.slots = {}
        self.dram = False


class Dk(Tk):
    def __init__(self, name):
        super().__init__(name)
        self.dram = True


class Slot:
    def __init__(self, sem):
        self.sem = sem
        self.cnt = 0


class Rec:
    def __init__(self, nc, es, nslots=44):
        self.nc = nc
        self.sem = {e: es.enter_context(nc.semaphore("sem_" + e)) for e in ENGS}
        self.n = {e: 0 for e in ENGS}
        self.seen = {e: {} for e in ENGS}
        self.prog = {e: [] for e in ENGS}
        self.pools = {q: [Slot(es.enter_context(nc.semaphore("d%s%d" % (q, i)))) for i in range(nslots)]
                      for q in ("sp", "pool")}
        self.base = {"sp": 0, "pool": 0}
        self.nxt = {"sp": 0, "pool": 0}

    def tile(self, name, dma=False):
        return Tk(name)

    def _slot(self, t, q):
        sl = t.slots.get(q)
        if sl is None:
            assert self.nxt[q] < len(self.pools[q]), "out of dma semaphores"
            sl = self.pools[q][self.nxt[q]]
            self.nxt[q] += 1
            t.slots[q] = sl
        return sl

    def freeze_global(self):
        self.base = dict(self.nxt)

    def _deps(self, reads, writes, own=None, own_t=None):
        deps = {}

        def add(d, skip=None):
            for k, (sem, v) in d.items():
                if skip is not None and k == skip:
                    continue
                if deps.get(k, (None, 0))[1] < v:
                    deps[k] = (sem, v)
        for t in reads:
            add(t.w)
        for t in writes:
            if t.dram:
                add(t.r)
            else:
                add(t.w, skip=(own if (own_t is t) else None))
                add(t.r)
        return deps

    def _filter(self, eng, deps):
        seen = self.seen[eng]
        out = []
        for k, (sem, v) in deps.items():
            if eng == "pe" and k == self.sem["pe"].num:
                continue
            if seen.get(k, 0) >= v:
                continue
            seen[k] = v
            out.append((sem, v))
        return out

    def _mark(self, ev, reads, writes):
        k = ev[0].num
        for t in reads:
            t.r[k] = ev
        for t in writes:
            if t.dram:
                t.w[k] = ev
            else:
                t.w = {k: ev}
                t.r = {}

    def op(self, eng, fn, reads=(), writes=(), inc=True):
        waits = self._filter(eng, self._deps(reads, writes))
        ev = (self.sem[eng], self.n[eng] + 1)
        if inc:
            self.n[eng] += 1
        self.prog[eng].append((waits, fn, inc, None))
        self._mark(ev, reads, writes)

    def dma(self, q, out, in_, st, reads=(), writes=()):
        sl = self._slot(st, q)
        waits = self._filter(q, self._deps(reads, writes, own=sl.sem.num, own_t=st))
        sl.cnt += 16
        ev = (sl.sem, sl.cnt)
        self.prog[q].append((waits, (lambda e: e.dma_start(out=out, in_=in_)), False, sl.sem))
        self._mark(ev, reads, writes)

    def end_phase(self, blk):
        deps = {s.sem.num: (s.sem, s.cnt) for q in self.pools for s in self.pools[q] if s.cnt > 0}
        self.prog["sp"].append((self._filter("sp", deps), None, False, None))
        names = dict(pe="tensor", act="scalar", dve="vector", pool="gpsimd", sp="sync")
        for e in ENGS:
            prog = self.prog[e]
            sem_e = self.sem[e]

            def body(eh, prog=prog, sem_e=sem_e):
                for waits, fn, inc, dsem in prog:
                    for sem, v in waits:
                        eh.wait_ge(sem, v)
                    if fn is None:
                        continue
                    ins = fn(eh)
                    if dsem is not None:
                        ins.then_inc(dsem, 16)
                    elif inc:
                        ins.then_inc(sem_e, 1)
            getattr(blk, names[e])(body)
        self.prog = {e: [] for e in ENGS}
        self.nxt = dict(self.base)


def build(nlayers=L, dbg=()):
    nc = bass.Bass("TRN2", target_bir_lowering=False)

    def din(name, shape, dt=F32):
        return nc.dram_tensor(name, list(shape), dt, kind="ExternalInput").ap()

    def dscr(name, shape, dt):
        kind = "ExternalOutput" if name in dbg else "Internal"
        return nc.dram_tensor(name, list(shape), dt, kind=kind).ap()

    xin = din("xin", [NTOK, D])
    cc = din("cc", [128, 8, 2])
    ada_w = din("ada_w", [L, D, 6 * D])
    ada_b = din("ada_b", [L, 6 * D])
    ident2_d = din("ident2", [2, 2])
    n1gc = din("n1gc", [128, L, 8])
    n2gc = din("n2gc", [128, L, 8])
    w_in_r = din("w_in_r", [L, D, WIN])
    w_out_r = din("w_out_r", [L, 10 * 128, D])
    w_ff1 = din("w_ff1", [L, D, DFF])
    w_ff2 = din("w_ff2", [L, DFF, D])
    maskT = din("maskT", [L, 128, 12800])
    sgu_wT = din("sgu_wT", [L, 128, 512])
    sgu_bc = din("sgu_bc", [128, L, 4])
    sgu_lng = din("sgu_lng", [L, GW])
    sgu_lnb = din("sgu_lnb", [L, GW])
    dl = [din("diff_l%d" % i, [L, 32]) for i in range(4)]
    subg_c = din("subg_c", [64, L])
    final_g = din("final_g", [1, D])
    identb_d = din("identb", [128, 128], BF16)
    cs_d = din("cs_tab", [128, 256], BF16)
    dft_d = din("dft", [2, 8, 128, 32 * 512], BF16)
    dft256_d = din("dft256", [128, 2 * 2 * 256], BF16)
    cos_d = din("cos_tab", [128, NTOK])
    sin_d = din("sin_tab", [128, NTOK])
    sel_d = din("sel65", [65, 64])
    ones64_d = din("ones64", [64, 64])
    onesb_d = din("onesb", [128, 128], BF16)
    laminit_d = din("laminit", [64, L])
    omli_d = din("omli", [64, L])
    out_d = nc.dram_tensor("out", [SEQ, D], F32, kind="ExternalOutput").ap()

    xres = dscr("xres", [NTOK, D], F32)
    Bd = dscr("Bd", [NTOK, 512], BF16)
    fmT = dscr("fmT", [8, 128, NTOK], BF16)
    vtok = dscr("vtok", [NTOK, 772], BF16)
    yT = dscr("yT", [10, 128, NTOK], BF16)
    gates = dscr("gates", [L, 2, 2 * D], F32)
    w_in_b = dscr("w_in_b", [L, D, WIN], BF16)
    w_out_b = dscr("w_out_b", [L, 10 * 128, D], BF16)
    w_ff1_b = dscr("w_ff1_b", [L, D, DFF], BF16)
    w_ff2_b = dscr("w_ff2_b", [L, DFF, D], BF16)
    maskT_b = dscr("maskT_b", [L, 128, 12800], BF16)
    sgu_wT_b = dscr("sgu_wT_b", [L, 128, 512], BF16)

    Dx, DB, Dfm, Dv, Dy, Dg, Dw, Dout = (Dk("x"), Dk("B"), Dk("fm"), Dk("v"), Dk("y"), Dk("g"), Dk("w"),
                                          Dk("out"))

    with contextlib.ExitStack() as es:
        R = Rec(nc, es)

        uid = [0]

        def SB(st, name, shape, dt, dma=False):
            uid[0] += 1
            nm = "s%d_%s" % (uid[0], name)
            return st.enter_context(nc.sbuf_tensor(nm, list(shape), dt)), R.tile(nm, dma)

        def PS(st, name, shape, dt=F32):
            uid[0] += 1
            nm = "p%d_%s" % (uid[0], name)
            return st.enter_context(nc.psum_tensor(nm, list(shape), dt)), R.tile(nm)

        def mm(out, lhsT, rhs, start, stop, rd, wr, inc=True, skip=False):
            R.op("pe", lambda e: e.matmul(out, lhsT=lhsT, rhs=rhs, start=start, stop=stop,
                                          skip_group_check=skip), rd, wr, inc)

        def tr(out, in_, ident, rd, wr, inc=True):
            R.op("pe", lambda e: e.transpose(out=out, in_=in_, identity=ident), rd, wr, inc)

        def act(out, in_, func, rd, wr, **kw):
            R.op("act", lambda e: e.activation(out=out, in_=in_, func=func, **kw), rd, wr)

        def tt(eng, out, in0, in1, op, rd, wr):
            R.op(eng, lambda e: e.tensor_tensor(out=out, in0=in0, in1=in1, op=op), rd, wr)

        def ts(eng, out, in0, s1, s2, op0, op1, rd, wr):
            if op1 is None:
                R.op(eng, lambda e: e.tensor_scalar(out=out, in0=in0, scalar1=s1, scalar2=None, op0=op0), rd, wr)
            else:
                R.op(eng, lambda e: e.tensor_scalar(out=out, in0=in0, scalar1=s1, scalar2=s2, op0=op0, op1=op1),
                     rd, wr)

        def stt(out, in0, scalar, in1, op0, op1, rd, wr):
            R.op("dve", lambda e: e.scalar_tensor_tensor(out=out, in0=in0, scalar=scalar, in1=in1, op0=op0,
                                                        op1=op1), rd, wr)

        def recip(out, in_, rd, wr):
            R.op("dve", lambda e: e.reciprocal(out=out, in_=in_), rd, wr)

        def cp(eng, out, in_, rd, wr):
            R.op(eng, lambda e: e.tensor_copy(out=out, in_=in_), rd, wr)

        def memset(eng, ap, val, wr):
            R.op(eng, lambda e: e.memset(ap, val), (), wr)

        def bcast(ap2d, rows):
            n = ap2d.shape[-1]
            return bass.AP(tensor=ap2d.tensor, offset=ap2d.offset, ap=[[0, rows], [1, n]])

        identb, Tid = SB(es, "identb", [128, 128], BF16, True)
        colp, Tcolp = SB(es, "colp", [128, L, 4, 8, 2], F32)
        neglam, Tnl = SB(es, "neglam", [64, L], F32)
        gsub, Tgs = SB(es, "gsub", [64, L], F32, True)
        sgub, Tsgub = SB(es, "sgub", [128, L, 4], F32, True)
        sel65, Tsel = SB(es, "sel65", [65, 64], F32, True)
        ones64, Tones = SB(es, "ones64", [64, 64], F32, True)
        onesb, Tonesb = SB(es, "onesb", [128, 128], BF16, True)
        R.freeze_global()

        def cast_weights(l):
            Twc = R.tile("wcast%d" % l, True)
            for (src, dst, rows) in ((w_in_r, w_in_b, D), (w_out_r, w_out_b, 1280), (w_ff1, w_ff1_b, D),
                                     (w_ff2, w_ff2_b, DFF), (maskT, maskT_b, 128), (sgu_wT, sgu_wT_b, 128)):
                for r0 in range(0, rows, 512):
                    r1_ = min(rows, r0 + 512)
                    R.dma("pool", dst[l, r0:r1_, :], src[l, r0:r1_, :], Twc, writes=[Dw])
                    yield None

        with contextlib.ExitStack() as ps:
            R.dma("sp", identb[:], identb_d[:, :], Tid, writes=[Tid])
            R.dma("sp", sgub[:], sgu_bc[:, :, :], Tsgub, writes=[Tsgub])
            R.dma("sp", sel65[:], sel_d[:, :], Tsel, writes=[Tsel])
            R.dma("sp", ones64[:], ones64_d[:, :], Tones, writes=[Tones])
            R.dma("sp", onesb[:], onesb_d[:, :], Tonesb, writes=[Tonesb])
            for _ in cast_weights(0):
                pass
            zt, Tzt = SB(ps, "zt", [64, NTOK], BF16, True)
            memset("pool", zt[:], 0.0, [Tzt])
            for h in range(4):
                R.dma("sp", yT[6 + h, 64:128, :], zt[:], Tzt, reads=[Tzt], writes=[Dy])
            cct, Tcc = SB(ps, "cct", [128, 8, 2], F32, True)
            sct, Tsc = SB(ps, "sct", [128, 8, 2], F32)
            g1c, Tg1c = SB(ps, "g1c", [128, L, 8], F32, True)
            g2c, Tg2c = SB(ps, "g2c", [128, L, 8], F32, True)
            ab2, Tab2 = SB(ps, "ab2", [2, 6 * D], F32, True)
            idf2, Tidf2 = SB(ps, "idf2", [2, 2], F32, True)
            rows_sb, Trows = SB(ps, "rows_sb", [2, 6 * D], F32, True)
            aw = [SB(ps, "aw%d" % i, [128, 8, D], F32, True) for i in range(2)]
            colps_f, Tcolps = PS(ps, "colps", [128, 512])
            colps = colps_f[:, 0:64].rearrange("p (w f s) -> p w f s", w=4, f=8)
            rowps = [PS(ps, "rowps%d" % i, [2, 512]) for i in range(4)]
            R.dma("sp", cct[:], cc[:, :, :], Tcc, writes=[Tcc])
            R.dma("sp", g1c[:], n1gc[:, :, :], Tg1c, writes=[Tg1c])
            R.dma("sp", g2c[:], n2gc[:, :, :], Tg2c, writes=[Tg2c])
            R.dma("sp", idf2[:], ident2_d[:, :], Tidf2, writes=[Tidf2])
            act(sct[:], cct[:], AF.Silu, [Tcc], [Tsc])
            slab_w = {0: 0, 1: 1, 3: 2, 4: 3}
            ai = 0
            for l in range(nlayers):
                R.dma("sp", ab2[:], bass.AP(tensor=ada_b.tensor, offset=l * 6 * D, ap=[[0, 2], [1, 6 * D]]), Tab2,
                      writes=[Tab2])
                awv = ada_w[l].rearrange("(k p) n -> p k n", p=128)
                for sl in range(6):
                    awt, Taw = aw[ai % 2]
                    ai += 1
                    R.dma("sp", awt[:], awv[:, :, sl * D:(sl + 1) * D], Taw, writes=[Taw])
                    for j in range(2):
                        rp, Trp = rowps[(sl % 2) * 2 + j]
                        for k in range(8):
                            mm(rp[:, :], sct[:, k, :], awt[:, k, j * 512:(j + 1) * 512], k == 0, k == 7, [Taw, Tsc],
                               [Trp], inc=(k == 7))
                        c0 = sl * D + j * 512
                        tt("dve", rows_sb[:, c0:c0 + 512], rp[:, :], ab2[:, c0:c0 + 512], ALU.add, [Trp, Tab2], [Trows])
                    if sl in slab_w:
                        w = slab_w[sl]
                        for fc in range(8):
                            tr(colps[:, w, fc, :], rows_sb[:, sl * D + fc * 128: sl * D + (fc + 1) * 128], idf2[:],
                               [Trows, Tidf2], [Tcolps], inc=(fc == 7))
                R.dma("sp", gates[l, :, 0:D], rows_sb[:, 2 * D:3 * D], Trows, reads=[Trows], writes=[Dg])
                R.dma("sp", gates[l, :, D:2 * D], rows_sb[:, 5 * D:6 * D], Trows, reads=[Trows], writes=[Dg])
                cp("dve", colp[:, l, :, :, :], colps[:, :, :, :], [Tcolps], [Tcolp])
                for s_ in range(2):
                    for (w, gt, Tg) in ((1, g1c, Tg1c), (3, g2c, Tg2c)):
                        stt(colp[:, l, w, :, s_], colp[:, l, w, :, s_], 1.0, gt[:, l, :], ALU.add, ALU.mult,
                            [Tcolp, Tg], [Tcolp])
            lq = [SB(ps, "lq%d" % i, [64, L, 32], F32, True) for i in range(4)]
            li, Tli = SB(ps, "li", [64, L], F32, True)
            om, Tom = SB(ps, "om", [64, L], F32, True)
            sgc, Tsgc = SB(ps, "sgc", [64, L], F32, True)
            pr, Tpr = SB(ps, "pr", [64, 2, L, 32], F32)
            sm, Tsm = SB(ps, "sm", [64, 2, L], F32)
            for i in range(4):
                src = bass.AP(tensor=dl[i].tensor, offset=0, ap=[[0, 64], [1, L * 32]])
                R.dma("sp", lq[i][0][:].rearrange("p l d -> p (l d)"), src, lq[i][1], writes=[lq[i][1]])
            R.dma("sp", li[:], laminit_d[:, :], Tli, writes=[Tli])
            R.dma("sp", om[:], omli_d[:, :], Tom, writes=[Tom])
            R.dma("sp", sgc[:], subg_c[:, :], Tsgc, writes=[Tsgc])
            for m in range(2):
                tt("dve", pr[:, m, :, :], lq[2 * m][0][:], lq[2 * m + 1][0][:], ALU.mult,
                   [lq[2 * m][1], lq[2 * m + 1][1]], [Tpr])
            R.op("dve", lambda e: e.tensor_reduce(out=sm[:], in_=pr[:], axis=AX.X, op=ALU.add), [Tpr], [Tsm])
            act(sm[:], sm[:], AF.Exp, [Tsm], [Tsm])
            tt("dve", neglam[:], sm[:, 1, :], sm[:, 0, :], ALU.subtract, [Tsm], [Tnl])
            tt("dve", neglam[:], neglam[:], li[:], ALU.subtract, [Tnl, Tli], [Tnl])
            tt("dve", gsub[:], sgc[:], om[:], ALU.mult, [Tsgc, Tom], [Tgs])
            with nc.Block() as blk:
                R.end_phase(blk)

        blocks = [(i * 512, 512, 0) for i in range(8)] + [(SEQ, 256, 1)]

        def norm_A1(st_tiles, xt, Txt, nt, nhalf=None):
            junk, Tjunk, ss, Tss, rstd, Trstd, xn, Txn = st_tiles
            for j in range(nt):
                act(junk[:], xt[:, j, :], AF.Square, [Txt], [Tjunk, Tss], accum_out=ss[:, j:j + 1])
            if nhalf is None:
                act(rstd[:, 0:nt], ss[:, 0:nt], AF.Sqrt, [Tss], [Trstd], scale=1.0 / D, bias=EPS)
                recip(rstd[:, 0:nt], rstd[:, 0:nt], [Trstd], [Trstd])
            else:
                ts("dve", rstd[:, 0:nt], ss[:, 0:nt], 1.0 / D, EPS, ALU.mult, ALU.add, [Tss], [Trstd])
                tt("pool", rstd[:, 0:nt], rstd[:, 0:nt], nhalf[0][:, 0:nt], ALU.pow, [Trstd, nhalf[1]], [Trstd])
            for j in range(nt):
                if j % 2 == 0:
                    act(xn[:, j, :], xt[:, j, :], AF.Copy, [Txt, Trstd], [Txn], scale=rstd[:, j:j + 1])
                else:
                    ts("dve", xn[:, j, :], xt[:, j, :], rstd[:, j:j + 1], None, ALU.mult, None, [Txt, Trstd], [Txn])

        def norm_A2(st_tiles, nt, l, wsh, wsc, s, tps, hT, ThT):
            junk, Tjunk, ss, Tss, rstd, Trstd, xn, Txn = st_tiles
            for k in range(8):
                tp, Ttp = tps[k % 2]
                for j in range(nt):
                    tr(tp[:, j * 128:(j + 1) * 128], xn[:, j, k * 128:(k + 1) * 128], identb[:], [Txn, Tid], [Ttp],
                       inc=(j == nt - 1))
                if k % 2 == 0:
                    act(hT[:, k, 0:nt * 128], tp[:, 0:nt * 128], AF.Identity, [Ttp, Tcolp], [ThT],
                        scale=colp[:, l, wsc, k, s:s + 1], bias=colp[:, l, wsh, k, s:s + 1])
                else:
                    ts("dve", hT[:, k, 0:nt * 128], tp[:, 0:nt * 128], colp[:, l, wsc, k, s:s + 1],
                       colp[:, l, wsh, k, s:s + 1], ALU.mult, ALU.add, [Ttp, Tcolp], [ThT])

        def norm_to_hT(st_tiles, xt, Txt, nt, l, wsh, wsc, s, tps, hT, ThT, tag):
            norm_A1(st_tiles, xt, Txt, nt)
            norm_A2(st_tiles, nt, l, wsh, wsc, s, tps, hT, ThT)

        for l in range(nlayers):
            last = (l == L - 1)
            xsrc = xin if l == 0 else xres
            Dxs = Dk("xin") if l == 0 else Dx

            with contextlib.ExitStack() as ps:
                wi, Twi = SB(ps, "wi", [128, 8, WIN], BF16, True)
                cst, Tcs = SB(ps, "cst", [128, 256], BF16, True)
                wsT, TwsT = SB(ps, "wsT", [128, 512], BF16, True)
                lng, Tlng = SB(ps, "lng", [128, GW], F32, True)
                lnb, Tlnb = SB(ps, "lnb", [128, GW], F32, True)
                xts = [SB(ps, "xt%d" % i, [128, 4, D], F32, True) for i in range(3)]
                junk, Tjunk = SB(ps, "junk", [128, D], BF16)
                nst = []
                for i in range(2):
                    ss_, Tss_ = SB(ps, "ss%d" % i, [128, 4], F32)
                    rstd_, Trstd_ = SB(ps, "rstd%d" % i, [128, 4], F32)
                    xn_, Txn_ = SB(ps, "xn%d" % i, [128, 4, D], BF16)
                    nst.append((junk, Tjunk, ss_, Tss_, rstd_, Trstd_, xn_, Txn_))
                hTs = [SB(ps, "hT%d" % i, [128, 8, 512], BF16) for i in range(2)]
                coss = [SB(ps, "cost%d" % i, [128, 512], F32, True) for i in range(3)]
                sins = [SB(ps, "sint%d" % i, [128, 512], F32, True) for i in range(3)]
                aT, TaT = SB(ps, "aT", [128, 2, 512], BF16)
                Bsb, TBsb = SB(ps, "Bsb", [128, 4, 512], BF16, True)
                fmo = [SB(ps, "fmo%d" % i, [128, 512], BF16, True) for i in range(3)]
                r1, Tr1 = SB(ps, "r1", [128, 512], F32)
                r2, Tr2 = SB(ps, "r2", [128, 512], F32)
                vt, Tvt = SB(ps, "vt", [128, 4, 772], BF16, True)
                bst, Tbst = SB(ps, "bst", [128, 6], F32)
                mv, Tmv = SB(ps, "mv", [128, 2], F32)
                vn, Tvn = SB(ps, "vn", [128, GW], F32)
                tps = [PS(ps, "tp%d" % i, [128, 1024], BF16) for i in range(2)]
                fps = [PS(ps, "fps%d" % i, [128, 512]) for i in range(2)]
                tms = [PS(ps, "tms%d" % i, [128, 512]) for i in range(3)]
                sgp, Tsgp = PS(ps, "sgp", [128, 1024], BF16)

                R.dma("sp", wi[:], w_in_b[l].rearrange("(k p) n -> p k n", p=128), Twi, reads=[Dw], writes=[Twi])
                R.dma("sp", cst[:], cs_d[:, :], Tcs, writes=[Tcs])
                R.dma("sp", wsT[:], sgu_wT_b[l], TwsT, reads=[Dw], writes=[TwsT])
                R.dma("sp", lng[:], bcast(sgu_lng[l:l + 1, :], 128), Tlng, writes=[Tlng])
                R.dma("sp", lnb[:], bcast(sgu_lnb[l:l + 1, :], 128), Tlnb, writes=[Tlnb])
                memset("pool", vt[:], 1.0, [Tvt])
                nhf, Tnhf = SB(ps, "nhf", [128, 4], F32)
                memset("pool", nhf[:], -0.5, [Tnhf])
                fmi = 0
                nblk = len(blocks)

                def p1_LD(bi):
                    t0, ntok, s = blocks[bi]
                    xt, Txt = xts[bi % 3]
                    R.dma("sp", xt[:, 0:ntok // 128, :], xsrc[t0:t0 + ntok, :].rearrange("(j p) d -> p j d", p=128),
                          Txt, reads=[Dxs], writes=[Txt])
                    R.dma("sp", coss[bi % 3][0][:, 0:ntok], cos_d[:, t0:t0 + ntok], coss[bi % 3][1],
                          writes=[coss[bi % 3][1]])
                    R.dma("sp", sins[bi % 3][0][:, 0:ntok], sin_d[:, t0:t0 + ntok], sins[bi % 3][1],
                          writes=[sins[bi % 3][1]])

                def p1_A1(bi):
                    t0, ntok, s = blocks[bi]
                    norm_A1(nst[bi % 2], xts[bi % 3][0], xts[bi % 3][1], ntok // 128, nhalf=(nhf, Tnhf))

                def p1_A2(bi):
                    t0, ntok, s = blocks[bi]
                    norm_A2(nst[bi % 2], ntok // 128, l, 0, 1, s, tps, hTs[bi % 2][0], hTs[bi % 2][1])

                gels = [SB(ps, "gel%d" % i, [128, 512], F32) for i in range(3)]
                vlns = [SB(ps, "vln%d" % i, [128, GW], BF16) for i in range(3)]
                ycs = [SB(ps, "yc%d" % i, [128, GW], BF16) for i in range(2)]
                ycTs = [SB(ps, "ycT%d" % i, [128, 2, 512], BF16, True) for i in range(2)]
                gtile = [0]
                sgu_ent = {}

                def p1_sgu_step(kind, ent):
                    gi, bi_, j, t0_, ntok_ = ent
                    gel, Tgel = gels[gi % 3]
                    vln, Tvln = vlns[gi % 3]
                    yc, Tyc = ycs[gi % 2]
                    ycT, TycT = ycTs[bi_ % 2]
                    if kind == "S1":
                        tm, Ttm = tms[2]
                        for g in range(4):
                            mm(tm[:, g * 64:(g + 1) * 64], wsT[:, g * 128:(g + 1) * 128], vln[:, g * 64:(g + 1) * 64],
                               True, True, [TwsT, Tvln], [Ttm], inc=(g == 3))
                        for g in range(4):
                            stt(yc[:, g * 64:(g + 1) * 64], tm[:, g * 64:(g + 1) * 64], sgub[:, l, g:g + 1],
                                gel[:, g * 64:(g + 1) * 64], ALU.add, ALU.mult, [Ttm, Tsgub, Tgel], [Tyc])
                    else:
                        for c_ in range(2):
                            tr(sgp[:, c_ * 128:(c_ + 1) * 128], yc[:, c_ * 128:(c_ + 1) * 128], identb[:], [Tyc, Tid],
                               [Tsgp], inc=(c_ == 1))
                        act(ycT[:, :, j * 128:(j + 1) * 128], sgp[:, 0:256].rearrange("p (c t) -> p c t", c=2), AF.Copy,
                            [Tsgp], [TycT])
                        if (j + 1) * 128 == ntok_:
                            R.dma("pool", yT[4:6, :, t0_:t0_ + ntok_].rearrange("c p t -> p c t"), ycT[:, :, 0:ntok_],
                                  TycT, reads=[TycT], writes=[Dy])

                p1_LD(0)
                p1_LD(1)
                p1_A1(0)
                p1_A2(0)
                for bi, (t0, ntok, s) in enumerate(blocks):
                    nt = ntok // 128
                    hT, ThT = hTs[bi % 2]
                    cost, Tcos = coss[bi % 3]
                    sint, Tsin = sins[bi % 3]
                    if bi + 2 < nblk:
                        p1_LD(bi + 2)
                    if bi + 1 < nblk:
                        p1_A1(bi + 1)
                    order = [0, 1, 2, 3, 4, 5, 8, 6, 9, 7, 12, 10, 13, 11]
                    for ci, c in enumerate(order):
                        fp, Tfp = fps[ci % 2]
                        for k in range(8):
                            mm(fp[:, 0:ntok], wi[:, k, c * 128:(c + 1) * 128], hT[:, k, 0:ntok], k == 0, k == 7,
                               [Twi, ThT], [Tfp], inc=(k == 7))
                        if c < 2:
                            act(aT[:, c, 0:ntok], fp[:, 0:ntok], AF.Copy, [Tfp], [TaT])
                        elif c < 6:
                            fo, Tfo = fmo[fmi % 3]
                            fmi += 1
                            act(fo[:, 0:ntok], fp[:, 0:ntok], AF.Copy, [Tfp], [Tfo], scale=(0.125 if c < 4 else 1.0))
                            R.dma("pool", fmT[c - 2, :, t0:t0 + ntok], fo[:, 0:ntok], Tfo, reads=[Tfo], writes=[Dfm])
                        elif c in (8, 9, 12, 13):
                            tt("dve", r1[:, 0:ntok], fp[:, 0:ntok], sint[:, 0:ntok], ALU.mult, [Tfp, Tsin], [Tr1])
                        else:
                            tt("dve", r2[:, 0:ntok], fp[:, 0:ntok], cost[:, 0:ntok], ALU.mult, [Tfp, Tcos], [Tr2])
                            fo, Tfo = fmo[fmi % 3]
                            fmi += 1
                            tt("pool", fo[:, 0:ntok], r1[:, 0:ntok], r2[:, 0:ntok], ALU.add, [Tr1, Tr2], [Tfo])
                            dst = {6: 4, 7: 5, 10: 6, 11: 7}[c]
                            R.dma("pool", fmT[dst, :, t0:t0 + ntok], fo[:, 0:ntok], Tfo, reads=[Tfo], writes=[Dfm])
                    for j in range(nt):
                        gi = gtile[0]
                        gtile[0] += 1
                        gel, Tgel = gels[gi % 3]
                        vln, Tvln = vlns[gi % 3]
                        tm, Ttm = tms[0]
                        for c_ in range(2):
                            mm(tm[:, c_ * 256:(c_ + 1) * 256], aT[:, c_, j * 128:(j + 1) * 128], cst[:, :], True, True,
                               [TaT, Tcs], [Ttm])
                        cp("dve", Bsb[:, j, :], tm[:, :], [Ttm], [TBsb])
                        tm, Ttm = tms[1]
                        for k in range(8):
                            mm(tm[:, :], hT[:, k, j * 128:(j + 1) * 128], wi[:, k, NFM * 128:NFM * 128 + 512], k == 0,
                               k == 7, [ThT, Twi], [Ttm], inc=(k == 7))
                        act(vt[:, j, 0:260].rearrange("p (h d) -> p h d", d=65)[:, :, 0:64],
                            tm[:, 0:256].rearrange("p (h d) -> p h d", d=64), AF.Copy, [Ttm], [Tvt])
                        cp("dve", vt[:, j, 260:772].rearrange("p (h d) -> p h d", d=128)[:, :, 0:64],
                           tm[:, 256:512].rearrange("p (h d) -> p h d", d=64), [Ttm], [Tvt])
                        tm, Ttm = tms[0]
                        for k in range(8):
                            mm(tm[:, :], hT[:, k, j * 128:(j + 1) * 128], wi[:, k, NFM * 128 + 512:NFM * 128 + 1024],
                               k == 0, k == 7, [ThT, Twi], [Ttm], inc=(k == 7))
                        act(gel[:], tm[:, :], AF.Gelu_apprx_tanh, [Ttm], [Tgel])
                        R.op("dve", lambda e, gel=gel: e.bn_stats(out=bst[:], in_=gel[:, 256:512]), [Tgel], [Tbst])
                        R.op("dve", lambda e: e.bn_aggr(out=mv[:], in_=bst[:]), [Tbst], [Tmv])
                        ts("dve", vn[:], gel[:, 256:512], mv[:, 0:1], None, ALU.subtract, None, [Tgel, Tmv], [Tvn])
                        ts("dve", mv[:, 1:2], mv[:, 1:2], EPS, None, ALU.add, None, [Tmv], [Tmv])
                        tt("pool", mv[:, 1:2], mv[:, 1:2], nhf[:, 0:1], ALU.pow, [Tmv, Tnhf], [Tmv])
                        stt(vn[:], vn[:], mv[:, 1:2], lng[:], ALU.mult, ALU.mult, [Tvn, Tmv, Tlng], [Tvn])
                        tt("pool", vln[:], vn[:], lnb[:], ALU.add, [Tvn, Tlnb], [Tvln])
                        sgu_ent[gi] = (gi, bi, j, t0, ntok)
                        if gi - 2 >= 0:
                            p1_sgu_step("S1", sgu_ent[gi - 2])
                        if gi - 3 >= 0:
                            p1_sgu_step("S2", sgu_ent[gi - 3])
                    if bi + 1 < nblk:
                        p1_A2(bi + 1)
                    R.dma("pool", Bd[t0:t0 + ntok, :].rearrange("(j p) c -> p j c", p=128), Bsb[:, 0:nt, :], TBsb,
                          reads=[TBsb], writes=[DB])
                    R.dma("pool", vtok[t0:t0 + ntok, :].rearrange("(j p) c -> p j c", p=128),
                          vt[:, 0:nt, :], Tvt, reads=[Tvt], writes=[Dv])
                gl_ = gtile[0] - 1
                p1_sgu_step("S1", sgu_ent[gl_ - 1])
                p1_sgu_step("S2", sgu_ent[gl_ - 2])
                p1_sgu_step("S1", sgu_ent[gl_])
                p1_sgu_step("S2", sgu_ent[gl_ - 1])
                p1_sgu_step("S2", sgu_ent[gl_])
                with nc.Block() as blk:
                    R.end_phase(blk)

            with contextlib.ExitStack() as ps:
                Ball, TBall = SB(ps, "Ball", [128, NTILE, 512], BF16, True)
                d256, Td256 = SB(ps, "d256", [128, 2, 2, 256], BF16, True)
                dts = [[SB(ps, "dft%d_%d" % (kd, i), [128, 8, 512], BF16, True) for i in range(3)] for kd in range(2)]
                yaT = [SB(ps, "yaT%d" % i, [128, 2, 512], BF16, True) for i in range(2)]
                yps = [PS(ps, "yps%d" % i, [128, 512]) for i in range(4)]
                R.dma("sp", Ball[:], Bd.rearrange("(j p) c -> p j c", p=128), TBall, reads=[DB], writes=[TBall])
                R.dma("sp", d256[:].rearrange("p a b c -> p (a b c)"), dft256_d[:, :], Td256, writes=[Td256])
                li_ = 0
                for nb in range(8):
                    ya, Tya = yaT[nb % 2]
                    for qd in range(4):
                        bufs = []
                        for kd in range(2):
                            dt_, Tdt = dts[kd][li_ % 3]
                            R.dma("sp", dt_[:].rearrange("p a b -> p (a b)"),
                                  dft_d[kd, nb, :, qd * 8 * 512:(qd + 1) * 8 * 512], Tdt, writes=[Tdt])
                            bufs.append((dt_, Tdt))
                        li_ += 1
                        for c in range(2):
                            yp, Typ = yps[(nb % 2) * 2 + c]
                            for n8 in range(8):
                                nti = qd * 8 + n8
                                for kd in range(2):
                                    dt_, Tdt = bufs[kd]
                                    lastmm = (qd == 3 and n8 == 7 and kd == 1)
                                    mm(yp[:, :], Ball[:, nti, c * 256 + kd * 128:c * 256 + (kd + 1) * 128],
                                       dt_[:, n8, :], (qd == 0 and n8 == 0 and kd == 0), lastmm, [TBall, Tdt], [Typ],
                                       inc=(lastmm or (n8 == 7 and kd == 1)))
                    for c in range(2):
                        yp, Typ = yps[(nb % 2) * 2 + c]
                        if c == 0:
                            act(ya[:, c, :], yp[:, :], AF.Copy, [Typ], [Tya])
                        else:
                            cp("dve", ya[:, c, :], yp[:, :], [Typ], [Tya])
                    R.dma("pool", yT[0:2, :, nb * 512:(nb + 1) * 512].rearrange("c p t -> p c t"), ya[:], Tya,
                          reads=[Tya], writes=[Dy])
                if not last:
                    ya, Tya = yaT[0]
                    for c in range(2):
                        yp, Typ = yps[c]
                        for n2 in range(2):
                            for kd in range(2):
                                mm(yp[:, 0:256], Ball[:, 32 + n2, c * 256 + kd * 128:c * 256 + (kd + 1) * 128],
                                   d256[:, n2, kd, :], (n2 == 0 and kd == 0), (n2 == 1 and kd == 1), [TBall, Td256],
                                   [Typ], inc=(n2 == 1 and kd == 1))
                        cp("dve", ya[:, c, 0:256], yp[:, 0:256], [Typ], [Tya])
                    R.dma("pool", yT[0:2, :, SEQ:NTOK].rearrange("c p t -> p c t"), ya[:, :, 0:256], Tya, reads=[Tya],
                          writes=[Dy])
                with nc.Block() as blk:
                    R.end_phase(blk)

            with contextlib.ExitStack() as ps:
                qT, TqT = SB(ps, "qT", [128, 2, NTOK], BF16, True)
                kTs = [SB(ps, "kTp%d" % h, [128, NTOK], BF16, True) for h in range(4)]
                vna, Tvna = SB(ps, "vna", [128, NTILE, 260], BF16, True)
                msk, Tmsk = SB(ps, "msk", [128, 12800], BF16, True)
                Es = [SB(ps, "E%d" % i, [128, 7, 128], BF16) for i in range(3)]
                rc, Trc = SB(ps, "rc", [128, 4], F32)
                ybs = [SB(ps, "yb%d" % i, [128, GW], BF16) for i in range(2)]
                ybTs = [SB(ps, "ybT%d" % i, [128, 2, 512], BF16, True) for i in range(2)]
                sABs = [PS(ps, "sAB%d" % i, [128, 1024]) for i in range(2)]
                ops_ = [PS(ps, "ops%d" % i, [128, 4, 65]) for i in range(2)]
                trps = [PS(ps, "trp%d" % i, [128, 1024], BF16) for i in range(2)]
                R.dma("sp", qT[:], fmT[0:2].rearrange("c p t -> p c t"), TqT, reads=[Dfm], writes=[TqT])
                for h in range(4):
                    r0 = (h % 2) * 64
                    memset("dve" if h % 2 == 0 else "pool", kTs[h][0][:], 0.0, [kTs[h][1]])
                    R.dma("sp", kTs[h][0][r0:r0 + 64, :], fmT[2 + h // 2, r0:r0 + 64, :], kTs[h][1], reads=[Dfm],
                          writes=[kTs[h][1]])
                R.dma("sp", vna[:], vtok[:, 0:260].rearrange("(j p) c -> p j c", p=128), Tvna, reads=[Dv],
                      writes=[Tvna])
                R.dma("sp", msk[:], maskT_b[l], Tmsk, reads=[Dw], writes=[Tmsk])
                ntq = 32 if last else 34
                items = []
                for t in range(ntq):
                    if t < 32:
                        kt0 = min(max(t - 2, 0), 27)
                        kts = [kt0 + i for i in range(5)] + [32, 33]
                        pat = {0: 0, 1: 1, 30: 3, 31: 4}.get(t, 2)
                    else:
                        kts, pat = [32, 33], None
                    for h in range(4):
                        items.append((t, h, kts, pat))

                def na_S(i):
                    t, h, kts, pat = items[i]
                    cq, b0 = h // 2, (h % 2) * 64
                    sAB, TsAB = sABs[i % 2]
                    for idx, kt in enumerate(kts):
                        dsl = sAB[:, idx * 128:(idx + 1) * 128]
                        masked = (pat is not None and idx < 5)
                        lastm = (idx == len(kts) - 1)
                        mm(dsl, kTs[h][0][:, kt * 128:(kt + 1) * 128], qT[:, cq, t * 128:(t + 1) * 128],
                           True, not masked, [kTs[h][1], TqT], [TsAB], inc=(lastm and not masked))
                        if masked:
                            m0 = ((pat * 4 + h) * 5 + idx) * 128
                            mm(dsl, identb[:], msk[:, m0:m0 + 128], False, True, [Tid, Tmsk], [TsAB], inc=lastm)

                def na_EX(i):
                    t, h, kts, pat = items[i]
                    nk = len(kts)
                    sAB, TsAB = sABs[i % 2]
                    E, TE = Es[i % 3]
                    act(E[:, 0:nk, :], sAB[:, 0:nk * 128].rearrange("p (a q) -> p a q", q=128), AF.Exp, [TsAB], [TE])

                def na_PV(i):
                    t, h, kts, pat = items[i]
                    nk = len(kts)
                    E, TE = Es[i % 3]
                    op_, Top = ops_[t % 2]
                    for idx, kt in enumerate(kts):
                        mm(op_[:, h, :], E[:, idx, :], vna[:, kt, h * 65:(h + 1) * 65], idx == 0, idx == nk - 1,
                           [TE, Tvna], [Top], inc=(idx == nk - 1))

                def na_tail1(t):
                    op_, Top = ops_[t % 2]
                    yb, Tyb = ybs[t % 2]
                    recip(rc[:], op_[:, :, 64], [Top], [Trc])
                    for h in range(4):
                        ts("dve", yb[:, h * 64:(h + 1) * 64], op_[:, h, 0:64], rc[:, h:h + 1], None, ALU.mult, None,
                           [Top, Trc], [Tyb])

                def na_tail2(t):
                    yb, Tyb = ybs[t % 2]
                    trp, Ttrp = trps[t % 2]
                    for c in range(2):
                        tr(trp[:, c * 128:(c + 1) * 128], yb[:, c * 128:(c + 1) * 128], identb[:], [Tyb, Tid], [Ttrp],
                           inc=(c == 1))

                def na_tail3(t):
                    trp, Ttrp = trps[t % 2]
                    ybT, TybT = ybTs[(t // 4) % 2]
                    j4 = t % 4
                    act(ybT[:, :, j4 * 128:(j4 + 1) * 128], trp[:, 0:256].rearrange("p (c t) -> p c t", c=2), AF.Copy,
                        [Ttrp], [TybT])
                    if j4 == 3 or t == ntq - 1:
                        tb = (t // 4) * 512
                        n_ = (j4 + 1) * 128
                        R.dma("pool", yT[2:4, :, tb:tb + n_].rearrange("c p t -> p c t"), ybT[:, :, 0:n_], TybT,
                              reads=[TybT], writes=[Dy])

                pend = []
                n_it = len(items)
                na_S(0)
                na_S(1)
                for i in range(n_it):
                    na_EX(i)
                    if i + 2 < n_it:
                        na_S(i + 2)
                    na_PV(i)
                    t, h = items[i][0], items[i][1]
                    if h == 3:
                        na_tail1(t)
                        pend.append((i + 1, na_tail2, t))
                        pend.append((i + 2, na_tail3, t))
                    keep = []
                    for (due, fn, arg) in pend:
                        if due <= i:
                            fn(arg)
                        else:
                            keep.append((due, fn, arg))
                    pend = keep
                for (due, fn, arg) in pend:
                    fn(arg)
                with nc.Block() as blk:
                    R.end_phase(blk)

            with contextlib.ExitStack() as ps:
                Qh, TQh = SB(ps, "Qc", [128, 2, NTOK], BF16, True)
                Kps = [SB(ps, "Kp%d" % v, [128, NTOK], BF16, True) for v in range(8)]
                vdf, Tvdf = SB(ps, "vdf", [128, NTILE, 512], BF16, True)
                Eb = [SB(ps, "Eb%d" % i, [128, 2, 512], BF16) for i in range(3)]
                rr = [SB(ps, "rr%d" % i, [64, 512], F32) for i in range(2)]
                dd, Tdd = SB(ps, "dd", [64, 512], F32)
                d2, Td2 = SB(ps, "d2", [64, 512], F32)
                dsq, Tdsq = SB(ps, "dsq", [64, 512], F32)
                rs_, Trs = SB(ps, "rs_", [64, 512], F32)
                ydT, TydT = SB(ps, "ydT", [64, 512], BF16, True)
                sps = [PS(ps, "sps%d" % i, [128, 2, 512]) for i in range(2)]
                ops2 = [PS(ps, "ops2_%d" % i, [128, 512]) for i in range(4)]

                def acc_of(bn_, m_):
                    return ops2[2 * (bn_ % 2) + m_]
                dhi, Tdhi = SB(ps, "dhi", [128, 512], BF16)
                dlo, Tdlo = SB(ps, "dlo", [128, 512], BF16)
                memset("pool", dhi[:], 0.0, [Tdhi])
                memset("pool", dlo[:], 0.0, [Tdlo])
                R.dma("sp", Qh[:], fmT[4:6].rearrange("c p t -> p c t"), TQh, reads=[Dfm], writes=[TQh])
                for v in range(8):
                    ch, r0 = v // 4, (v % 4) * 32
                    memset("dve" if v % 2 == 0 else "pool", Kps[v][0][:], 0.0, [Kps[v][1]])
                    R.dma("sp", Kps[v][0][r0:r0 + 32, :], fmT[6 + ch, r0:r0 + 32, :], Kps[v][1], reads=[Dfm],
                          writes=[Kps[v][1]])
                R.dma("sp", vdf[:], vtok[:, 260:772].rearrange("(j p) c -> p j c", p=128), Tvdf, reads=[Dv],
                      writes=[Tvdf])
                cast_gen = cast_weights(l + 1) if l + 1 < nlayers else iter(())
                qblocks = blocks[:8] if last else blocks
                sc_ = 32.0 ** -0.5
                steps = []
                bnum = 0
                for h in range(4):
                    for (q0, nq, s) in qblocks:
                        kts = list(range(NTILE)) if s == 0 else [32, 33]
                        for ki, kt in enumerate(kts):
                            steps.append((h, q0, nq, ki, kt, len(kts), bnum))
                        bnum += 1

                def df_S(i):
                    h, q0, nq, ki, kt, nk, bn = steps[i]
                    sp_, Tsp = sps[i % 2]
                    for m in range(2):
                        mm(sp_[:, m, 0:nq], Kps[h * 2 + m][0][:, kt * 128:(kt + 1) * 128], Qh[:, h // 2, q0:q0 + nq], True,
                           True, [Kps[h * 2 + m][1], TQh], [Tsp], inc=(m == 1))

                def df_EX(i):
                    h, q0, nq, ki, kt, nk, bn = steps[i]
                    sp_, Tsp = sps[i % 2]
                    E, TE = Eb[i % 3]
                    act(E[:, :, 0:nq], sp_[:, :, 0:nq], AF.Exp, [Tsp], [TE], scale=sc_)

                def df_PV(i):
                    h, q0, nq, ki, kt, nk, bn = steps[i]
                    E, TE = Eb[i % 3]
                    for m in range(2):
                        o2, To2 = acc_of(bn, m)
                        mm(o2[:, 0:nq], vdf[:, kt, h * 128:(h + 1) * 128], E[:, m, 0:nq], ki == 0, ki == nk - 1,
                           [Tvdf, TE], [To2], inc=True)

                def df_post1(arg):
                    h, q0, nq, bn = arg
                    a0, Ta0 = acc_of(bn, 0)
                    a1, Ta1 = acc_of(bn, 1)
                    recip(rr[0][0][:, 0:nq], a0[64:128, 0:nq], [Ta0], [rr[0][1]])
                    tt("dve", dd[:, 0:nq], a0[0:64, 0:nq], rr[0][0][:, 0:nq], ALU.mult, [Ta0, rr[0][1]], [Tdd])
                    recip(rr[1][0][:, 0:nq], a1[64:128, 0:nq], [Ta1], [rr[1][1]])
                    tt("dve", d2[:, 0:nq], a1[0:64, 0:nq], rr[1][0][:, 0:nq], ALU.mult, [Ta1, rr[1][1]], [Td2])
                    stt(dd[:, 0:nq], d2[:, 0:nq], neglam[:, l:l + 1], dd[:, 0:nq], ALU.mult, ALU.add, [Td2, Tnl, Tdd],
                        [Tdd])
                    tt("pool", dsq[:, 0:nq], dd[:, 0:nq], dd[:, 0:nq], ALU.mult, [Tdd], [Tdsq])
                    cp("dve", dhi[0:64, 0:nq], dsq[:, 0:nq], [Tdsq], [Tdhi])
                    tt("pool", dlo[0:64, 0:nq], dsq[:, 0:nq], dhi[0:64, 0:nq], ALU.subtract, [Tdsq, Tdhi], [Tdlo])

                def df_post2(arg):
                    h, q0, nq, bn = arg
                    bp, Tbp = acc_of(bn, 0)
                    mm(bp[:, 0:nq], onesb[:, :], dhi[:, 0:nq], True, False, [Tonesb, Tdhi], [Tbp], inc=False)
                    mm(bp[:, 0:nq], onesb[:, :], dlo[:, 0:nq], False, True, [Tonesb, Tdlo], [Tbp])

                def df_post3(arg):
                    h, q0, nq, bn = arg
                    bp, Tbp = acc_of(bn, 0)
                    act(rs_[:, 0:nq], bp[0:64, 0:nq], AF.Ln, [Tbp], [Trs], scale=1.0 / 64, bias=EPS)
                    act(rs_[:, 0:nq], rs_[:, 0:nq], AF.Exp, [Trs], [Trs], scale=-0.5)
                    stt(ydT[:, 0:nq], dd[:, 0:nq], gsub[:, l:l + 1], rs_[:, 0:nq], ALU.mult, ALU.mult,
                        [Tdd, Tgs, Trs], [TydT])
                    R.dma("pool", yT[6 + h, 0:64, q0:q0 + nq], ydT[:, 0:nq], TydT, reads=[TydT], writes=[Dy])

                pend = []
                n_it = len(steps)
                df_S(0)
                df_S(1)
                for i in range(n_it):
                    df_EX(i)
                    if i + 2 < n_it:
                        df_S(i + 2)
                    df_PV(i)
                    if i >= 16 and i % 16 == 0:
                        next(cast_gen, None)
                    h, q0, nq, ki, kt, nk, bn = steps[i]
                    if ki == nk - 1:
                        for (due, fn, arg) in pend:
                            fn(arg)
                        pend = []
                        df_post1((h, q0, nq, bn))
                        pend.append((i + 12, df_post2, (h, q0, nq, bn)))
                        pend.append((i + 16, df_post3, (h, q0, nq, bn)))
                    keep = []
                    for (due, fn, arg) in pend:
                        if due <= i:
                            fn(arg)
                        else:
                            keep.append((due, fn, arg))
                    pend = keep
                for (due, fn, arg) in pend:
                    fn(arg)
                for _ in cast_gen:
                    pass
                with nc.Block() as blk:
                    R.end_phase(blk)

            with contextlib.ExitStack() as ps:
                wo, Two = SB(ps, "wo", [128, 10, D], BF16, True)
                gb = [[SB(ps, "gb%d_%d" % (s, g), [128, D], F32, True) for g in range(2)] for s in range(2)]
                fgb, Tfgb = SB(ps, "fgb", [128, D], F32, True)
                xts = [SB(ps, "x3_%d" % i, [128, 4, D], F32, True) for i in range(2)]
                yTb, TyTb = SB(ps, "yTb", [128, 10, 512], BF16, True)
                junk, Tjunk = SB(ps, "junk3", [128, D], BF16)
                ss, Tss = SB(ps, "ss3", [128, 4], F32)
                rstd, Trstd = SB(ps, "rstd3", [128, 4], F32)
                xn, Txn = SB(ps, "xn3", [128, 4, D], BF16)
                hT, ThT = SB(ps, "h2T", [128, 8, 512], BF16)
                aT, TaT = SB(ps, "aT3", [128, 32, 512], BF16)
                tmpv = [SB(ps, "tmp%d" % i, [128, 512], F32) for i in range(2)]
                sqv = [SB(ps, "sq%d" % i, [128, 512], F32) for i in range(2)]
                w1b = [SB(ps, "w1b%d" % i, [128, 8, 512], BF16, True) for i in range(2)]
                w2b = [SB(ps, "w2b%d" % i, [128, 4, 512], BF16, True) for i in range(3)]
                acc = [PS(ps, "acc%d" % i, [128, 512]) for i in range(4)]
                f1p = [PS(ps, "f1p%d" % i, [128, 512]) for i in range(2)]
                tps = [PS(ps, "tp3_%d" % i, [128, 1024], BF16) for i in range(2)]
                R.dma("sp", wo[:], w_out_b[l].rearrange("(c p) n -> p c n", p=128), Two, reads=[Dw], writes=[Two])
                for s in range(2):
                    for g in range(2):
                        R.dma("sp", gb[s][g][0][:], bcast(gates[l, s:s + 1, g * D:(g + 1) * D], 128), gb[s][g][1],
                              reads=[Dg], writes=[gb[s][g][1]])
                if last:
                    R.dma("sp", fgb[:], bcast(final_g[0:1, :], 128), Tfgb, writes=[Tfgb])
                w1v = w_ff1_b[l].rearrange("(k p) f -> p k f", p=128)
                w2v = w_ff2_b[l].rearrange("(c p) d -> p c d", p=128)
                i1 = i2 = 0
                tcount = 0
                p3blocks = blocks[:8] if last else blocks

                def p3_load(bi_, q):
                    t0_, ntok_, s_ = p3blocks[bi_]
                    xt_, Txt_ = xts[bi_ % 2]
                    R.dma(q, xt_[:, 0:ntok_ // 128, :], xsrc[t0_:t0_ + ntok_, :].rearrange("(j p) d -> p j d", p=128),
                          Txt_, reads=[Dxs], writes=[Txt_])
                    R.dma(q, yTb[:, :, 0:ntok_], yT[:, :, t0_:t0_ + ntok_].rearrange("c p t -> p c t"), TyTb,
                          reads=[Dy], writes=[TyTb])

                p3_load(0, "sp")
                for bi, (t0, ntok, s) in enumerate(p3blocks):
                    nt = ntok // 128
                    xt, Txt = xts[bi % 2]
                    g1b, Tg1b = gb[s][0]
                    g2b, Tg2b = gb[s][1]
                    for j in range(nt):
                        for n in range(2):
                            ap_, Tap = acc[(j * 2 + n) % 4]
                            for c in range(10):
                                kc = 128 if c < 6 else 64
                                mm(ap_[:, :], yTb[0:kc, c, j * 128:(j + 1) * 128], wo[0:kc, c, n * 512:(n + 1) * 512],
                                   c == 0, c == 9, [TyTb, Two], [Tap], inc=(c == 9))
                            tv, Ttv = tmpv[tcount % 2]
                            tcount += 1
                            tt("dve", tv[:], ap_[:, :], g1b[:, n * 512:(n + 1) * 512], ALU.mult, [Tap, Tg1b], [Ttv])
                            tt("pool", xt[:, j, n * 512:(n + 1) * 512], xt[:, j, n * 512:(n + 1) * 512], tv[:], ALU.add,
                               [Txt, Ttv], [Txt])
                    if bi + 1 < len(p3blocks):
                        p3_load(bi + 1, "pool")
                    norm_to_hT((junk, Tjunk, ss, Tss, rstd, Trstd, xn, Txn), xt, Txt, nt, l, 2, 3, s, tps, hT, ThT, "p3")
                    for g8 in range(8):
                        w1t, Tw1 = w1b[i1 % 2]
                        i1 += 1
                        R.dma("sp", w1t[:], w1v[:, :, g8 * 512:(g8 + 1) * 512], Tw1, reads=[Dw], writes=[Tw1])
                        for c4 in range(4):
                            c = g8 * 4 + c4
                            fp, Tfp = f1p[c % 2]
                            for k in range(8):
                                mm(fp[:, 0:ntok], w1t[:, k, c4 * 128:(c4 + 1) * 128], hT[:, k, 0:ntok], k == 0, k == 7,
                                   [Tw1, ThT], [Tfp], inc=(k == 7))
                            sq, Tsq = sqv[c % 2]
                            act(sq[:, 0:ntok], fp[:, 0:ntok], AF.Square, [Tfp], [Tsq])
                            stt(aT[:, c, 0:ntok], fp[:, 0:ntok], 0.0, sq[:, 0:ntok], ALU.is_gt, ALU.mult, [Tfp, Tsq], [TaT])
                    for n in range(2):
                        for g8 in range(8):
                            w2t, Tw2 = w2b[i2 % 3]
                            i2 += 1
                            R.dma("sp", w2t[:], w2v[:, g8 * 4:(g8 + 1) * 4, n * 512:(n + 1) * 512], Tw2, reads=[Dw],
                                  writes=[Tw2])
                            for j in range(nt):
                                ap_, Tap = acc[j]
                                for c4 in range(4):
                                    c = g8 * 4 + c4
                                    mm(ap_[:, :], aT[:, c, j * 128:(j + 1) * 128], w2t[:, c4, :], (g8 == 0 and c4 == 0),
                                       (g8 == 7 and c4 == 3), [TaT, Tw2], [Tap], inc=(c4 == 3))
                        for j in range(nt):
                            ap_, Tap = acc[j]
                            tv, Ttv = tmpv[tcount % 2]
                            tcount += 1
                            tt("dve", tv[:], ap_[:, :], g2b[:, n * 512:(n + 1) * 512], ALU.mult, [Tap, Tg2b], [Ttv])
                            tt("pool", xt[:, j, n * 512:(n + 1) * 512], xt[:, j, n * 512:(n + 1) * 512], tv[:], ALU.add,
                               [Txt, Ttv], [Txt])
                    if not last:
                        R.dma("pool", xres[t0:t0 + ntok, :].rearrange("(j p) d -> p j d", p=128), xt[:, 0:nt, :], Txt,
                              reads=[Txt], writes=[Dx])
                    else:
                        for j in range(nt):
                            act(junk[:], xt[:, j, :], AF.Square, [Txt], [Tjunk, Tss], accum_out=ss[:, j:j + 1])
                        act(rstd[:, 0:nt], ss[:, 0:nt], AF.Sqrt, [Tss], [Trstd], scale=1.0 / D, bias=EPS)
                        recip(rstd[:, 0:nt], rstd[:, 0:nt], [Trstd], [Trstd])
                        for j in range(nt):
                            stt(xt[:, j, :], xt[:, j, :], rstd[:, j:j + 1], fgb[:], ALU.mult, ALU.mult,
                                [Txt, Trstd, Tfgb], [Txt])
                        R.dma("pool", out_d[t0:t0 + ntok, :].rearrange("(j p) d -> p j d", p=128), xt[:, 0:nt, :], Txt,
                              reads=[Txt], writes=[Dout])
                with nc.Block() as blk:
                    R.end_phase(blk)
    return nc


def _consts():
    bf = ml_dtypes.bfloat16
    c = {}
    c["identb"] = np.eye(128, dtype=np.float32).astype(bf)
    k = np.arange(64)
    th = 2 * np.pi * np.outer(k, k) / 64.0
    cc, sc = np.cos(th) / 8.0, np.sin(th) / 8.0
    z = np.zeros((64, 64))
    cs = np.concatenate([np.block([[cc, z], [z, cc]]), np.block([[sc, z], [z, sc]])], axis=1)
    c["cs_tab"] = cs.astype(np.float32).astype(bf)
    n = np.arange(SEQ)
    dft = np.empty((2, 8, 128, 32, 512), dtype=bf)
    for nb in range(8):
        npr = nb * 512 + np.arange(512)
        m = (np.outer(n, npr) % SEQ).astype(np.float64)
        ang = 2 * np.pi * m / SEQ
        cm = (np.cos(ang) / 64.0).astype(np.float32).reshape(32, 128, 512).transpose(1, 0, 2)
        sm = (-np.sin(ang) / 64.0).astype(np.float32).reshape(32, 128, 512).transpose(1, 0, 2)
        dft[0, nb] = cm.astype(bf)
        dft[1, nb] = sm.astype(bf)
    c["dft"] = dft.reshape(2, 8, 128, 32 * 512)
    n2 = np.arange(CTX)
    ang = 2 * np.pi * (np.outer(n2, n2) % CTX) / CTX
    d256 = np.stack([np.cos(ang) / 16.0, -np.sin(ang) / 16.0], axis=0)
    d256 = d256.reshape(2, 2, 128, 256).transpose(2, 1, 0, 3)
    c["dft256"] = np.ascontiguousarray(d256).astype(np.float32).astype(bf).reshape(128, 1024)
    t = np.arange(SEQ)
    rows, cols = (t // 64).astype(np.float32), (t % 64).astype(np.float32)
    inv = (10000.0 ** (-np.arange(8, dtype=np.float32) / 8)).astype(np.float32)
    cos_t = np.ones((128, NTOK), dtype=np.float32)
    sin_t = np.zeros((128, NTOK), dtype=np.float32)
    for p in range(128):
        i = p % 32
        a, f, hh = i // 16, i % 8, (i // 8) % 2
        ang = ((rows if a == 0 else cols) * inv[f]).astype(np.float32)
        cos_t[p, :SEQ] = np.cos(ang)
        sin_t[p, :SEQ] = np.sin(ang) * (-1.0 if hh == 0 else 1.0)
    c["cos_tab"], c["sin_t"] = cos_t, sin_t
    sel = np.zeros((65, 64), dtype=np.float32)
    sel[64, :] = 1.0
    c["sel65"] = sel
    c["ones64"] = np.ones((64, 64), dtype=np.float32)
    ob = np.zeros((128, 128), dtype=np.float32)
    ob[0:64, :] = 1.0
    c["onesb"] = ob.astype(bf)
    li = np.array([0.8 - 0.6 * math.exp(-0.3 * l) for l in range(L)], dtype=np.float32)
    c["laminit"] = np.tile(li[None, :], (64, 1)).astype(np.float32)
    c["omli"] = np.tile((1.0 - li)[None, :], (64, 1)).astype(np.float32)
    c["ident2"] = np.eye(2, dtype=np.float32)
    return c


def _mask_index():
    treps = [0, 1, 2, 30, 31]
    p = np.arange(128)[:, None]
    q = np.arange(128)[None, :]
    idx = np.zeros((128, 5, 5, 128), dtype=np.int64)
    val = np.zeros((128, 5, 5, 128), dtype=bool)
    for pi, t in enumerate(treps):
        kt0 = min(max(t - 2, 0), 27)
        for j in range(5):
            kr = 2 * (kt0 + j) + p // 64
            kc = p % 64
            r = 2 * t + q // 64
            c = q % 64
            rs = np.clip(r - 4, 0, 56)
            cs = np.clip(c - 8, 0, 48)
            ok = (kr >= rs) & (kr < rs + 8) & (kc >= cs) & (kc < cs + 16)
            dr = kr - r + 7
            dc = np.clip(kc - c, -15, 15) + 15
            idx[:, pi, j, :] = np.where(ok, dr * 31 + dc, 0)
            val[:, pi, j, :] = ok
    return idx, val


_CACHE = {}


def _prep_shared(inp):
    f = np.float32
    sh = {}
    sh["ada_w"] = np.ascontiguousarray(inp["ada_w"], dtype=f)
    sh["ada_b"] = np.ascontiguousarray(inp["ada_b"], dtype=f)
    sh["n1gc"] = np.ascontiguousarray(np.asarray(inp["norm1_g"], dtype=f).reshape(L, 8, 128).transpose(2, 0, 1))
    sh["n2gc"] = np.ascontiguousarray(np.asarray(inp["norm2_g"], dtype=f).reshape(L, 8, 128).transpose(2, 0, 1))
    w_in = np.asarray(inp["w_in"], dtype=f)
    cd = 6 * GW
    j = np.arange(256)
    cols = np.concatenate([np.arange(0, 256), np.arange(256, 512), np.arange(512, 768),
                           cd + j, cd + (j ^ 8), cd + 256 + j, cd + 256 + (j ^ 8),
                           np.arange(768, 1024), np.arange(cd + 512, cd + 768), np.arange(1024, 1536)])
    assert cols.shape[0] == WIN
    sh["w_in_r"] = np.ascontiguousarray(w_in[:, :, cols])
    w_out = np.asarray(inp["w_out"], dtype=f)
    wo = np.zeros((L, 10, 128, D), dtype=f)
    wo[:, 0:6] = w_out[:, 0:768].reshape(L, 6, 128, D)
    wo[:, 6:10, 0:64] = w_out[:, 768:1024].reshape(L, 4, 64, D)
    sh["w_out_r"] = wo.reshape(L, 1280, D)
    sh["w_ff1"] = np.ascontiguousarray(inp["w_ff1"], dtype=f)
    sh["w_ff2"] = np.ascontiguousarray(inp["w_ff2"], dtype=f)
    idx, val = _mask_index()
    rpb = np.asarray(inp["na_rpb"], dtype=f).reshape(L, 4, 15 * 31)
    mk = np.empty((L, 128, 5, 4, 5, 128), dtype=f)
    for l in range(L):
        for h in range(4):
            mk[l, :, :, h] = np.where(val, rpb[l, h][idx], f(NEG))
    sh["maskT"] = mk.reshape(L, 128, 12800)
    sw = np.asarray(inp["sgu_w"], dtype=f)
    sh["sgu_wT"] = np.ascontiguousarray(sw.transpose(0, 3, 1, 2)).reshape(L, 128, 512)
    sh["sgu_bc"] = np.ascontiguousarray(np.asarray(inp["sgu_b"], dtype=f).transpose(2, 0, 1))
    sh["sgu_lng"] = np.ascontiguousarray(inp["sgu_ln_g"], dtype=f)
    sh["sgu_lnb"] = np.ascontiguousarray(inp["sgu_ln_b"], dtype=f)
    for i, nm in enumerate(("diff_lq1", "diff_lk1", "diff_lq2", "diff_lk2")):
        sh["diff_l%d" % i] = np.ascontiguousarray(inp[nm], dtype=f)
    sh["subg_c"] = np.ascontiguousarray(np.asarray(inp["diff_subln_g"], dtype=f).T)
    sh["final_g"] = np.asarray(inp["final_g"], dtype=f).reshape(1, D)
    return sh


def make_in_maps(inp):
    if "consts" not in _CACHE:
        c = _consts()
        c["sin_tab"] = c.pop("sin_t")
        _CACHE["consts"] = c
    sh = _prep_shared(inp)
    sh.update(_CACHE["consts"])
    x = np.asarray(inp["x"], dtype=np.float32)
    ctx = np.asarray(inp["ctx"], dtype=np.float32)
    c = np.asarray(inp["c"], dtype=np.float32)
    c_ctx = np.asarray(inp["c_ctx"], dtype=np.float32)
    maps = []
    for b in range(8):
        m = dict(sh)
        m["xin"] = np.concatenate([x[b], ctx[b]], axis=0)
        cc = np.stack([c[b], c_ctx], axis=-1).reshape(8, 128, 2).transpose(1, 0, 2)
        m["cc"] = np.ascontiguousarray(cc)
        maps.append(m)
    return maps


def kernel(**inputs):
    if "nc" not in _CACHE:
        _CACHE["nc"] = build()
    nc = _CACHE["nc"]
    maps = make_in_maps(inputs)
    res = run_bass_kernel_spmd(nc, maps, core_ids=list(range(8)))
    return np.stack([np.asarray(r["out"], dtype=np.float32) for r in res.results], axis=0)
```

```python
import contextlib
import itertools
import math
import numpy as np
import ml_dtypes
import concourse.bass as bass
import concourse.mybir as mybir
from concourse.bass_utils import run_bass_kernel_spmd

F32 = mybir.dt.float32
BF16 = mybir.dt.bfloat16
AF = mybir.ActivationFunctionType
ALU = mybir.AluOpType
AX = mybir.AxisListType

D = 1024
SEQ = 4096
CTX = 256
NTOK = SEQ + CTX
NTILE = NTOK // 128
L = 4
GW = 256
DFF = 4096
DIN = 2304
EPS = 1e-6
NFM = 14
WIN = NFM * 128 + 1024
NEG = -30000.0

ENGS = ("pe", "act", "dve", "pool", "sp")


class Tk:
    def __init__(self, name):
        self.name = name
        self.w = {}
        self.r = {}
        self.slots = {}
        self.dram = False


class Dk(Tk):
    def __init__(self, name):
        super().__init__(name)
        self.dram = True


class Slot:
    def __init__(self, sem):
        self.sem = sem
        self.cnt = 0


class Rec:
    def __init__(self, nc, es, nslots=44):
        self.nc = nc
        self.sem = {e: es.enter_context(nc.semaphore("sem_" + e)) for e in ENGS}
        self.n = {e: 0 for e in ENGS}
        self.seen = {e: {} for e in ENGS}
        self.prog = {e: [] for e in ENGS}
        self.pools = {q: [Slot(es.enter_context(nc.semaphore("d%s%d" % (q, i)))) for i in range(nslots)]
                      for q in ("sp", "pool")}
        self.base = {"sp": 0, "pool": 0}
        self.nxt = {"sp": 0, "pool": 0}

    def tile(self, name, dma=False):
        return Tk(name)

    def _slot(self, t, q):
        sl = t.slots.get(q)
        if sl is None:
            assert self.nxt[q] < len(self.pools[q]), "out of dma semaphores"
            sl = self.pools[q][self.nxt[q]]
            self.nxt[q] += 1
            t.slots[q] = sl
        return sl

    def freeze_global(self):
        self.base = dict(self.nxt)

    def _deps(self, reads, writes, own=None, own_t=None):
        deps = {}

        def add(d, skip=None):
            for k, (sem, v) in d.items():
                if skip is not None and k == skip:
                    continue
                if deps.get(k, (None, 0))[1] < v:
                    deps[k] = (sem, v)
        for t in reads:
            add(t.w)
        for t in writes:
            if t.dram:
                add(t.r)
            else:
                add(t.w, skip=(own if (own_t is t) else None))
                add(t.r)
        return deps

    def _filter(self, eng, deps):
        seen = self.seen[eng]
        out = []
        for k, (sem, v) in deps.items():
            if eng == "pe" and k == self.sem["pe"].num:
                continue
            if seen.get(k, 0) >= v:
                continue
            seen[k] = v
            out.append((sem, v))
        return out

    def _mark(self, ev, reads, writes):
        k = ev[0].num
        for t in reads:
            t.r[k] = ev
        for t in writes:
            if t.dram:
                t.w[k] = ev
            else:
                t.w = {k: ev}
                t.r = {}

    def op(self, eng, fn, reads=(), writes=(), inc=True):
        waits = self._filter(eng, self._deps(reads, writes))
        ev = (self.sem[eng], self.n[eng] + 1)
        if inc:
            self.n[eng] += 1
        self.prog[eng].append((waits, fn, inc, None))
        self._mark(ev, reads, writes)

    def dma(self, q, out, in_, st, reads=(), writes=()):
        sl = self._slot(st, q)
        waits = self._filter(q, self._deps(reads, writes, own=sl.sem.num, own_t=st))
        sl.cnt += 16
        ev = (sl.sem, sl.cnt)
        self.prog[q].append((waits, (lambda e: e.dma_start(out=out, in_=in_)), False, sl.sem))
        self._mark(ev, reads, writes)

    def end_phase(self, blk):
        deps = {s.sem.num: (s.sem, s.cnt) for q in self.pools for s in self.pools[q] if s.cnt > 0}
        self.prog["sp"].append((self._filter("sp", deps), None, False, None))
        names = dict(pe="tensor", act="scalar", dve="vector", pool="gpsimd", sp="sync")
        for e in ENGS:
            prog = self.prog[e]
            sem_e = self.sem[e]

            def body(eh, prog=prog, sem_e=sem_e):
                for waits, fn, inc, dsem in prog:
                    for sem, v in waits:
                        eh.wait_ge(sem, v)
                    if fn is None:
                        continue
                    ins = fn(eh)
                    if dsem is not None:
                        ins.then_inc(dsem, 16)
                    elif inc:
                        ins.then_inc(sem_e, 1)
            getattr(blk, names[e])(body)
        self.prog = {e: [] for e in ENGS}
        self.nxt = dict(self.base)


def build(nlayers=L, dbg=()):
    nc = bass.Bass("TRN2", target_bir_lowering=False)

    def din(name, shape, dt=F32):
        return nc.dram_tensor(name, list(shape), dt, kind="ExternalInput").ap()

    def dscr(name, shape, dt):
        kind = "ExternalOutput" if name in dbg else "Internal"
        return nc.dram_tensor(name, list(shape), dt, kind=kind).ap()

    xin = din("xin", [NTOK, D])
    cc = din("cc", [128, 8, 2])
    ada_w = din("ada_w", [L, D, 6 * D])
    ada_b = din("ada_b", [L, 6 * D])
    ident2_d = din("ident2", [2, 2])
    n1gc = din("n1gc", [128, L, 8])
    n2gc = din("n2gc", [128, L, 8])
    w_in_r = din("w_in_r", [L, D, WIN])
    w_out_r = din("w_out_r", [L, 10 * 128, D])
    w_ff1 = din("w_ff1", [L, D, DFF])
    w_ff2 = din("w_ff2", [L, DFF, D])
    maskT = din("maskT", [L, 128, 12800])
    sgu_wT = din("sgu_wT", [L, 128, 512])
    sgu_bc = din("sgu_bc", [128, L, 4])
    sgu_lng = din("sgu_lng", [L, GW])
    sgu_lnb = din("sgu_lnb", [L, GW])
    dl = [din("diff_l%d" % i, [L, 32]) for i in range(4)]
    subg_c = din("subg_c", [64, L])
    final_g = din("final_g", [1, D])
    identb_d = din("identb", [128, 128], BF16)
    cs_d = din("cs_tab", [128, 256], BF16)
    dft_d = din("dft", [2, 8, 128, 32 * 512], BF16)
    dft256_d = din("dft256", [128, 2 * 2 * 256], BF16)
    cos_d = din("cos_tab", [128, NTOK])
    sin_d = din("sin_tab", [128, NTOK])
    sel_d = din("sel65", [65, 64])
    ones64_d = din("ones64", [64, 64])
    onesb_d = din("onesb", [128, 128], BF16)
    laminit_d = din("laminit", [64, L])
    omli_d = din("omli", [64, L])
    out_d = nc.dram_tensor("out", [SEQ, D], F32, kind="ExternalOutput").ap()

    xres = dscr("xres", [NTOK, D], F32)
    Bd = dscr("Bd", [NTOK, 512], BF16)
    fmT = dscr("fmT", [8, 128, NTOK], BF16)
    vtok = dscr("vtok", [NTOK, 772], BF16)
    yT = dscr("yT", [10, 128, NTOK], BF16)
    gates = dscr("gates", [L, 2, 2 * D], F32)
    w_in_b = dscr("w_in_b", [L, D, WIN], BF16)
    w_out_b = dscr("w_out_b", [L, 10 * 128, D], BF16)
    w_ff1_b = dscr("w_ff1_b", [L, D, DFF], BF16)
    w_ff2_b = dscr("w_ff2_b", [L, DFF, D], BF16)
    maskT_b = dscr("maskT_b", [L, 128, 12800], BF16)
    sgu_wT_b = dscr("sgu_wT_b", [L, 128, 512], BF16)

    Dx, DB, Dfm, Dv, Dy, Dg, Dw, Dout = (Dk("x"), Dk("B"), Dk("fm"), Dk("v"), Dk("y"), Dk("g"), Dk("w"),
                                          Dk("out"))

    with contextlib.ExitStack() as es:
        R = Rec(nc, es)

        uid = [0]

        def SB(st, name, shape, dt, dma=False):
            uid[0] += 1
            nm = "s%d_%s" % (uid[0], name)
            return st.enter_context(nc.sbuf_tensor(nm, list(shape), dt)), R.tile(nm, dma)

        def PS(st, name, shape, dt=F32):
            uid[0] += 1
            nm = "p%d_%s" % (uid[0], name)
            return st.enter_context(nc.psum_tensor(nm, list(shape), dt)), R.tile(nm)

        def mm(out, lhsT, rhs, start, stop, rd, wr, inc=True, skip=False):
            R.op("pe", lambda e: e.matmul(out, lhsT=lhsT, rhs=rhs, start=start, stop=stop,
                                          skip_group_check=skip), rd, wr, inc)

        def tr(out, in_, ident, rd, wr, inc=True):
            R.op("pe", lambda e: e.transpose(out=out, in_=in_, identity=ident), rd, wr, inc)

        def act(out, in_, func, rd, wr, **kw):
            R.op("act", lambda e: e.activation(out=out, in_=in_, func=func, **kw), rd, wr)

        def tt(eng, out, in0, in1, op, rd, wr):
            R.op(eng, lambda e: e.tensor_tensor(out=out, in0=in0, in1=in1, op=op), rd, wr)

        def ts(eng, out, in0, s1, s2, op0, op1, rd, wr):
            if op1 is None:
                R.op(eng, lambda e: e.tensor_scalar(out=out, in0=in0, scalar1=s1, scalar2=None, op0=op0), rd, wr)
            else:
                R.op(eng, lambda e: e.tensor_scalar(out=out, in0=in0, scalar1=s1, scalar2=s2, op0=op0, op1=op1),
                     rd, wr)

        def stt(out, in0, scalar, in1, op0, op1, rd, wr):
            R.op("dve", lambda e: e.scalar_tensor_tensor(out=out, in0=in0, scalar=scalar, in1=in1, op0=op0,
                                                        op1=op1), rd, wr)

        def recip(out, in_, rd, wr):
            R.op("dve", lambda e: e.reciprocal(out=out, in_=in_), rd, wr)

        def cp(eng, out, in_, rd, wr):
            R.op(eng, lambda e: e.tensor_copy(out=out, in_=in_), rd, wr)

        def memset(eng, ap, val, wr):
            R.op(eng, lambda e: e.memset(ap, val), (), wr)

        def bcast(ap2d, rows):
            n = ap2d.shape[-1]
            return bass.AP(tensor=ap2d.tensor, offset=ap2d.offset, ap=[[0, rows], [1, n]])

        identb, Tid = SB(es, "identb", [128, 128], BF16, True)
        colp, Tcolp = SB(es, "colp", [128, L, 4, 8, 2], F32)
        neglam, Tnl = SB(es, "neglam", [64, L], F32)
        gsub, Tgs = SB(es, "gsub", [64, L], F32, True)
        sgub, Tsgub = SB(es, "sgub", [128, L, 4], F32, True)
        sel65, Tsel = SB(es, "sel65", [65, 64], F32, True)
        ones64, Tones = SB(es, "ones64", [64, 64], F32, True)
        onesb, Tonesb = SB(es, "onesb", [128, 128], BF16, True)
        R.freeze_global()

        def cast_weights(l, part="all"):
            Twc = R.tile("wcast%d" % l, True)
            lst = ((w_in_r, w_in_b, D, "early"), (w_out_r, w_out_b, 1280, "early"), (w_ff1, w_ff1_b, D, "late"),
                   (w_ff2, w_ff2_b, DFF, "late"), (maskT, maskT_b, 128, "early"), (sgu_wT, sgu_wT_b, 128, "early"))
            for (src, dst, rows, kind_) in lst:
                if part != "all" and kind_ != part:
                    continue
                for r0 in range(0, rows, 512):
                    r1_ = min(rows, r0 + 512)
                    R.dma("pool", dst[l, r0:r1_, :], src[l, r0:r1_, :], Twc, writes=[Dw])
                    yield None

        with contextlib.ExitStack() as ps:
            R.dma("sp", identb[:], identb_d[:, :], Tid, writes=[Tid])
            R.dma("sp", sgub[:], sgu_bc[:, :, :], Tsgub, writes=[Tsgub])
            R.dma("sp", sel65[:], sel_d[:, :], Tsel, writes=[Tsel])
            R.dma("sp", ones64[:], ones64_d[:, :], Tones, writes=[Tones])
            R.dma("sp", onesb[:], onesb_d[:, :], Tonesb, writes=[Tonesb])
            for _ in cast_weights(0, "early"):
                pass
            zt, Tzt = SB(ps, "zt", [64, NTOK], BF16, True)
            memset("pool", zt[:], 0.0, [Tzt])
            for h in range(4):
                R.dma("sp", yT[6 + h, 64:128, :], zt[:], Tzt, reads=[Tzt], writes=[Dy])
            cct, Tcc = SB(ps, "cct", [128, 8, 2], F32, True)
            sct, Tsc = SB(ps, "sct", [128, 8, 2], F32)
            g1c, Tg1c = SB(ps, "g1c", [128, L, 8], F32, True)
            g2c, Tg2c = SB(ps, "g2c", [128, L, 8], F32, True)
            ab2, Tab2 = SB(ps, "ab2", [2, 6 * D], F32, True)
            idf2, Tidf2 = SB(ps, "idf2", [2, 2], F32, True)
            rows_sb, Trows = SB(ps, "rows_sb", [2, 6 * D], F32, True)
            aw = [SB(ps, "aw%d" % i, [128, 8, D], F32, True) for i in range(2)]
            colps_f, Tcolps = PS(ps, "colps", [128, 512])
            colps = colps_f[:, 0:64].rearrange("p (w f s) -> p w f s", w=4, f=8)
            rowps = [PS(ps, "rowps%d" % i, [2, 512]) for i in range(4)]
            R.dma("sp", cct[:], cc[:, :, :], Tcc, writes=[Tcc])
            R.dma("sp", g1c[:], n1gc[:, :, :], Tg1c, writes=[Tg1c])
            R.dma("sp", g2c[:], n2gc[:, :, :], Tg2c, writes=[Tg2c])
            R.dma("sp", idf2[:], ident2_d[:, :], Tidf2, writes=[Tidf2])
            act(sct[:], cct[:], AF.Silu, [Tcc], [Tsc])
            slab_w = {0: 0, 1: 1, 3: 2, 4: 3}
            ai = 0
            for l in range(nlayers):
                R.dma("sp", ab2[:], bass.AP(tensor=ada_b.tensor, offset=l * 6 * D, ap=[[0, 2], [1, 6 * D]]), Tab2,
                      writes=[Tab2])
                awv = ada_w[l].rearrange("(k p) n -> p k n", p=128)
                for sl in range(6):
                    awt, Taw = aw[ai % 2]
                    ai += 1
                    R.dma("sp", awt[:], awv[:, :, sl * D:(sl + 1) * D], Taw, writes=[Taw])
                    for j in range(2):
                        rp, Trp = rowps[(sl % 2) * 2 + j]
                        for k in range(8):
                            mm(rp[:, :], sct[:, k, :], awt[:, k, j * 512:(j + 1) * 512], k == 0, k == 7, [Taw, Tsc],
                               [Trp], inc=(k == 7))
                        c0 = sl * D + j * 512
                        tt("dve", rows_sb[:, c0:c0 + 512], rp[:, :], ab2[:, c0:c0 + 512], ALU.add, [Trp, Tab2], [Trows])
                    if sl in slab_w:
                        w = slab_w[sl]
                        for fc in range(8):
                            tr(colps[:, w, fc, :], rows_sb[:, sl * D + fc * 128: sl * D + (fc + 1) * 128], idf2[:],
                               [Trows, Tidf2], [Tcolps], inc=(fc == 7))
                R.dma("sp", gates[l, :, 0:D], rows_sb[:, 2 * D:3 * D], Trows, reads=[Trows], writes=[Dg])
                R.dma("sp", gates[l, :, D:2 * D], rows_sb[:, 5 * D:6 * D], Trows, reads=[Trows], writes=[Dg])
                cp("dve", colp[:, l, :, :, :], colps[:, :, :, :], [Tcolps], [Tcolp])
                for s_ in range(2):
                    for (w, gt, Tg) in ((1, g1c, Tg1c), (3, g2c, Tg2c)):
                        stt(colp[:, l, w, :, s_], colp[:, l, w, :, s_], 1.0, gt[:, l, :], ALU.add, ALU.mult,
                            [Tcolp, Tg], [Tcolp])
            lq = [SB(ps, "lq%d" % i, [64, L, 32], F32, True) for i in range(4)]
            li, Tli = SB(ps, "li", [64, L], F32, True)
            om, Tom = SB(ps, "om", [64, L], F32, True)
            sgc, Tsgc = SB(ps, "sgc", [64, L], F32, True)
            pr, Tpr = SB(ps, "pr", [64, 2, L, 32], F32)
            sm, Tsm = SB(ps, "sm", [64, 2, L], F32)
            for i in range(4):
                src = bass.AP(tensor=dl[i].tensor, offset=0, ap=[[0, 64], [1, L * 32]])
                R.dma("sp", lq[i][0][:].rearrange("p l d -> p (l d)"), src, lq[i][1], writes=[lq[i][1]])
            R.dma("sp", li[:], laminit_d[:, :], Tli, writes=[Tli])
            R.dma("sp", om[:], omli_d[:, :], Tom, writes=[Tom])
            R.dma("sp", sgc[:], subg_c[:, :], Tsgc, writes=[Tsgc])
            for m in range(2):
                tt("dve", pr[:, m, :, :], lq[2 * m][0][:], lq[2 * m + 1][0][:], ALU.mult,
                   [lq[2 * m][1], lq[2 * m + 1][1]], [Tpr])
            R.op("dve", lambda e: e.tensor_reduce(out=sm[:], in_=pr[:], axis=AX.X, op=ALU.add), [Tpr], [Tsm])
            act(sm[:], sm[:], AF.Exp, [Tsm], [Tsm])
            tt("dve", neglam[:], sm[:, 1, :], sm[:, 0, :], ALU.subtract, [Tsm], [Tnl])
            tt("dve", neglam[:], neglam[:], li[:], ALU.subtract, [Tnl, Tli], [Tnl])
            tt("dve", gsub[:], sgc[:], om[:], ALU.mult, [Tsgc, Tom], [Tgs])
            with nc.Block() as blk:
                R.end_phase(blk)

        blocks = [(i * 512, 512, 0) for i in range(8)] + [(SEQ, 256, 1)]

        def norm_A1(st_tiles, xt, Txt, nt, nhalf=None):
            junk, Tjunk, ss, Tss, rstd, Trstd, xn, Txn = st_tiles
            for j in range(nt):
                act(junk[:], xt[:, j, :], AF.Square, [Txt], [Tjunk, Tss], accum_out=ss[:, j:j + 1])
            if nhalf is None:
                act(rstd[:, 0:nt], ss[:, 0:nt], AF.Sqrt, [Tss], [Trstd], scale=1.0 / D, bias=EPS)
                recip(rstd[:, 0:nt], rstd[:, 0:nt], [Trstd], [Trstd])
            else:
                ts("dve", rstd[:, 0:nt], ss[:, 0:nt], 1.0 / D, EPS, ALU.mult, ALU.add, [Tss], [Trstd])
                tt("pool", rstd[:, 0:nt], rstd[:, 0:nt], nhalf[0][:, 0:nt], ALU.pow, [Trstd, nhalf[1]], [Trstd])
            for j in range(nt):
                if j % 2 == 0:
                    act(xn[:, j, :], xt[:, j, :], AF.Copy, [Txt, Trstd], [Txn], scale=rstd[:, j:j + 1])
                else:
                    ts("dve", xn[:, j, :], xt[:, j, :], rstd[:, j:j + 1], None, ALU.mult, None, [Txt, Trstd], [Txn])

        def norm_A2(st_tiles, nt, l, wsh, wsc, s, tps, hT, ThT):
            junk, Tjunk, ss, Tss, rstd, Trstd, xn, Txn = st_tiles
            for k in range(8):
                tp, Ttp = tps[k % 2]
                for j in range(nt):
                    tr(tp[:, j * 128:(j + 1) * 128], xn[:, j, k * 128:(k + 1) * 128], identb[:], [Txn, Tid], [Ttp],
                       inc=(j == nt - 1))
                if k % 2 == 0:
                    act(hT[:, k, 0:nt * 128], tp[:, 0:nt * 128], AF.Identity, [Ttp, Tcolp], [ThT],
                        scale=colp[:, l, wsc, k, s:s + 1], bias=colp[:, l, wsh, k, s:s + 1])
                else:
                    ts("dve", hT[:, k, 0:nt * 128], tp[:, 0:nt * 128], colp[:, l, wsc, k, s:s + 1],
                       colp[:, l, wsh, k, s:s + 1], ALU.mult, ALU.add, [Ttp, Tcolp], [ThT])

        def norm_to_hT(st_tiles, xt, Txt, nt, l, wsh, wsc, s, tps, hT, ThT, tag):
            norm_A1(st_tiles, xt, Txt, nt)
            norm_A2(st_tiles, nt, l, wsh, wsc, s, tps, hT, ThT)

        for l in range(nlayers):
            last = (l == L - 1)
            xsrc = xin if l == 0 else xres
            Dxs = Dk("xin") if l == 0 else Dx

            with contextlib.ExitStack() as ps:
                wi, Twi = SB(ps, "wi", [128, 8, WIN], BF16, True)
                cst, Tcs = SB(ps, "cst", [128, 256], BF16, True)
                wsT, TwsT = SB(ps, "wsT", [128, 512], BF16, True)
                lng, Tlng = SB(ps, "lng", [128, GW], F32, True)
                lnb, Tlnb = SB(ps, "lnb", [128, GW], F32, True)
                xts = [SB(ps, "xt%d" % i, [128, 4, D], F32, True) for i in range(3)]
                junk, Tjunk = SB(ps, "junk", [128, D], BF16)
                nst = []
                for i in range(2):
                    ss_, Tss_ = SB(ps, "ss%d" % i, [128, 4], F32)
                    rstd_, Trstd_ = SB(ps, "rstd%d" % i, [128, 4], F32)
                    xn_, Txn_ = SB(ps, "xn%d" % i, [128, 4, D], BF16)
                    nst.append((junk, Tjunk, ss_, Tss_, rstd_, Trstd_, xn_, Txn_))
                hTs = [SB(ps, "hT%d" % i, [128, 8, 512], BF16) for i in range(2)]
                coss = [SB(ps, "cost%d" % i, [128, 512], F32, True) for i in range(3)]
                sins = [SB(ps, "sint%d" % i, [128, 512], F32, True) for i in range(3)]
                aT, TaT = SB(ps, "aT", [128, 2, 512], BF16)
                Bsb, TBsb = SB(ps, "Bsb", [128, 4, 512], BF16, True)
                fmo = [SB(ps, "fmo%d" % i, [128, 512], BF16, True) for i in range(3)]
                r1, Tr1 = SB(ps, "r1", [128, 512], F32)
                r2, Tr2 = SB(ps, "r2", [128, 512], F32)
                vt, Tvt = SB(ps, "vt", [128, 4, 772], BF16, True)
                bst, Tbst = SB(ps, "bst", [128, 6], F32)
                mv, Tmv = SB(ps, "mv", [128, 2], F32)
                vn, Tvn = SB(ps, "vn", [128, GW], F32)
                tps = [PS(ps, "tp%d" % i, [128, 1024], BF16) for i in range(2)]
                fps = [PS(ps, "fps%d" % i, [128, 512]) for i in range(2)]
                tms = [PS(ps, "tms%d" % i, [128, 512]) for i in range(3)]
                sgp, Tsgp = PS(ps, "sgp", [128, 1024], BF16)

                R.dma("sp", wi[:], w_in_b[l].rearrange("(k p) n -> p k n", p=128), Twi, reads=[Dw], writes=[Twi])
                R.dma("sp", cst[:], cs_d[:, :], Tcs, writes=[Tcs])
                R.dma("sp", wsT[:], sgu_wT_b[l], TwsT, reads=[Dw], writes=[TwsT])
                R.dma("sp", lng[:], bcast(sgu_lng[l:l + 1, :], 128), Tlng, writes=[Tlng])
                R.dma("sp", lnb[:], bcast(sgu_lnb[l:l + 1, :], 128), Tlnb, writes=[Tlnb])
                memset("pool", vt[:], 1.0, [Tvt])
                nhf, Tnhf = SB(ps, "nhf", [128, 4], F32)
                memset("pool", nhf[:], -0.5, [Tnhf])
                fmi = 0
                nblk = len(blocks)

                def p1_LD(bi):
                    t0, ntok, s = blocks[bi]
                    xt, Txt = xts[bi % 3]
                    R.dma("sp", xt[:, 0:ntok // 128, :], xsrc[t0:t0 + ntok, :].rearrange("(j p) d -> p j d", p=128),
                          Txt, reads=[Dxs], writes=[Txt])
                    R.dma("sp", coss[bi % 3][0][:, 0:ntok], cos_d[:, t0:t0 + ntok], coss[bi % 3][1],
                          writes=[coss[bi % 3][1]])
                    R.dma("sp", sins[bi % 3][0][:, 0:ntok], sin_d[:, t0:t0 + ntok], sins[bi % 3][1],
                          writes=[sins[bi % 3][1]])

                def p1_A1(bi):
                    t0, ntok, s = blocks[bi]
                    norm_A1(nst[bi % 2], xts[bi % 3][0], xts[bi % 3][1], ntok // 128, nhalf=(nhf, Tnhf))

                def p1_A2(bi):
                    t0, ntok, s = blocks[bi]
                    norm_A2(nst[bi % 2], ntok // 128, l, 0, 1, s, tps, hTs[bi % 2][0], hTs[bi % 2][1])

                gels = [SB(ps, "gel%d" % i, [128, 512], F32) for i in range(3)]
                vlns = [SB(ps, "vln%d" % i, [128, GW], BF16) for i in range(3)]
                ycs = [SB(ps, "yc%d" % i, [128, GW], BF16) for i in range(2)]
                ycTs = [SB(ps, "ycT%d" % i, [128, 2, 512], BF16, True) for i in range(2)]
                gtile = [0]
                sgu_ent = {}

                def p1_sgu_step(kind, ent):
                    gi, bi_, j, t0_, ntok_ = ent
                    gel, Tgel = gels[gi % 3]
                    vln, Tvln = vlns[gi % 3]
                    yc, Tyc = ycs[gi % 2]
                    ycT, TycT = ycTs[bi_ % 2]
                    if kind == "S1":
                        tm, Ttm = tms[2]
                        for g in range(4):
                            mm(tm[:, g * 64:(g + 1) * 64], wsT[:, g * 128:(g + 1) * 128], vln[:, g * 64:(g + 1) * 64],
                               True, True, [TwsT, Tvln], [Ttm], inc=(g == 3))
                        for g in range(4):
                            stt(yc[:, g * 64:(g + 1) * 64], tm[:, g * 64:(g + 1) * 64], sgub[:, l, g:g + 1],
                                gel[:, g * 64:(g + 1) * 64], ALU.add, ALU.mult, [Ttm, Tsgub, Tgel], [Tyc])
                    else:
                        for c_ in range(2):
                            tr(sgp[:, c_ * 128:(c_ + 1) * 128], yc[:, c_ * 128:(c_ + 1) * 128], identb[:], [Tyc, Tid],
                               [Tsgp], inc=(c_ == 1))
                        act(ycT[:, :, j * 128:(j + 1) * 128], sgp[:, 0:256].rearrange("p (c t) -> p c t", c=2), AF.Copy,
                            [Tsgp], [TycT])
                        if (j + 1) * 128 == ntok_:
                            R.dma("pool", yT[4:6, :, t0_:t0_ + ntok_].rearrange("c p t -> p c t"), ycT[:, :, 0:ntok_],
                                  TycT, reads=[TycT], writes=[Dy])

                p1_LD(0)
                p1_LD(1)
                p1_A1(0)
                p1_A2(0)
                for bi, (t0, ntok, s) in enumerate(blocks):
                    nt = ntok // 128
                    hT, ThT = hTs[bi % 2]
                    cost, Tcos = coss[bi % 3]
                    sint, Tsin = sins[bi % 3]
                    if bi + 2 < nblk:
                        p1_LD(bi + 2)
                    if bi + 1 < nblk:
                        p1_A1(bi + 1)
                    order = [0, 1, 2, 3, 4, 5, 8, 6, 9, 7, 12, 10, 13, 11]
                    for ci, c in enumerate(order):
                        fp, Tfp = fps[ci % 2]
                        for k in range(8):
                            mm(fp[:, 0:ntok], wi[:, k, c * 128:(c + 1) * 128], hT[:, k, 0:ntok], k == 0, k == 7,
                               [Twi, ThT], [Tfp], inc=(k == 7))
                        if c < 2:
                            act(aT[:, c, 0:ntok], fp[:, 0:ntok], AF.Copy, [Tfp], [TaT])
                        elif c < 6:
                            fo, Tfo = fmo[fmi % 3]
                            fmi += 1
                            act(fo[:, 0:ntok], fp[:, 0:ntok], AF.Copy, [Tfp], [Tfo], scale=(0.125 if c < 4 else 1.0))
                            R.dma("pool", fmT[c - 2, :, t0:t0 + ntok], fo[:, 0:ntok], Tfo, reads=[Tfo], writes=[Dfm])
                        elif c in (8, 9, 12, 13):
                            tt("dve", r1[:, 0:ntok], fp[:, 0:ntok], sint[:, 0:ntok], ALU.mult, [Tfp, Tsin], [Tr1])
                        else:
                            tt("dve", r2[:, 0:ntok], fp[:, 0:ntok], cost[:, 0:ntok], ALU.mult, [Tfp, Tcos], [Tr2])
                            fo, Tfo = fmo[fmi % 3]
                            fmi += 1
                            tt("pool", fo[:, 0:ntok], r1[:, 0:ntok], r2[:, 0:ntok], ALU.add, [Tr1, Tr2], [Tfo])
                            dst = {6: 4, 7: 5, 10: 6, 11: 7}[c]
                            R.dma("pool", fmT[dst, :, t0:t0 + ntok], fo[:, 0:ntok], Tfo, reads=[Tfo], writes=[Dfm])
                    for j in range(nt):
                        gi = gtile[0]
                        gtile[0] += 1
                        gel, Tgel = gels[gi % 3]
                        vln, Tvln = vlns[gi % 3]
                        tm, Ttm = tms[0]
                        for c_ in range(2):
                            mm(tm[:, c_ * 256:(c_ + 1) * 256], aT[:, c_, j * 128:(j + 1) * 128], cst[:, :], True, True,
                               [TaT, Tcs], [Ttm])
                        cp("dve", Bsb[:, j, :], tm[:, :], [Ttm], [TBsb])
                        tm, Ttm = tms[1]
                        for k in range(8):
                            mm(tm[:, :], hT[:, k, j * 128:(j + 1) * 128], wi[:, k, NFM * 128:NFM * 128 + 512], k == 0,
                               k == 7, [ThT, Twi], [Ttm], inc=(k == 7))
                        act(vt[:, j, 0:260].rearrange("p (h d) -> p h d", d=65)[:, :, 0:64],
                            tm[:, 0:256].rearrange("p (h d) -> p h d", d=64), AF.Copy, [Ttm], [Tvt])
                        cp("dve", vt[:, j, 260:772].rearrange("p (h d) -> p h d", d=128)[:, :, 0:64],
                           tm[:, 256:512].rearrange("p (h d) -> p h d", d=64), [Ttm], [Tvt])
                        tm, Ttm = tms[0]
                        for k in range(8):
                            mm(tm[:, :], hT[:, k, j * 128:(j + 1) * 128], wi[:, k, NFM * 128 + 512:NFM * 128 + 1024],
                               k == 0, k == 7, [ThT, Twi], [Ttm], inc=(k == 7))
                        act(gel[:], tm[:, :], AF.Gelu_apprx_tanh, [Ttm], [Tgel])
                        R.op("dve", lambda e, gel=gel: e.bn_stats(out=bst[:], in_=gel[:, 256:512]), [Tgel], [Tbst])
                        R.op("dve", lambda e: e.bn_aggr(out=mv[:], in_=bst[:]), [Tbst], [Tmv])
                        ts("dve", vn[:], gel[:, 256:512], mv[:, 0:1], None, ALU.subtract, None, [Tgel, Tmv], [Tvn])
                        ts("dve", mv[:, 1:2], mv[:, 1:2], EPS, None, ALU.add, None, [Tmv], [Tmv])
                        tt("pool", mv[:, 1:2], mv[:, 1:2], nhf[:, 0:1], ALU.pow, [Tmv, Tnhf], [Tmv])
                        stt(vn[:], vn[:], mv[:, 1:2], lng[:], ALU.mult, ALU.mult, [Tvn, Tmv, Tlng], [Tvn])
                        tt("pool", vln[:], vn[:], lnb[:], ALU.add, [Tvn, Tlnb], [Tvln])
                        sgu_ent[gi] = (gi, bi, j, t0, ntok)
                        if gi - 2 >= 0:
                            p1_sgu_step("S1", sgu_ent[gi - 2])
                        if gi - 3 >= 0:
                            p1_sgu_step("S2", sgu_ent[gi - 3])
                    if bi + 1 < nblk:
                        p1_A2(bi + 1)
                    R.dma("pool", Bd[t0:t0 + ntok, :].rearrange("(j p) c -> p j c", p=128), Bsb[:, 0:nt, :], TBsb,
                          reads=[TBsb], writes=[DB])
                    R.dma("pool", vtok[t0:t0 + ntok, :].rearrange("(j p) c -> p j c", p=128),
                          vt[:, 0:nt, :], Tvt, reads=[Tvt], writes=[Dv])
                gl_ = gtile[0] - 1
                p1_sgu_step("S1", sgu_ent[gl_ - 1])
                p1_sgu_step("S2", sgu_ent[gl_ - 2])
                p1_sgu_step("S1", sgu_ent[gl_])
                p1_sgu_step("S2", sgu_ent[gl_ - 1])
                p1_sgu_step("S2", sgu_ent[gl_])
                with nc.Block() as blk:
                    R.end_phase(blk)

            with contextlib.ExitStack() as ps:
                Ball, TBall = SB(ps, "Ball", [128, NTILE, 512], BF16, True)
                d256, Td256 = SB(ps, "d256", [128, 2, 2, 256], BF16, True)
                dts = [[SB(ps, "dft%d_%d" % (kd, i), [128, 8, 512], BF16, True) for i in range(3)] for kd in range(2)]
                yaT = [SB(ps, "yaT%d" % i, [128, 2, 512], BF16, True) for i in range(2)]
                yps = [PS(ps, "yps%d" % i, [128, 512]) for i in range(4)]
                R.dma("sp", Ball[:], Bd.rearrange("(j p) c -> p j c", p=128), TBall, reads=[DB], writes=[TBall])
                R.dma("sp", d256[:].rearrange("p a b c -> p (a b c)"), dft256_d[:, :], Td256, writes=[Td256])
                li_ = 0
                for nb in range(8):
                    ya, Tya = yaT[nb % 2]
                    for qd in range(4):
                        bufs = []
                        for kd in range(2):
                            dt_, Tdt = dts[kd][li_ % 3]
                            R.dma("sp", dt_[:].rearrange("p a b -> p (a b)"),
                                  dft_d[kd, nb, :, qd * 8 * 512:(qd + 1) * 8 * 512], Tdt, writes=[Tdt])
                            bufs.append((dt_, Tdt))
                        li_ += 1
                        for c in range(2):
                            yp, Typ = yps[(nb % 2) * 2 + c]
                            for n8 in range(8):
                                nti = qd * 8 + n8
                                for kd in range(2):
                                    dt_, Tdt = bufs[kd]
                                    lastmm = (qd == 3 and n8 == 7 and kd == 1)
                                    mm(yp[:, :], Ball[:, nti, c * 256 + kd * 128:c * 256 + (kd + 1) * 128],
                                       dt_[:, n8, :], (qd == 0 and n8 == 0 and kd == 0), lastmm, [TBall, Tdt], [Typ],
                                       inc=(lastmm or (n8 == 7 and kd == 1)))
                    for c in range(2):
                        yp, Typ = yps[(nb % 2) * 2 + c]
                        if c == 0:
                            act(ya[:, c, :], yp[:, :], AF.Copy, [Typ], [Tya])
                        else:
                            cp("dve", ya[:, c, :], yp[:, :], [Typ], [Tya])
                    R.dma("pool", yT[0:2, :, nb * 512:(nb + 1) * 512].rearrange("c p t -> p c t"), ya[:], Tya,
                          reads=[Tya], writes=[Dy])
                if not last:
                    ya, Tya = yaT[0]
                    for c in range(2):
                        yp, Typ = yps[c]
                        for n2 in range(2):
                            for kd in range(2):
                                mm(yp[:, 0:256], Ball[:, 32 + n2, c * 256 + kd * 128:c * 256 + (kd + 1) * 128],
                                   d256[:, n2, kd, :], (n2 == 0 and kd == 0), (n2 == 1 and kd == 1), [TBall, Td256],
                                   [Typ], inc=(n2 == 1 and kd == 1))
                        cp("dve", ya[:, c, 0:256], yp[:, 0:256], [Typ], [Tya])
                    R.dma("pool", yT[0:2, :, SEQ:NTOK].rearrange("c p t -> p c t"), ya[:, :, 0:256], Tya, reads=[Tya],
                          writes=[Dy])
                with nc.Block() as blk:
                    R.end_phase(blk)

            with contextlib.ExitStack() as ps:
                qT, TqT = SB(ps, "qT", [128, 2, NTOK], BF16, True)
                kTs = [SB(ps, "kTp%d" % h, [128, NTOK], BF16, True) for h in range(4)]
                vna, Tvna = SB(ps, "vna", [128, NTILE, 260], BF16, True)
                msk, Tmsk = SB(ps, "msk", [128, 12800], BF16, True)
                Es = [SB(ps, "E%d" % i, [128, 7, 128], BF16) for i in range(3)]
                rc, Trc = SB(ps, "rc", [128, 4], F32)
                ybs = [SB(ps, "yb%d" % i, [128, GW], BF16) for i in range(2)]
                ybTs = [SB(ps, "ybT%d" % i, [128, 2, 512], BF16, True) for i in range(2)]
                sABs = [PS(ps, "sAB%d" % i, [128, 1024]) for i in range(2)]
                ops_ = [PS(ps, "ops%d" % i, [128, 4, 65]) for i in range(2)]
                trps = [PS(ps, "trp%d" % i, [128, 1024], BF16) for i in range(2)]
                R.dma("sp", qT[:], fmT[0:2].rearrange("c p t -> p c t"), TqT, reads=[Dfm], writes=[TqT])
                for h in range(4):
                    r0 = (h % 2) * 64
                    memset("dve" if h % 2 == 0 else "pool", kTs[h][0][:], 0.0, [kTs[h][1]])
                    R.dma("sp", kTs[h][0][r0:r0 + 64, :], fmT[2 + h // 2, r0:r0 + 64, :], kTs[h][1], reads=[Dfm],
                          writes=[kTs[h][1]])
                R.dma("sp", vna[:], vtok[:, 0:260].rearrange("(j p) c -> p j c", p=128), Tvna, reads=[Dv],
                      writes=[Tvna])
                R.dma("sp", msk[:], maskT_b[l], Tmsk, reads=[Dw], writes=[Tmsk])
                ntq = 32 if last else 34
                items = []
                for t in range(ntq):
                    if t < 32:
                        kt0 = min(max(t - 2, 0), 27)
                        kts = [kt0 + i for i in range(5)] + [32, 33]
                        pat = {0: 0, 1: 1, 30: 3, 31: 4}.get(t, 2)
                    else:
                        kts, pat = [32, 33], None
                    for h in range(4):
                        items.append((t, h, kts, pat))

                def na_S(i):
                    t, h, kts, pat = items[i]
                    cq, b0 = h // 2, (h % 2) * 64
                    sAB, TsAB = sABs[i % 2]
                    for idx, kt in enumerate(kts):
                        dsl = sAB[:, idx * 128:(idx + 1) * 128]
                        masked = (pat is not None and idx < 5)
                        lastm = (idx == len(kts) - 1)
                        mm(dsl, kTs[h][0][:, kt * 128:(kt + 1) * 128], qT[:, cq, t * 128:(t + 1) * 128],
                           True, not masked, [kTs[h][1], TqT], [TsAB], inc=(lastm and not masked))
                        if masked:
                            m0 = ((pat * 4 + h) * 5 + idx) * 128
                            mm(dsl, identb[:], msk[:, m0:m0 + 128], False, True, [Tid, Tmsk], [TsAB], inc=lastm)

                def na_EX(i):
                    t, h, kts, pat = items[i]
                    nk = len(kts)
                    sAB, TsAB = sABs[i % 2]
                    E, TE = Es[i % 3]
                    act(E[:, 0:nk, :], sAB[:, 0:nk * 128].rearrange("p (a q) -> p a q", q=128), AF.Exp, [TsAB], [TE])

                def na_PV(i):
                    t, h, kts, pat = items[i]
                    nk = len(kts)
                    E, TE = Es[i % 3]
                    op_, Top = ops_[t % 2]
                    for idx, kt in enumerate(kts):
                        mm(op_[:, h, :], E[:, idx, :], vna[:, kt, h * 65:(h + 1) * 65], idx == 0, idx == nk - 1,
                           [TE, Tvna], [Top], inc=(idx == nk - 1))

                def na_tail1(t):
                    op_, Top = ops_[t % 2]
                    yb, Tyb = ybs[t % 2]
                    recip(rc[:], op_[:, :, 64], [Top], [Trc])
                    for h in range(4):
                        ts("dve", yb[:, h * 64:(h + 1) * 64], op_[:, h, 0:64], rc[:, h:h + 1], None, ALU.mult, None,
                           [Top, Trc], [Tyb])

                def na_tail2(t):
                    yb, Tyb = ybs[t % 2]
                    trp, Ttrp = trps[t % 2]
                    for c in range(2):
                        tr(trp[:, c * 128:(c + 1) * 128], yb[:, c * 128:(c + 1) * 128], identb[:], [Tyb, Tid], [Ttrp],
                           inc=(c == 1))

                def na_tail3(t):
                    trp, Ttrp = trps[t % 2]
                    ybT, TybT = ybTs[(t // 4) % 2]
                    j4 = t % 4
                    act(ybT[:, :, j4 * 128:(j4 + 1) * 128], trp[:, 0:256].rearrange("p (c t) -> p c t", c=2), AF.Copy,
                        [Ttrp], [TybT])
                    if j4 == 3 or t == ntq - 1:
                        tb = (t // 4) * 512
                        n_ = (j4 + 1) * 128
                        R.dma("pool", yT[2:4, :, tb:tb + n_].rearrange("c p t -> p c t"), ybT[:, :, 0:n_], TybT,
                              reads=[TybT], writes=[Dy])

                pend = []
                n_it = len(items)
                na_S(0)
                na_S(1)
                for i in range(n_it):
                    na_EX(i)
                    if i + 2 < n_it:
                        na_S(i + 2)
                    na_PV(i)
                    t, h = items[i][0], items[i][1]
                    if h == 3:
                        na_tail1(t)
                        pend.append((i + 1, na_tail2, t))
                        pend.append((i + 2, na_tail3, t))
                    keep = []
                    for (due, fn, arg) in pend:
                        if due <= i:
                            fn(arg)
                        else:
                            keep.append((due, fn, arg))
                    pend = keep
                for (due, fn, arg) in pend:
                    fn(arg)
                with nc.Block() as blk:
                    R.end_phase(blk)

            with contextlib.ExitStack() as ps:
                Qh, TQh = SB(ps, "Qc", [128, 2, NTOK], BF16, True)
                Kps = [SB(ps, "Kp%d" % v, [128, NTOK], BF16, True) for v in range(8)]
                vdf, Tvdf = SB(ps, "vdf", [128, NTILE, 512], BF16, True)
                Eb = [SB(ps, "Eb%d" % i, [128, 2, 512], BF16) for i in range(3)]
                rr = [SB(ps, "rr%d" % i, [64, 512], F32) for i in range(2)]
                dd, Tdd = SB(ps, "dd", [64, 512], F32)
                d2, Td2 = SB(ps, "d2", [64, 512], F32)
                dsq, Tdsq = SB(ps, "dsq", [64, 512], F32)
                rs_, Trs = SB(ps, "rs_", [64, 512], F32)
                ydT, TydT = SB(ps, "ydT", [64, 512], BF16, True)
                sps = [PS(ps, "sps%d" % i, [128, 2, 512]) for i in range(2)]
                ops2 = [PS(ps, "ops2_%d" % i, [128, 512]) for i in range(4)]

                def acc_of(bn_, m_):
                    return ops2[2 * (bn_ % 2) + m_]
                dhi, Tdhi = SB(ps, "dhi", [128, 512], BF16)
                dlo, Tdlo = SB(ps, "dlo", [128, 512], BF16)
                memset("pool", dhi[:], 0.0, [Tdhi])
                memset("pool", dlo[:], 0.0, [Tdlo])
                R.dma("sp", Qh[:], fmT[4:6].rearrange("c p t -> p c t"), TQh, reads=[Dfm], writes=[TQh])
                for v in range(8):
                    ch, r0 = v // 4, (v % 4) * 32
                    memset("dve" if v % 2 == 0 else "pool", Kps[v][0][:], 0.0, [Kps[v][1]])
                    R.dma("sp", Kps[v][0][r0:r0 + 32, :], fmT[6 + ch, r0:r0 + 32, :], Kps[v][1], reads=[Dfm],
                          writes=[Kps[v][1]])
                R.dma("sp", vdf[:], vtok[:, 260:772].rearrange("(j p) c -> p j c", p=128), Tvdf, reads=[Dv],
                      writes=[Tvdf])
                gens_ = []
                if l == 0:
                    gens_.append(cast_weights(0, "late"))
                if l + 1 < nlayers:
                    gens_.append(cast_weights(l + 1))
                cast_gen = itertools.chain(*gens_)
                qblocks = blocks[:8] if last else blocks
                sc_ = 32.0 ** -0.5
                steps = []
                bnum = 0
                for h in range(4):
                    for (q0, nq, s) in qblocks:
                        kts = list(range(NTILE)) if s == 0 else [32, 33]
                        for ki, kt in enumerate(kts):
                            steps.append((h, q0, nq, ki, kt, len(kts), bnum))
                        bnum += 1

                def df_S(i):
                    h, q0, nq, ki, kt, nk, bn = steps[i]
                    sp_, Tsp = sps[i % 2]
                    for m in range(2):
                        mm(sp_[:, m, 0:nq], Kps[h * 2 + m][0][:, kt * 128:(kt + 1) * 128], Qh[:, h // 2, q0:q0 + nq], True,
                           True, [Kps[h * 2 + m][1], TQh], [Tsp], inc=(m == 1))

                def df_EX(i):
                    h, q0, nq, ki, kt, nk, bn = steps[i]
                    sp_, Tsp = sps[i % 2]
                    E, TE = Eb[i % 3]
                    act(E[:, :, 0:nq], sp_[:, :, 0:nq], AF.Exp, [Tsp], [TE], scale=sc_)

                def df_PV(i):
                    h, q0, nq, ki, kt, nk, bn = steps[i]
                    E, TE = Eb[i % 3]
                    for m in range(2):
                        o2, To2 = acc_of(bn, m)
                        mm(o2[:, 0:nq], vdf[:, kt, h * 128:(h + 1) * 128], E[:, m, 0:nq], ki == 0, ki == nk - 1,
                           [Tvdf, TE], [To2], inc=True)

                def df_post1(arg):
                    h, q0, nq, bn = arg
                    a0, Ta0 = acc_of(bn, 0)
                    a1, Ta1 = acc_of(bn, 1)
                    recip(rr[0][0][:, 0:nq], a0[64:128, 0:nq], [Ta0], [rr[0][1]])
                    tt("dve", dd[:, 0:nq], a0[0:64, 0:nq], rr[0][0][:, 0:nq], ALU.mult, [Ta0, rr[0][1]], [Tdd])
                    recip(rr[1][0][:, 0:nq], a1[64:128, 0:nq], [Ta1], [rr[1][1]])
                    tt("dve", d2[:, 0:nq], a1[0:64, 0:nq], rr[1][0][:, 0:nq], ALU.mult, [Ta1, rr[1][1]], [Td2])
                    stt(dd[:, 0:nq], d2[:, 0:nq], neglam[:, l:l + 1], dd[:, 0:nq], ALU.mult, ALU.add, [Td2, Tnl, Tdd],
                        [Tdd])
                    tt("pool", dsq[:, 0:nq], dd[:, 0:nq], dd[:, 0:nq], ALU.mult, [Tdd], [Tdsq])
                    cp("dve", dhi[0:64, 0:nq], dsq[:, 0:nq], [Tdsq], [Tdhi])
                    tt("pool", dlo[0:64, 0:nq], dsq[:, 0:nq], dhi[0:64, 0:nq], ALU.subtract, [Tdsq, Tdhi], [Tdlo])

                def df_post2(arg):
                    h, q0, nq, bn = arg
                    bp, Tbp = acc_of(bn, 0)
                    mm(bp[:, 0:nq], onesb[:, :], dhi[:, 0:nq], True, False, [Tonesb, Tdhi], [Tbp], inc=False)
                    mm(bp[:, 0:nq], onesb[:, :], dlo[:, 0:nq], False, True, [Tonesb, Tdlo], [Tbp])

                def df_post3(arg):
                    h, q0, nq, bn = arg
                    bp, Tbp = acc_of(bn, 0)
                    act(rs_[:, 0:nq], bp[0:64, 0:nq], AF.Ln, [Tbp], [Trs], scale=1.0 / 64, bias=EPS)
                    act(rs_[:, 0:nq], rs_[:, 0:nq], AF.Exp, [Trs], [Trs], scale=-0.5)
                    stt(ydT[:, 0:nq], dd[:, 0:nq], gsub[:, l:l + 1], rs_[:, 0:nq], ALU.mult, ALU.mult,
                        [Tdd, Tgs, Trs], [TydT])
                    R.dma("pool", yT[6 + h, 0:64, q0:q0 + nq], ydT[:, 0:nq], TydT, reads=[TydT], writes=[Dy])

                pend = []
                n_it = len(steps)
                df_S(0)
                df_S(1)
                for i in range(n_it):
                    df_EX(i)
                    if i + 2 < n_it:
                        df_S(i + 2)
                    df_PV(i)
                    if i >= 16 and i % 16 == 0:
                        next(cast_gen, None)
                    h, q0, nq, ki, kt, nk, bn = steps[i]
                    if ki == nk - 1:
                        for (due, fn, arg) in pend:
                            fn(arg)
                        pend = []
                        df_post1((h, q0, nq, bn))
                        pend.append((i + 12, df_post2, (h, q0, nq, bn)))
                        pend.append((i + 16, df_post3, (h, q0, nq, bn)))
                    keep = []
                    for (due, fn, arg) in pend:
                        if due <= i:
                            fn(arg)
                        else:
                            keep.append((due, fn, arg))
                    pend = keep
                for (due, fn, arg) in pend:
                    fn(arg)
                for _ in cast_gen:
                    pass
                with nc.Block() as blk:
                    R.end_phase(blk)

            with contextlib.ExitStack() as ps:
                wo, Two = SB(ps, "wo", [128, 10, D], BF16, True)
                gb = [[SB(ps, "gb%d_%d" % (s, g), [128, D], F32, True) for g in range(2)] for s in range(2)]
                fgb, Tfgb = SB(ps, "fgb", [128, D], F32, True)
                xts = [SB(ps, "x3_%d" % i, [128, 4, D], F32, True) for i in range(2)]
                yTb, TyTb = SB(ps, "yTb", [128, 10, 512], BF16, True)
                junk, Tjunk = SB(ps, "junk3", [128, D], BF16)
                ss, Tss = SB(ps, "ss3", [128, 4], F32)
                rstd, Trstd = SB(ps, "rstd3", [128, 4], F32)
                xn, Txn = SB(ps, "xn3", [128, 4, D], BF16)
                hT, ThT = SB(ps, "h2T", [128, 8, 512], BF16)
                aT, TaT = SB(ps, "aT3", [128, 32, 512], BF16)
                tmpv = [SB(ps, "tmp%d" % i, [128, 512], F32) for i in range(2)]
                sqv = [SB(ps, "sq%d" % i, [128, 512], F32) for i in range(2)]
                w1b = [SB(ps, "w1b%d" % i, [128, 8, 512], BF16, True) for i in range(2)]
                w2b = [SB(ps, "w2b%d" % i, [128, 4, 512], BF16, True) for i in range(3)]
                acc = [PS(ps, "acc%d" % i, [128, 512]) for i in range(4)]
                f1p = [PS(ps, "f1p%d" % i, [128, 512]) for i in range(2)]
                tps = [PS(ps, "tp3_%d" % i, [128, 1024], BF16) for i in range(2)]
                R.dma("sp", wo[:], w_out_b[l].rearrange("(c p) n -> p c n", p=128), Two, reads=[Dw], writes=[Two])
                for s in range(2):
                    for g in range(2):
                        R.dma("sp", gb[s][g][0][:], bcast(gates[l, s:s + 1, g * D:(g + 1) * D], 128), gb[s][g][1],
                              reads=[Dg], writes=[gb[s][g][1]])
                if last:
                    R.dma("sp", fgb[:], bcast(final_g[0:1, :], 128), Tfgb, writes=[Tfgb])
                w1v = w_ff1_b[l].rearrange("(k p) f -> p k f", p=128)
                w2v = w_ff2_b[l].rearrange("(c p) d -> p c d", p=128)
                i1 = i2 = 0
                tcount = 0
                p3blocks = blocks[:8] if last else blocks

                def p3_load(bi_, q):
                    t0_, ntok_, s_ = p3blocks[bi_]
                    xt_, Txt_ = xts[bi_ % 2]
                    R.dma(q, xt_[:, 0:ntok_ // 128, :], xsrc[t0_:t0_ + ntok_, :].rearrange("(j p) d -> p j d", p=128),
                          Txt_, reads=[Dxs], writes=[Txt_])
                    R.dma(q, yTb[:, :, 0:ntok_], yT[:, :, t0_:t0_ + ntok_].rearrange("c p t -> p c t"), TyTb,
                          reads=[Dy], writes=[TyTb])

                p3_load(0, "sp")
                for bi, (t0, ntok, s) in enumerate(p3blocks):
                    nt = ntok // 128
                    xt, Txt = xts[bi % 2]
                    g1b, Tg1b = gb[s][0]
                    g2b, Tg2b = gb[s][1]
                    for j in range(nt):
                        for n in range(2):
                            ap_, Tap = acc[(j * 2 + n) % 4]
                            for c in range(10):
                                kc = 128 if c < 6 else 64
                                mm(ap_[:, :], yTb[0:kc, c, j * 128:(j + 1) * 128], wo[0:kc, c, n * 512:(n + 1) * 512],
                                   c == 0, c == 9, [TyTb, Two], [Tap], inc=(c == 9))
                            tv, Ttv = tmpv[tcount % 2]
                            tcount += 1
                            tt("dve", tv[:], ap_[:, :], g1b[:, n * 512:(n + 1) * 512], ALU.mult, [Tap, Tg1b], [Ttv])
                            tt("pool", xt[:, j, n * 512:(n + 1) * 512], xt[:, j, n * 512:(n + 1) * 512], tv[:], ALU.add,
                               [Txt, Ttv], [Txt])
                    if bi + 1 < len(p3blocks):
                        p3_load(bi + 1, "pool")
                    norm_to_hT((junk, Tjunk, ss, Tss, rstd, Trstd, xn, Txn), xt, Txt, nt, l, 2, 3, s, tps, hT, ThT, "p3")
                    for g8 in range(8):
                        w1t, Tw1 = w1b[i1 % 2]
                        i1 += 1
                        R.dma("sp", w1t[:], w1v[:, :, g8 * 512:(g8 + 1) * 512], Tw1, reads=[Dw], writes=[Tw1])
                        for c4 in range(4):
                            c = g8 * 4 + c4
                            fp, Tfp = f1p[c % 2]
                            for k in range(8):
                                mm(fp[:, 0:ntok], w1t[:, k, c4 * 128:(c4 + 1) * 128], hT[:, k, 0:ntok], k == 0, k == 7,
                                   [Tw1, ThT], [Tfp], inc=(k == 7))
                            sq, Tsq = sqv[c % 2]
                            act(sq[:, 0:ntok], fp[:, 0:ntok], AF.Square, [Tfp], [Tsq])
                            stt(aT[:, c, 0:ntok], fp[:, 0:ntok], 0.0, sq[:, 0:ntok], ALU.is_gt, ALU.mult, [Tfp, Tsq], [TaT])
                    for n in range(2):
                        for g8 in range(8):
                            w2t, Tw2 = w2b[i2 % 3]
                            i2 += 1
                            R.dma("sp", w2t[:], w2v[:, g8 * 4:(g8 + 1) * 4, n * 512:(n + 1) * 512], Tw2, reads=[Dw],
                                  writes=[Tw2])
                            for j in range(nt):
                                ap_, Tap = acc[j]
                                for c4 in range(4):
                                    c = g8 * 4 + c4
                                    mm(ap_[:, :], aT[:, c, j * 128:(j + 1) * 128], w2t[:, c4, :], (g8 == 0 and c4 == 0),
                                       (g8 == 7 and c4 == 3), [TaT, Tw2], [Tap], inc=(c4 == 3))
                        for j in range(nt):
                            ap_, Tap = acc[j]
                            tv, Ttv = tmpv[tcount % 2]
                            tcount += 1
                            tt("dve", tv[:], ap_[:, :], g2b[:, n * 512:(n + 1) * 512], ALU.mult, [Tap, Tg2b], [Ttv])
                            tt("pool", xt[:, j, n * 512:(n + 1) * 512], xt[:, j, n * 512:(n + 1) * 512], tv[:], ALU.add,
                               [Txt, Ttv], [Txt])
                    if not last:
                        R.dma("pool", xres[t0:t0 + ntok, :].rearrange("(j p) d -> p j d", p=128), xt[:, 0:nt, :], Txt,
                              reads=[Txt], writes=[Dx])
                    else:
                        for j in range(nt):
                            act(junk[:], xt[:, j, :], AF.Square, [Txt], [Tjunk, Tss], accum_out=ss[:, j:j + 1])
                        act(rstd[:, 0:nt], ss[:, 0:nt], AF.Sqrt, [Tss], [Trstd], scale=1.0 / D, bias=EPS)
                        recip(rstd[:, 0:nt], rstd[:, 0:nt], [Trstd], [Trstd])
                        for j in range(nt):
                            stt(xt[:, j, :], xt[:, j, :], rstd[:, j:j + 1], fgb[:], ALU.mult, ALU.mult,
                                [Txt, Trstd, Tfgb], [Txt])
                        R.dma("pool", out_d[t0:t0 + ntok, :].rearrange("(j p) d -> p j d", p=128), xt[:, 0:nt, :], Txt,
                              reads=[Txt], writes=[Dout])
                with nc.Block() as blk:
                    R.end_phase(blk)
    return nc


def _consts():
    bf = ml_dtypes.bfloat16
    c = {}
    c["identb"] = np.eye(128, dtype=np.float32).astype(bf)
    k = np.arange(64)
    th = 2 * np.pi * np.outer(k, k) / 64.0
    cc, sc = np.cos(th) / 8.0, np.sin(th) / 8.0
    z = np.zeros((64, 64))
    cs = np.concatenate([np.block([[cc, z], [z, cc]]), np.block([[sc, z], [z, sc]])], axis=1)
    c["cs_tab"] = cs.astype(np.float32).astype(bf)
    n = np.arange(SEQ)
    dft = np.empty((2, 8, 128, 32, 512), dtype=bf)
    for nb in range(8):
        npr = nb * 512 + np.arange(512)
        m = (np.outer(n, npr) % SEQ).astype(np.float64)
        ang = 2 * np.pi * m / SEQ
        cm = (np.cos(ang) / 64.0).astype(np.float32).reshape(32, 128, 512).transpose(1, 0, 2)
        sm = (-np.sin(ang) / 64.0).astype(np.float32).reshape(32, 128, 512).transpose(1, 0, 2)
        dft[0, nb] = cm.astype(bf)
        dft[1, nb] = sm.astype(bf)
    c["dft"] = dft.reshape(2, 8, 128, 32 * 512)
    n2 = np.arange(CTX)
    ang = 2 * np.pi * (np.outer(n2, n2) % CTX) / CTX
    d256 = np.stack([np.cos(ang) / 16.0, -np.sin(ang) / 16.0], axis=0)
    d256 = d256.reshape(2, 2, 128, 256).transpose(2, 1, 0, 3)
    c["dft256"] = np.ascontiguousarray(d256).astype(np.float32).astype(bf).reshape(128, 1024)
    t = np.arange(SEQ)
    rows, cols = (t // 64).astype(np.float32), (t % 64).astype(np.float32)
    inv = (10000.0 ** (-np.arange(8, dtype=np.float32) / 8)).astype(np.float32)
    cos_t = np.ones((128, NTOK), dtype=np.float32)
    sin_t = np.zeros((128, NTOK), dtype=np.float32)
    for p in range(128):
        i = p % 32
        a, f, hh = i // 16, i % 8, (i // 8) % 2
        ang = ((rows if a == 0 else cols) * inv[f]).astype(np.float32)
        cos_t[p, :SEQ] = np.cos(ang)
        sin_t[p, :SEQ] = np.sin(ang) * (-1.0 if hh == 0 else 1.0)
    c["cos_tab"], c["sin_t"] = cos_t, sin_t
    sel = np.zeros((65, 64), dtype=np.float32)
    sel[64, :] = 1.0
    c["sel65"] = sel
    c["ones64"] = np.ones((64, 64), dtype=np.float32)
    ob = np.zeros((128, 128), dtype=np.float32)
    ob[0:64, :] = 1.0
    c["onesb"] = ob.astype(bf)
    li = np.array([0.8 - 0.6 * math.exp(-0.3 * l) for l in range(L)], dtype=np.float32)
    c["laminit"] = np.tile(li[None, :], (64, 1)).astype(np.float32)
    c["omli"] = np.tile((1.0 - li)[None, :], (64, 1)).astype(np.float32)
    c["ident2"] = np.eye(2, dtype=np.float32)
    return c


def _mask_index():
    treps = [0, 1, 2, 30, 31]
    p = np.arange(128)[:, None]
    q = np.arange(128)[None, :]
    idx = np.zeros((128, 5, 5, 128), dtype=np.int64)
    val = np.zeros((128, 5, 5, 128), dtype=bool)
    for pi, t in enumerate(treps):
        kt0 = min(max(t - 2, 0), 27)
        for j in range(5):
            kr = 2 * (kt0 + j) + p // 64
            kc = p % 64
            r = 2 * t + q // 64
            c = q % 64
            rs = np.clip(r - 4, 0, 56)
            cs = np.clip(c - 8, 0, 48)
            ok = (kr >= rs) & (kr < rs + 8) & (kc >= cs) & (kc < cs + 16)
            dr = kr - r + 7
            dc = np.clip(kc - c, -15, 15) + 15
            idx[:, pi, j, :] = np.where(ok, dr * 31 + dc, 0)
            val[:, pi, j, :] = ok
    return idx, val


_CACHE = {}


def _prep_shared(inp):
    f = np.float32
    sh = {}
    sh["ada_w"] = np.ascontiguousarray(inp["ada_w"], dtype=f)
    sh["ada_b"] = np.ascontiguousarray(inp["ada_b"], dtype=f)
    sh["n1gc"] = np.ascontiguousarray(np.asarray(inp["norm1_g"], dtype=f).reshape(L, 8, 128).transpose(2, 0, 1))
    sh["n2gc"] = np.ascontiguousarray(np.asarray(inp["norm2_g"], dtype=f).reshape(L, 8, 128).transpose(2, 0, 1))
    w_in = np.asarray(inp["w_in"], dtype=f)
    cd = 6 * GW
    j = np.arange(256)
    cols = np.concatenate([np.arange(0, 256), np.arange(256, 512), np.arange(512, 768),
                           cd + j, cd + (j ^ 8), cd + 256 + j, cd + 256 + (j ^ 8),
                           np.arange(768, 1024), np.arange(cd + 512, cd + 768), np.arange(1024, 1536)])
    assert cols.shape[0] == WIN
    sh["w_in_r"] = np.ascontiguousarray(w_in[:, :, cols])
    w_out = np.asarray(inp["w_out"], dtype=f)
    wo = np.zeros((L, 10, 128, D), dtype=f)
    wo[:, 0:6] = w_out[:, 0:768].reshape(L, 6, 128, D)
    wo[:, 6:10, 0:64] = w_out[:, 768:1024].reshape(L, 4, 64, D)
    sh["w_out_r"] = wo.reshape(L, 1280, D)
    sh["w_ff1"] = np.ascontiguousarray(inp["w_ff1"], dtype=f)
    sh["w_ff2"] = np.ascontiguousarray(inp["w_ff2"], dtype=f)
    idx, val = _mask_index()
    rpb = np.asarray(inp["na_rpb"], dtype=f).reshape(L, 4, 15 * 31)
    mk = np.empty((L, 128, 5, 4, 5, 128), dtype=f)
    for l in range(L):
        for h in range(4):
            mk[l, :, :, h] = np.where(val, rpb[l, h][idx], f(NEG))
    sh["maskT"] = mk.reshape(L, 128, 12800)
    sw = np.asarray(inp["sgu_w"], dtype=f)
    sh["sgu_wT"] = np.ascontiguousarray(sw.transpose(0, 3, 1, 2)).reshape(L, 128, 512)
    sh["sgu_bc"] = np.ascontiguousarray(np.asarray(inp["sgu_b"], dtype=f).transpose(2, 0, 1))
    sh["sgu_lng"] = np.ascontiguousarray(inp["sgu_ln_g"], dtype=f)
    sh["sgu_lnb"] = np.ascontiguousarray(inp["sgu_ln_b"], dtype=f)
    for i, nm in enumerate(("diff_lq1", "diff_lk1", "diff_lq2", "diff_lk2")):
        sh["diff_l%d" % i] = np.ascontiguousarray(inp[nm], dtype=f)
    sh["subg_c"] = np.ascontiguousarray(np.asarray(inp["diff_subln_g"], dtype=f).T)
    sh["final_g"] = np.asarray(inp["final_g"], dtype=f).reshape(1, D)
    return sh


def make_in_maps(inp):
    if "consts" not in _CACHE:
        c = _consts()
        c["sin_tab"] = c.pop("sin_t")
        _CACHE["consts"] = c
    sh = _prep_shared(inp)
    sh.update(_CACHE["consts"])
    x = np.asarray(inp["x"], dtype=np.float32)
    ctx = np.asarray(inp["ctx"], dtype=np.float32)
    c = np.asarray(inp["c"], dtype=np.float32)
    c_ctx = np.asarray(inp["c_ctx"], dtype=np.float32)
    maps = []
    for b in range(8):
        m = dict(sh)
        m["xin"] = np.concatenate([x[b], ctx[b]], axis=0)
        cc = np.stack([c[b], c_ctx], axis=-1).reshape(8, 128, 2).transpose(1, 0, 2)
        m["cc"] = np.ascontiguousarray(cc)
        maps.append(m)
    return maps


def kernel(**inputs):
    if "nc" not in _CACHE:
        _CACHE["nc"] = build()
    nc = _CACHE["nc"]
    maps = make_in_maps(inputs)
    res = run_bass_kernel_spmd(nc, maps, core_ids=list(range(8)))
    return np.stack([np.asarray(r["out"], dtype=np.float32) for r in res.results], axis=0)
```
